# Optimizing a Trainium2 kernel written in Bass

```python
import jax, jax.numpy as jnp
from jax import lax
import numpy as np

D_MODEL = 1024
BATCH = 2
SEQ = 8192
DEPTH = 1
DEC_BATCH = 32
DEC_SEQ = 64
PAST_LEN = 4096

CHUNK = 64
Q_BLOCK = 128
GLA_HEADS = 4
GLA_DK = D_MODEL // 2 // GLA_HEADS
GLA_DV = D_MODEL // GLA_HEADS
GLA_QK = GLA_HEADS * GLA_DK
GLA_V = GLA_HEADS * GLA_DV
GLA_GATE_RANK = 16
GLA_GATE_TEMP = 16.0
SB_HEADS = 8
SB_DH = D_MODEL // SB_HEADS
SB_W = SB_HEADS * SB_DH
D_FF = 2816
LN_EPS = 1e-5
DN_ALPHA = (2 * DEPTH) ** 0.25
DN_BETA = (8 * DEPTH) ** -0.25
SPLITS = (GLA_QK, GLA_QK, GLA_V, GLA_V, GLA_GATE_RANK, SB_W, SB_W, SB_W, D_MODEL, D_MODEL)
D_IN = sum(SPLITS)
SPLIT_POINTS = tuple(int(v) for v in np.cumsum(SPLITS)[:-1])

kernel_name = 'gla_stickbreaking_macaron_deepnorm_stream'


def layer_norm(x, g, b):
    xf = x.astype(jnp.float32)
    mu = jnp.mean(xf, axis=-1, keepdims=True)
    var = jnp.mean(jnp.square(xf - mu), axis=-1, keepdims=True)
    return ((xf - mu) * lax.rsqrt(var + LN_EPS) * g + b).astype(x.dtype)


def swiglu(x, w_in, w_out):
    gate, up = jnp.split(x @ w_in, 2, axis=-1)
    return (jax.nn.silu(gate) * up) @ w_out


def head_rms_norm(o, g):
    of = o.astype(jnp.float32)
    return of * lax.rsqrt(jnp.mean(jnp.square(of), axis=-1, keepdims=True) + LN_EPS) * g


def gla_chunked(q, k, v, log_a, s0):
    B, T, H, dk = q.shape
    dv = v.shape[-1]
    c = min(CHUNK, T)
    n = T // c

    def chunks(t):
        return jnp.moveaxis(t.reshape(B, n, c, H, t.shape[-1]), 1, 0)

    causal = jnp.tril(jnp.ones((c, c), dtype=bool))

    def step(S, inp):
        qc, kc, vc, gc = inp
        b = jnp.cumsum(gc, axis=1)
        b_last = b[:, -1]
        qg = qc * jnp.exp(b)
        kg = kc * jnp.exp(-b)
        kd = kc * jnp.exp(b_last[:, None] - b)
        att = jnp.where(causal, jnp.einsum('bthk,bshk->bhts', qg, kg), 0.0)
        o = jnp.einsum('bhts,bshv->bthv', att, vc) + jnp.einsum('bthk,bhkv->bthv', qg, S)
        S = jnp.exp(b_last)[..., None] * S + jnp.einsum('bshk,bshv->bhkv', kd, vc)
        return S, o

    S, o = lax.scan(step, s0.astype(jnp.float32), (chunks(q), chunks(k), chunks(v), chunks(log_a)))
    o = jnp.moveaxis(o, 0, 1).reshape(B, T, H, dv)
    return o, S.astype(s0.dtype)


def sb_block(qb, q_pos, k, v, k_pos):
    z = jnp.einsum('bqhd,bkhd->bhqk', qb, k).astype(jnp.float32) * (SB_DH ** -0.5)
    visible = k_pos[None, :] < q_pos[:, None]
    log_beta = jax.nn.log_sigmoid(z)
    log_keep = jnp.where(visible, jax.nn.log_sigmoid(-z), 0.0)
    rev = lax.cumsum(log_keep, axis=3, reverse=True)
    between = jnp.concatenate([rev[..., 1:], jnp.zeros_like(rev[..., :1])], axis=-1)
    w = jnp.exp(jnp.where(visible, log_beta + between, -jnp.inf))
    return jnp.einsum('bhqk,bkhd->bqhd', w, v).astype(qb.dtype)


def stick_breaking(q, k, v, n_past):
    B, T, H, d = q.shape
    blk = min(Q_BLOCK, T)
    nb = T // blk
    k_pos = jnp.arange(k.shape[1])
    q_pos = (n_past + jnp.arange(T)).reshape(nb, blk)
    q_blocks = jnp.moveaxis(q.reshape(B, nb, blk, H, d), 1, 0)
    out = lax.map(lambda a: sb_block(a[0], a[1], k, v, k_pos), (q_blocks, q_pos))
    return jnp.moveaxis(out, 0, 1).reshape(B, T, H, d)


def token_mix(h, s0, k_past, v_past, w_in, w_gla_gate_up, b_gla_gate, g_gla_norm, w_gla_o, w_sb_o, w_out):
    B, T, _ = h.shape
    gq, gk, gv, gr, glr, sq, sk, sv, ga, gb = jnp.split(h @ w_in, SPLIT_POINTS, axis=-1)
    q = gq.reshape(B, T, GLA_HEADS, GLA_DK) * (GLA_DK ** -0.5)
    k = gk.reshape(B, T, GLA_HEADS, GLA_DK)
    v = gv.reshape(B, T, GLA_HEADS, GLA_DV)
    log_a = (jax.nn.log_sigmoid((glr @ w_gla_gate_up + b_gla_gate).astype(jnp.float32))
             / GLA_GATE_TEMP).reshape(B, T, GLA_HEADS, GLA_DK)
    o_a, s_new = gla_chunked(q, k, v, log_a, s0)
    o_a = head_rms_norm(o_a, g_gla_norm).reshape(B, T, GLA_V).astype(h.dtype) * jax.nn.silu(gr)
    branch_a = o_a @ w_gla_o
    sq = sq.reshape(B, T, SB_HEADS, SB_DH)
    sk = sk.reshape(B, T, SB_HEADS, SB_DH)
    sv = sv.reshape(B, T, SB_HEADS, SB_DH)
    k_all = jnp.concatenate([k_past, sk], axis=1)
    v_all = jnp.concatenate([v_past, sv], axis=1)
    o_b = stick_breaking(sq, k_all, v_all, k_past.shape[1])
    branch_b = o_b.reshape(B, T, SB_W) @ w_sb_o
    merged = jax.nn.sigmoid(ga) * branch_a + jax.nn.sigmoid(gb) * branch_b
    return merged @ w_out, s_new, sk, sv


def encoder_layer(x, s0, k_past, v_past, ffn1_w_in, ffn1_w_out, ln1_g, ln1_b,
                  w_in, w_gla_gate_up, b_gla_gate, g_gla_norm, w_gla_o, w_sb_o, w_out, ln2_g, ln2_b,
                  ffn2_w_in, ffn2_w_out, ln3_g, ln3_b):
    h = layer_norm(DN_ALPHA * x + 0.5 * swiglu(x, ffn1_w_in, ffn1_w_out), ln1_g, ln1_b)
    mix, s_new, k_new, v_new = token_mix(h, s0, k_past, v_past, w_in, w_gla_gate_up, b_gla_gate,
                                         g_gla_norm, w_gla_o, w_sb_o, w_out)
    h = layer_norm(DN_ALPHA * h + mix, ln2_g, ln2_b)
    y = layer_norm(DN_ALPHA * h + 0.5 * swiglu(h, ffn2_w_in, ffn2_w_out), ln3_g, ln3_b)
    return y, s_new, k_new, v_new


def setup_inputs(seed: int = 0) -> dict:
    key = jax.random.key(seed)
    ks = iter(jax.random.split(key, 32))

    def nrm(shape, scale):
        return jax.random.normal(next(ks), shape, jnp.float32) * scale

    L = DEPTH
    return {
        'x_prompt': nrm((BATCH, SEQ, D_MODEL), 1.0),
        'x_sample': nrm((DEC_BATCH, DEC_SEQ, D_MODEL), 1.0),
        'state_gla': nrm((L, DEC_BATCH, GLA_HEADS, GLA_DK, GLA_DV), 1.0),
        'cache_sb_k': nrm((L, DEC_BATCH, PAST_LEN, SB_HEADS, SB_DH), 1.0),
        'cache_sb_v': nrm((L, DEC_BATCH, PAST_LEN, SB_HEADS, SB_DH), 1.0),
        'ffn1_w_in': nrm((L, D_MODEL, 2 * D_FF), D_MODEL ** -0.5),
        'ffn1_w_out': nrm((L, D_FF, D_MODEL), D_FF ** -0.5 * DN_BETA),
        'ln1_g': 1.0 + nrm((L, D_MODEL), 0.02),
        'ln1_b': nrm((L, D_MODEL), 0.02),
        'w_in': nrm((L, D_MODEL, D_IN), D_MODEL ** -0.5),
        'w_gla_gate_up': nrm((L, GLA_GATE_RANK, GLA_QK), GLA_GATE_RANK ** -0.5),
        'b_gla_gate': nrm((L, GLA_QK), 0.1),
        'g_gla_norm': 1.0 + nrm((L, GLA_HEADS, GLA_DV), 0.02),
        'w_gla_o': nrm((L, GLA_V, D_MODEL), GLA_V ** -0.5),
        'w_sb_o': nrm((L, SB_W, D_MODEL), SB_W ** -0.5),
        'w_out': nrm((L, D_MODEL, D_MODEL), D_MODEL ** -0.5 * DN_BETA),
        'ln2_g': 1.0 + nrm((L, D_MODEL), 0.02),
        'ln2_b': nrm((L, D_MODEL), 0.02),
        'ffn2_w_in': nrm((L, D_MODEL, 2 * D_FF), D_MODEL ** -0.5),
        'ffn2_w_out': nrm((L, D_FF, D_MODEL), D_FF ** -0.5 * DN_BETA),
        'ln3_g': 1.0 + nrm((L, D_MODEL), 0.02),
        'ln3_b': nrm((L, D_MODEL), 0.02),
    }


def reference(x_prompt, x_sample, state_gla, cache_sb_k, cache_sb_v,
              ffn1_w_in, ffn1_w_out, ln1_g, ln1_b,
              w_in, w_gla_gate_up, b_gla_gate, g_gla_norm, w_gla_o, w_sb_o, w_out, ln2_g, ln2_b,
              ffn2_w_in, ffn2_w_out, ln3_g, ln3_b):
    bp = x_prompt.shape[0]
    xp, xs = x_prompt, x_sample
    gla_p, k_p, v_p, gla_s, k_s, v_s = [], [], [], [], [], []
    for l in range(DEPTH):
        lw = (ffn1_w_in[l], ffn1_w_out[l], ln1_g[l], ln1_b[l],
              w_in[l], w_gla_gate_up[l], b_gla_gate[l], g_gla_norm[l], w_gla_o[l], w_sb_o[l], w_out[l],
              ln2_g[l], ln2_b[l], ffn2_w_in[l], ffn2_w_out[l], ln3_g[l], ln3_b[l])
        s0_p = jnp.zeros((bp, GLA_HEADS, GLA_DK, GLA_DV), xp.dtype)
        past_p = jnp.zeros((bp, 0, SB_HEADS, SB_DH), xp.dtype)
        xp, sp, kp, vp = encoder_layer(xp, s0_p, past_p, past_p, *lw)
        xs, ss, ksn, vsn = encoder_layer(xs, state_gla[l], cache_sb_k[l], cache_sb_v[l], *lw)
        gla_p.append(sp); k_p.append(kp); v_p.append(vp)
        gla_s.append(ss); k_s.append(ksn); v_s.append(vsn)
    return (xp, xs, jnp.stack(gla_p), jnp.stack(k_p), jnp.stack(v_p),
            jnp.stack(gla_s), jnp.stack(k_s), jnp.stack(v_s))
```

```python
import numpy as np
from contextlib import ExitStack
import concourse.bass as bass
import concourse.mybir as mybir
from concourse.bass_utils import run_bass_kernel_spmd

F32 = mybir.dt.float32
BF16 = mybir.dt.bfloat16
AF = mybir.ActivationFunctionType
ALU = mybir.AluOpType

NCORES = 8
D = 1024
KC = 8
SEQ = 8192
NP_TOK = 2048
NS_TOK = 256
NTOK = NP_TOK + NS_TOK
PAST = 4096
DFF = 2816
NFC = 22
ALPHA = 2.0 ** 0.25
EPS = 1e-5
NEG = -30000.0
G = 768
NG = NTOK // G


class Buf:
    __slots__ = ("w", "r", "excl")

    def __init__(self, excl=False):
        self.w = None
        self.r = []
        self.excl = excl


def bufs(n):
    return [Buf() for _ in range(n)]


class DSem:
    def __init__(self, sem, inc=16):
        self.sem = sem
        self.inc = inc
        self.val = 0
        self.last = None


class Op:
    __slots__ = ("eng", "fn", "deps", "dsem", "dval", "needed", "count")


class Sched:
    ENG = ("pe", "act", "dve", "pool", "sp")

    def __init__(self, nc, esems):
        self.nc = nc
        self.esem = esems
        self.ops = []
        self.flushed = 0
        self.cnt = {e: 0 for e in self.ENG}
        self.last = {e: None for e in self.ENG}
        self.waited = {e: {} for e in self.ENG}
        self.dsems = []
        self.dq = {}
        self.dqi = {}

    def dsem(self, sem, inc=16):
        d = DSem(sem, inc)
        self.dsems.append(d)
        return d

    def add(self, eng, fn, R=(), W=(), dsem=None, extra=()):
        op = Op()
        deps = set(extra)
        if dsem is not None:
            key = "cc" if dsem == "cc" else eng
            i = self.dqi.get(key, 0)
            self.dqi[key] = i + 1
            dsem = self.dq[key][i % len(self.dq[key])]
            if dsem.last is not None:
                deps.add(dsem.last)
        op.eng, op.fn, op.dsem, op.needed, op.count, op.dval = eng, fn, dsem, False, 0, 0
        W = list(W) + [b for b in R if b.excl]
        R = [b for b in R if not b.excl]
        for b in R:
            if b.w is not None:
                deps.add(b.w)
        for b in W:
            if b.w is not None:
                deps.add(b.w)
            for r in b.r:
                deps.add(r)
        deps.discard(op)
        op.deps = deps
        for b in R:
            b.r.append(op)
        for b in W:
            b.w = op
            b.r = []
        if dsem is not None:
            dsem.val += dsem.inc
            op.dval = dsem.val
            dsem.last = op
        self.ops.append(op)
        self.last[eng] = op
        return op

    def barrier(self):
        deps = [o for o in self.last.values() if o is not None]
        deps += [d.last for d in self.dsems if d.last is not None]
        for e in self.ENG:
            self.add(e, None, extra=deps)

    def flush(self):
        self.barrier()
        ops = self.ops[self.flushed:]
        self.flushed = len(self.ops)
        for op in ops:
            for d in op.deps:
                if d.dsem is None and not (d.eng == "pe" and op.eng == "pe"):
                    d.needed = True
        for op in ops:
            if op.dsem is None and op.needed:
                self.cnt[op.eng] += 1
                op.count = self.cnt[op.eng]
        per = {e: [o for o in ops if o.eng == e] for e in self.ENG}

        def emit(e, name):
            waited = self.waited[name]
            for op in per[name]:
                w = {}
                for d in op.deps:
                    if d.dsem is not None:
                        key, so, val = ("d", id(d.dsem)), d.dsem.sem, d.dval
                    else:
                        if d.eng == "pe" and name == "pe":
                            continue
                        key, so, val = d.eng, self.esem[d.eng], d.count
                    if val > w.get(key, (None, 0))[1]:
                        w[key] = (so, val)
                for key, (so, val) in w.items():
                    if waited.get(key, 0) >= val:
                        continue
                    e.wait_ge(so, val)
                    waited[key] = val
                if op.fn is None:
                    continue
                ins = op.fn(e)
                if op.dsem is not None:
                    ins.then_inc(op.dsem.sem, op.dsem.inc)
                elif op.needed:
                    ins.then_inc(self.esem[name], 1)

        with self.nc.Block() as block:
            @block.tensor
            def _(e):
                emit(e, "pe")

            @block.scalar
            def _(e):
                emit(e, "act")

            @block.vector
            def _(e):
                emit(e, "dve")

            @block.gpsimd
            def _(e):
                emit(e, "pool")

            @block.sync
            def _(e):
                emit(e, "sp")


class Ring:
    def __init__(self, tiles):
        self.t = tiles
        self.b = bufs(len(tiles))
        self.i = 0

    def next(self):
        k = self.i % len(self.t)
        self.i += 1
        return self.t[k], self.b[k]


def build_nc(dbg=False, stop=None, test2b=False):
    nc = bass.Bass("TRN2", target_bir_lowering=False)

    T2B = ("consts", "consts2", "ws_gla", "ws_sb", "wglr", "wup", "gnorm", "state_s", "kcT", "vc")

    def din(name, shape, dt=F32):
        if test2b and name not in T2B:
            return None
        return nc.dram_tensor(name, list(shape), dt, kind="ExternalInput").ap()

    def dout(name, shape, dt=F32):
        return nc.dram_tensor(name, list(shape), dt, kind="ExternalOutput").ap()

    def dint(name, shape, dt):
        return nc.dram_tensor(name, list(shape), dt, kind="Internal").ap()

    xT_d = din("xT", [D, NTOK])
    xtm_d = din("xtm", [NTOK, D])
    f1_win_d = din("f1_win", [NFC, 128, KC * 256])
    f1_wout_d = din("f1_wout", [128, NFC * D])
    f2_win_d = din("f2_win", [NFC, 128, KC * 256])
    f2_wout_d = din("f2_wout", [128, NFC * D])
    lnp_d = din("lnp", [6, D])
    consts_d = din("consts", [128, 8 * 128])
    consts2_d = din("consts2", [128, 1024])
    wp_gla_d = din("wp_gla", [128, KC, 896])
    wp_sb_d = din("wp_sb", [128, KC, 1024])
    ws_gla_d = din("ws_gla", [4, 128, KC, 896])
    ws_sb_d = din("ws_sb", [128, KC, 4096])
    wglr_d = din("wglr", [128, KC, 16])
    wup_d = din("wup", [5, 128, 128])
    gnorm_d = din("gnorm", [5, 256])
    state_s_d = din("state_s", [4, 4, 128, 256])
    kcT_d = din("kcT", [4, 8, 128, PAST])
    vc_d = din("vc", [4, 128, 32, D])
    wmix_d = din("wmix", [8, 128, KC, 512])
    wo_d = din("wo", [128, KC, D])
    y_d = dout("y", [NTOK, D])
    st_p_d = dout("st_p", [128, 256])
    sbk_p_d = dout("sbk_p", [SEQ, 256])
    sbv_p_d = dout("sbv_p", [SEQ, 256])
    st_s_d = dout("st_s", [4, 4, 128, 256])
    sbk_s_d = dout("sbk_s", [NS_TOK, D])
    sbv_s_d = dout("sbv_s", [NS_TOK, D])
    h_tm_d = dint("h_tm", [NTOK, D], F32)
    hT_p_d = dint("hT_p", [4 * D, 512], BF16)
    ag1_d = dint("ag1", [4 * 4 * D, 512], BF16)
    ag2s_d = dint("ag2s", [8 * 512, 1024], BF16)
    ag2_d = dint("ag2", [8 * 2048, 1024], BF16)
    own_d = dint("ag2own", [4 * 512, NP_TOK], BF16)
    h2_tm_d = dint("h2_tm", [NTOK, D], F32)
    h2T_d = dint("h2T", [D, NTOK], BF16)
    GROUPS = [[0, 1, 2, 3], [4, 5, 6, 7]]

    es = ExitStack()
    with es:
        def sem(name):
            return es.enter_context(nc.semaphore(name))

        esems = {e: sem("s_" + e) for e in Sched.ENG}
        S = Sched(nc, esems)
        for q_, n_, inc_ in (("sp", 12, 16), ("pool", 12, 16), ("cc", 4, 1)):
            S.dq[q_] = [S.dsem(sem("dq_%s%d" % (q_, i)), inc_) for i in range(n_)]

        def sb(name, shape, dt, stack=es):
            return stack.enter_context(nc.sbuf_tensor(name, list(shape), dt))

        PS = [es.enter_context(nc.psum_tensor("ps%d" % i, [128, 1024], F32)) for i in range(4)]
        PSQ = [[Buf(excl=True)] * 4 for _ in range(8)]

        def bank(i):
            return PS[i // 2][:, (i % 2) * 512:(i % 2) * 512 + 512], PSQ[i]

        def mm(out, lhsT, rhs, start, stop, R, W, sg=False):
            if sg:
                return S.add("pe", lambda e: e.matmul(out, lhsT, rhs, start=start, stop=stop, skip_group_check=True),
                             R=R, W=W)
            return S.add("pe", lambda e: e.matmul(out, lhsT, rhs, start=start, stop=stop), R=R, W=W)

        def act(out, in_, func, R, W, bias=0.0, scale=1.0, accum_out=None):
            if accum_out is None:
                return S.add("act", lambda e: e.activation(out, in_, func, bias=bias, scale=scale), R=R, W=W)
            return S.add("act", lambda e: e.activation(out, in_, func, bias=bias, scale=scale,
                                                       accum_out=accum_out), R=R, W=W)

        def tt(eng, out, in0, in1, op, R, W):
            return S.add(eng, lambda e: e.tensor_tensor(out, in0, in1, op), R=R, W=W)

        def ts(eng, out, in0, s1, s2, op0, op1, R, W):
            if op1 is None:
                return S.add(eng, lambda e: e.tensor_scalar(out, in0, s1, None, op0), R=R, W=W)
            return S.add(eng, lambda e: e.tensor_scalar(out, in0, s1, s2, op0, op1), R=R, W=W)

        def stt(eng, out, in0, scalar, in1, op0, op1, R, W):
            return S.add(eng, lambda e: e.scalar_tensor_tensor(out, in0, scalar, in1, op0, op1), R=R, W=W)

        def recip(out, in_, R, W):
            return S.add("dve", lambda e: e.reciprocal(out, in_), R=R, W=W)

        def cp(eng, out, in_, R, W, scale=None):
            if eng == "act":
                if scale is None:
                    return S.add("act", lambda e: e.activation(out, in_, AF.Copy), R=R, W=W)
                return S.add("act", lambda e: e.activation(out, in_, AF.Copy, scale=scale), R=R, W=W)
            return S.add(eng, lambda e: e.tensor_copy(out, in_), R=R, W=W)

        def memset(out, val, W):
            return S.add("dve", lambda e: e.memset(out, val), R=[], W=W)

        def dma(q, out, in_, ds, R, W):
            return S.add(q, lambda e: e.dma_start(out=out, in_=in_), R=R, W=W, dsem=ds)

        def dsem(name, inc=16):
            return "cc" if inc == 1 else "auto"

        cst = sb("cst", [128, 8 * 128], BF16)
        cstf = sb("cstf", [128, 128], F32)
        B_cst = Buf()
        ds_c = dsem("d_c")
        dma("pool", cst[:], consts_d, ds_c, [], [B_cst])
        dma("sp", cstf[:], consts_d[:, 6 * 128:7 * 128], ds_c, [], [B_cst])
        ident = cst[:, 0:128]
        NEGTRI = cst[:, 128:256]
        NEGTRIC = cst[:, 256:384]
        MASKB = cst[:, 384:512]
        TRI_INCL = cst[:, 512:640]
        TRI_REV = cst[:, 640:768]
        NEGONES = cst[:, 896:1024]
        CAUSAL2 = cstf[:, 0:128]

        hT_s = sb("hT_s", [128, KC, NS_TOK], BF16)
        hT_sB = Buf()
        oaT_s = sb("oaT_s", [128, KC, NS_TOK], BF16)
        oaT_sB = bufs(8)
        obT_s = sb("obT_s", [128, KC, NS_TOK], BF16)
        obT_sB = bufs(4)
        lnw = dict(
            st=Ring([sb("ln_st%d" % i, [128, 12], F32) for i in range(2)]),
            mv=Ring([sb("ln_mv%d" % i, [128, 4], F32) for i in range(2)]),
        )

        def load_lnp(stack, li):
            t = sb("lnp_sb%d" % li, [128, 2 * D], F32, stack)
            b = Buf()
            for i in range(2):
                dma("sp", t[:, i * D:(i + 1) * D], lnp_d[2 * li + i:2 * li + i + 1, :].partition_broadcast(128),
                    ds_c, [], [b])
            return t, b

        def layer_norm(r_ap, rB, lnp, lnpB):
            st, stB = lnw["st"].next()
            mv, mvB = lnw["mv"].next()
            S.add("dve", lambda e: e.bn_stats(st[:, 0:6], r_ap[:, 0:512]), R=[rB], W=[stB])
            S.add("dve", lambda e: e.bn_stats(st[:, 6:12], r_ap[:, 512:1024]), R=[rB, stB], W=[stB])
            S.add("dve", lambda e: e.bn_aggr(mv[:, 0:2], st[:, 0:12]), R=[stB], W=[mvB])
            act(mv[:, 3:4], mv[:, 1:2], AF.Sqrt, [mvB], [mvB], bias=EPS)
            recip(mv[:, 2:3], mv[:, 3:4], [mvB], [mvB])
            ts("dve", r_ap, r_ap, mv[:, 0:1], mv[:, 2:3], ALU.subtract, ALU.mult, [rB, mvB], [rB])
            tt("dve", r_ap, r_ap, lnp[:, 0:D], ALU.mult, [rB, lnpB], [rB])
            tt("dve", r_ap, r_ap, lnp[:, D:2 * D], ALU.add, [rB, lnpB], [rB])

        def transpose_tile(hb, hbB, dst_fn, t):
            for half in range(2):
                pt, ptB = bank(6 + half)
                for j in range(4):
                    kc = half * 4 + j
                    mm(pt[:, j * 128:(j + 1) * 128], hb[:, kc * 128:(kc + 1) * 128], ident, True, True,
                       [hbB, B_cst], ptB[j:j + 1])
                dst, dB = dst_fn(half)
                cp("act" if half else "dve", dst, pt.rearrange("p (j n) -> p j n", j=4), ptB, [dB])

        def ffn_phase(tag, srcT_d, res_d, win_d, wout_d, li, out_tm_d, hT_out):
            with ExitStack() as p1:
                xT = sb(tag + "xT", [128, KC, NTOK], BF16, p1)
                xTB = bufs(NG)
                ds_x = dsem(tag + "d_x")
                xTv = srcT_d.rearrange("(kc p) t -> p kc t", p=128)
                for gi in range(NG):
                    for kc in range(KC):
                        dma("pool", xT[:, kc, gi * G:(gi + 1) * G], xTv[:, kc, gi * G:(gi + 1) * G], ds_x, [],
                            [xTB[gi]])
                wout = sb(tag + "wout", [128, NFC * D], BF16, p1)
                woutB = Buf()
                ds_wo = dsem(tag + "d_wo")
                for i in range(2):
                    dma("pool", wout[:, i * 11 * D:(i + 1) * 11 * D], wout_d[:, i * 11 * D:(i + 1) * 11 * D], ds_wo,
                        [], [woutB])
                lnp, lnpB = load_lnp(p1, li)
                wring = Ring([sb(tag + "w%d" % i, [128, KC * 256], BF16, p1) for i in range(3)])
                wds = [dsem(tag + "d_w%d" % i) for i in range(3)]
                sgr = Ring([sb(tag + "sg%d" % i, [128, 512], F32, p1) for i in range(2)])
                actT = sb(tag + "actT", [128, NFC, G], BF16, p1)
                actB = [bufs(2) for _ in range(NFC)]
                rr = Ring([sb(tag + "r%d" % i, [128, D], F32, p1) for i in range(3)])
                ds_ho = [dsem(tag + "d_ho%d" % i) for i in range(3)]
                xres = Ring([sb(tag + "x%d" % i, [128, D], F32, p1) for i in range(2)])
                xres_ds = [dsem(tag + "d_xr%d" % i) for i in range(2)]
                if hT_out is not None:
                    hbf = Ring([sb(tag + "hbf%d" % i, [128, D], BF16, p1) for i in range(2)])
                    hTg = sb(tag + "hTg", [128, KC, G], BF16, p1)
                    hTgB = [bufs(2) for _ in range(G // 128)]
                    hTgAll = [b for bb in hTgB for b in bb]
                nblk = [(o, min(512, G - o)) for o in range(0, G, 512)]
                gu = 0
                for gi in range(NG):
                    t0 = gi * G
                    for c in range(NFC):
                        w, wB = wring.next()
                        dma("pool", w[:], win_d[c], wds[(wring.i - 1) % 3], [], [wB])
                        for bi, (o, n) in enumerate(nblk):
                            pg, pgB = bank(2 * (gu % 2))
                            pu, puB = bank(2 * (gu % 2) + 1)
                            gu += 1
                            for kc in range(KC):
                                mm(pg[:, :n], w[:, kc * 256:kc * 256 + 128], xT[:, kc, t0 + o:t0 + o + n], kc == 0,
                                   kc == KC - 1, [wB, xTB[gi]], pgB)
                            for kc in range(KC):
                                mm(pu[:, :n], w[:, kc * 256 + 128:kc * 256 + 256], xT[:, kc, t0 + o:t0 + o + n],
                                   kc == 0, kc == KC - 1, [wB, xTB[gi]], puB)
                            sg, sgB = sgr.next()
                            act(sg[:, :n], pg[:, :n], AF.Silu, pgB, [sgB])
                            tt("dve", actT[:, c, o:o + n], sg[:, :n], pu[:, :n], ALU.mult, [sgB] + puB, [actB[c][bi]])
                    for t in range(G // 128):
                        po = PS[2]
                        for c in range(NFC):
                            for hf in range(2):
                                mm(po[:, hf * 512:(hf + 1) * 512], actT[:, c, t * 128:(t + 1) * 128],
                                   wout[:, c * D + hf * 512:c * D + hf * 512 + 512], c == 0, c == NFC - 1,
                                   [actB[c][(t * 128) // 512], woutB], PSQ[4 + hf])
                        x, xB = xres.next()
                        dma("sp", x[:], res_d[t0 + t * 128:t0 + (t + 1) * 128, :], xres_ds[(xres.i - 1) % 2], [], [xB])
                        r, rB = rr.next()
                        k = (rr.i - 1) % 3
                        act(r[:], x[:], AF.Copy, [xB], [rB], scale=ALPHA)
                        stt("dve", r[:], po[:], 0.5, r[:], ALU.mult, ALU.add, PSQ[4] + PSQ[5] + [rB], [rB])
                        layer_norm(r[:], rB, lnp, lnpB)
                        dma("sp", out_tm_d[t0 + t * 128:t0 + (t + 1) * 128, :], r[:], ds_ho[k], [rB], [])
                        if hT_out is not None:
                            hb, hbB = hbf.next()
                            cp("act", hb[:], r[:], [rB], [hbB])
                            transpose_tile(hb, hbB, lambda half, t=t: (
                                hTg[:, half * 4:half * 4 + 4, t * 128:(t + 1) * 128], hTgB[t][half]), t)
                    if hT_out is not None:
                        hT_out(gi, hTg, hTgAll)
                S.flush()

        ds_hT = dsem("d_hT")
        hTv = hT_p_d.rearrange("(j kc p) t -> j p kc t", j=4, p=128)
        B_hTp = bufs(4)

        def ship_h(gi, hTg, hB):
            t0 = gi * G
            npr = max(0, min(NP_TOK - t0, G))
            for j in range(4):
                a, b = max(t0, 512 * j), min(t0 + npr, 512 * (j + 1))
                if a < b:
                    dma("sp", hTv[j, :, :, a - 512 * j:b - 512 * j], hTg[:, :, a - t0:b - t0], ds_hT, hB, [B_hTp[j]])
            if npr < G:
                cp("dve", hT_s[:, :, :], hTg[:, :, npr:G], hB, [hT_sB])

        if test2b:
            hT_in_d = nc.dram_tensor("hT_s_in", [D, NS_TOK], F32, kind="ExternalInput").ap()
            dma("pool", hT_s[:], hT_in_d.rearrange("(kc p) t -> p kc t", p=128), ds_c, [], [hT_sB])
        else:
            ffn_phase("f1", xT_d, xtm_d, f1_win_d, f1_wout_d, 0, h_tm_d, ship_h)
        if stop == 1:
            return nc

        cc1 = dsem("cc1", 1)
        B_ag1 = bufs(4)

        def ag_fn(src, dst):
            return lambda e: e.collective_compute("AllGather", ALU.bypass, replica_groups=GROUPS, ins=[src], outs=[dst])
        for j in range(4):
            if test2b:
                break
            S.add("pool", ag_fn(hT_p_d[j * D:(j + 1) * D, :], ag1_d[j * 4 * D:(j + 1) * 4 * D, :]),
                  R=[B_hTp[j]], W=[B_ag1[j]], dsem=cc1)

        if stop == 1.5:
            S.flush()
            return nc
        LN_QS = float(np.log(128.0 ** -0.5))
        with ExitStack() as p2:
            cst2 = sb("cst2", [128, 1024], BF16, p2)
            dma("pool", cst2[:], consts2_d, ds_c, [], [B_cst])
            wglr = sb("wglr_sb", [128, KC, 128], BF16, p2)
            B_wglr = Buf()
            memset(wglr[:], 0.0, [B_wglr])
            dma("pool", wglr[:, :, 0:16], wglr_d, ds_c, [], [B_wglr])
            wgla = Ring([sb("wgla%d" % i, [128, KC, 896], BF16, p2) for i in range(2)])
            wgla_ds = [dsem("d_wgla%d" % i) for i in range(2)]
            wupr = Ring([sb("wup%d" % i, [128, 128], BF16, p2) for i in range(2)])
            gnbr = Ring([sb("gnb%d" % i, [128, 256], F32, p2) for i in range(2)])

            def load_gla_w(src_ap, hidx):
                w, wB = wgla.next()
                k = (wgla.i - 1) % 2
                dma("pool", w[:], src_ap, wgla_ds[k], [], [wB])
                wu, wuB = wupr.next()
                dma("pool", wu[:], wup_d[hidx], wgla_ds[k], [], [wuB])
                gn, gnB = gnbr.next()
                dma("sp", gn[:], gnorm_d[hidx:hidx + 1, :].partition_broadcast(128), wgla_ds[k], [], [gnB])
                W = dict(q=w[:, :, 0:128], k=w[:, :, 128:256], vr=w[:, :, 256:768], ktm=w[:, :, 768:896], B=wB)
                return W, (wu[:], wuB), (gn[:], gnB)

            def ring(name, shape, dt, n=2, zero=False, ones_row=False):
                tiles = [sb("%s%d" % (name, i), shape, dt, p2) for i in range(n)]
                r = Ring(tiles)
                if zero:
                    for t, b in zip(tiles, r.b):
                        memset(t[:], 0.0, [b])
                        if ones_row:
                            memset(t[32:33, :], 1.0, [b])
                return r

            R = dict(
                glrT=ring("g_glrT", [128, 128], BF16, zero=True, ones_row=True),
                e1=ring("g_e1", [128, 128], F32), la=ring("g_la", [128, 128], BF16),
                eb=ring("g_eb", [128, 128], F32), enb=ring("g_enb", [128, 128], F32),
                ek=ring("g_ek", [128, 128], F32), dec=ring("g_dec", [128, 2], F32),
                qA=ring("g_qA", [128, 128], BF16, zero=True), qB=ring("g_qB", [128, 128], BF16, zero=True),
                kg=ring("g_kg", [128, 128], BF16),
                kd0=ring("g_kd0", [128, 128], BF16, zero=True), kd1=ring("g_kd1", [128, 128], BF16, zero=True),
                vb=ring("g_vb", [128, 256], BF16), eg=ring("g_eg", [128, 256], F32),
                gg=ring("g_gg", [128, 256], F32), at=ring("g_at", [128, 128], BF16),
                ss=ring("g_ss", [128, 4], F32), oa=ring("g_oa", [128, 256], BF16),
                sbf=ring("g_sbf", [128, 256], BF16, n=6),
            )
            junk = sb("g_junk", [128, 256], F32, p2)
            junkB = Buf()

            class State:
                pass

            def snapshot(st):
                nb, nbB = R["sbf"].next()
                cp("act", nb[:], st.f32, [st.fB], [nbB])
                st.bf, st.bfB = nb[:], nbB

            def update(st, dec_ap, decB, U_ap, UB):
                stt("dve", st.f32, st.f32, dec_ap, U_ap, ALU.mult, ALU.add, [st.fB, decB] + UB, [st.fB])
                snapshot(st)

            def gla_tile(W, hT_t, hTB, wup, gnb, states, dst_fn):
                pA, Aq = bank(0)
                pB_, Bq = bank(1)
                pC, Cq = bank(2)
                pD, Dq = bank(3)
                wu, wuB = wup
                gn, gnB = gnb
                for j, key in enumerate(("q", "k", "glr")):
                    wk = wglr if key == "glr" else W[key]
                    wkB = B_wglr if key == "glr" else W["B"]
                    for kc in range(KC):
                        mm(pA[:, j * 128:(j + 1) * 128], wk[:, kc, :], hT_t[:, kc, :], kc == 0, kc == KC - 1,
                           [wkB] + hTB, Aq[j:j + 1])
                for kc in range(KC):
                    mm(pB_[:, 0:512], hT_t[:, kc, :], W["vr"][:, kc, :], kc == 0, kc == KC - 1, [W["B"]] + hTB, Bq)
                for kc in range(KC):
                    mm(pC[:, 0:128], hT_t[:, kc, :], W["ktm"][:, kc, :], kc == 0, kc == KC - 1, [W["B"]] + hTB,
                       Cq[0:1])
                yield
                gp, gpB = R["glrT"].next()
                cp("dve", gp[0:32, :], pA[0:32, 256:384], Aq[2:3], [gpB])
                mm(pC[:, 128:256], gp[:], wu, True, True, [gpB, wuB], Cq[1:2])
                yield
                e1, e1B = R["e1"].next()
                la, laB = R["la"].next()
                act(e1[:], pC[:, 128:256], AF.Exp, Cq[1:2], [e1B], scale=-1.0)
                act(la[:], e1[:], AF.Ln, [e1B], [laB], bias=1.0)
                mm(pC[:, 256:384], la[:], TRI_INCL, True, True, [laB, B_cst], Cq[2:3])
                mm(pC[:, 384:512], TRI_REV, la[:], True, True, [laB, B_cst], Cq[3:4])
                yield
                eb, ebB = R["eb"].next()
                enb, enbB = R["enb"].next()
                ek, ekB = R["ek"].next()
                dec, decB = R["dec"].next()
                act(eb[:], pC[:, 256:384], AF.Exp, Cq[2:3], [ebB], bias=LN_QS)
                act(enb[:], pC[:, 256:384], AF.Exp, Cq[2:3], [enbB], scale=-1.0)
                act(ek[:], pC[:, 384:512], AF.Exp, Cq[3:4], [ekB])
                act(dec[:, 0:1], pC[:, 256 + 63:256 + 64], AF.Exp, Cq[2:3], [decB])
                act(dec[:, 1:2], pC[:, 256 + 127:256 + 128], AF.Exp, Cq[2:3], [decB])
                yield
                qA, qAB = R["qA"].next()
                qB, qBB = R["qB"].next()
                tt("dve", qA[:, 0:64], pA[:, 0:64], eb[:, 0:64], ALU.mult, Aq[0:1] + [ebB], [qAB])
                tt("dve", qB[:, 64:128], pA[:, 64:128], eb[:, 64:128], ALU.mult, Aq[0:1] + [ebB], [qBB])
                kg, kgB = R["kg"].next()
                tt("dve", kg[:], pA[:, 128:256], enb[:], ALU.mult, Aq[1:2] + [enbB], [kgB])
                kd0, kd0B = R["kd0"].next()
                kd1, kd1B = R["kd1"].next()
                tt("dve", kd0[0:64, :], pC[0:64, 0:128], ek[0:64, :], ALU.mult, Cq[0:1] + [ekB], [kd0B])
                tt("dve", kd1[64:128, :], pC[64:128, 0:128], ek[64:128, :], ALU.mult, Cq[0:1] + [ekB], [kd1B])
                yield
                vb, vbB = R["vb"].next()
                cp("act", vb[:], pB_[:, 0:256], Bq[0:2], [vbB])
                eg, egB = R["eg"].next()
                gg, ggB = R["gg"].next()
                act(eg[:], pB_[:, 256:512], AF.Exp, Bq[2:4], [egB], scale=-1.0)
                act(eg[:], eg[:], AF.Ln, [egB], [egB], bias=1.0)
                act(eg[:], eg[:], AF.Exp, [egB], [egB], scale=-1.0)
                tt("dve", gg[:], pB_[:, 256:512], eg[:], ALU.mult, Bq[2:4] + [egB], [ggB])
                tt("dve", gg[:], gg[:], gn, ALU.mult, [ggB, gnB], [ggB])
                yield
                mm(pA[:, 384:512], kg[:], qA[:], True, False, [kgB, qAB], Aq[3:4])
                mm(pA[:, 384:512], kg[:], qB[:], False, True, [kgB, qBB], Aq[3:4])
                at, atB = R["at"].next()
                tt("dve", at[:], pA[:, 384:512], CAUSAL2, ALU.mult, Aq[3:4] + [B_cst], [atB])
                yield
                st0, st1 = states
                mm(pD[:, 0:256], at[:], vb[:], True, False, [atB, vbB], Dq[0:2])
                mm(pD[:, 0:256], qA[:], st0.bf, False, False, [qAB, st0.bfB], Dq[0:2])
                mm(pB_[:, 0:256], kd0[:], vb[:], True, True, [kd0B, vbB], Bq[0:2])
                mm(pB_[:, 256:512], kd1[:], vb[:], True, True, [kd1B, vbB], Bq[2:4])
                update(st0, dec[:, 0:1], decB, pB_[:, 0:256], Bq[0:2])
                yield
                mm(pD[:, 0:256], qB[:], st1.bf, False, True, [qBB, st1.bfB], Dq[0:2])
                update(st1, dec[:, 1:2], decB, pB_[:, 256:512], Bq[2:4])
                yield
                ss, ssB = R["ss"].next()
                act(junk[:], pD[:, 0:256], AF.Square, Dq[0:2], [junkB, ssB], accum_out=ss[:, 0:1])
                act(ss[:, 1:2], ss[:, 0:1], AF.Ln, [ssB], [ssB], scale=1.0 / 256, bias=EPS)
                act(ss[:, 2:3], ss[:, 1:2], AF.Exp, [ssB], [ssB], scale=-0.5)
                oa, oaB = R["oa"].next()
                stt("dve", oa[:], pD[:, 0:256], ss[:, 2:3], gg[:], ALU.mult, ALU.mult, Dq[0:2] + [ssB, ggB], [oaB])
                yield
                for c in range(2):
                    mm(pD[:, 256 + c * 128:256 + (c + 1) * 128], oa[:, c * 128:(c + 1) * 128], ident, True, True,
                       [oaB, B_cst], Dq[2 + c:3 + c])
                dst, dstB = dst_fn()
                cp("act", dst, pD[:, 256:512].rearrange("p (c n) -> p c n", c=2), Dq[2:4], dstB)

            Er = ring("s_E", [128, 512], BF16, 4)
            Lr = ring("s_L", [128, 512], BF16, 4)
            LSr = ring("s_LS", [128, 512], BF16, 3)
            Xr = ring("s_X", [128, 512], BF16, 3)
            Wr = ring("s_W", [128, 512], BF16, 4)
            zcnt = [0]

            def record(gen):
                rec = []
                S.add = lambda eng, fn, R=(), W=(), dsem=None, extra=(): rec.append((eng, fn, R, W, dsem, extra))
                try:
                    for _ in gen:
                        pass
                finally:
                    del S.add
                return rec

            class Replay:
                def __init__(self, rec, iters):
                    self.rec = rec
                    self.i = 0
                    self.k = max(1, -(-len(rec) // max(1, iters)))

                def step(self):
                    for _ in range(self.k):
                        if self.i < len(self.rec):
                            S.add(*self.rec[self.i])
                            self.i += 1

                def drain(self):
                    while self.i < len(self.rec):
                        S.add(*self.rec[self.i])
                        self.i += 1

            def sb_stream(blocks, evict, bg=None):
                ob, obB = bank(4)
                n = len(blocks)
                st = [None] * n

                def stage1(i):
                    if "lazy" in blocks[i]:
                        post = blocks[i].get("post")
                        blocks[i] = blocks[i]["lazy"]()
                        if post is not None:
                            blocks[i]["post"] = post
                    blk = blocks[i]
                    c0 = blk["c0"]
                    zb, zbB = bank(5 + zcnt[0] % 2)
                    zcnt[0] += 1
                    mk = blk.get("mask")
                    mfirst = blk.get("mask_first", False)
                    if mk is not None and mfirst:
                        mm(zb[:, mk[2]:mk[3]], mk[0], mk[1], True, False, [B_cst], zbB, sg=True)
                    for (kT, q, a, b, rb) in blk["z"]:
                        mm(zb[:, a:b], kT, q, not (mk is not None and mfirst), True, rb, zbB, sg=True)
                    if mk is not None and not mfirst:
                        mm(zb[:, mk[2]:mk[3]], mk[0], mk[1], False, True, [B_cst], zbB, sg=True)
                    E, EB = Er.next()
                    act(E[:, c0:512], zb[:, c0:512], AF.Exp, zbB, [EB])
                    st[i] = dict(E=E, EB=EB)

                def stage1b(i):
                    c0 = blocks[i]["c0"]
                    E, EB = st[i]["E"], st[i]["EB"]
                    L, LB = Lr.next()
                    act(L[:, c0:512], E[:, c0:512], AF.Ln, [EB], [LB], bias=1.0)
                    if i == 0:
                        ls = (L, LB, c0)
                    elif i < n - 1:
                        pLs, pLsB, pc0 = st[i - 1]["ls"]
                        Ls, LsB = LSr.next()
                        tt("dve", Ls[:, pc0:512], pLs[:, pc0:512], L[:, pc0:512], ALU.add, [pLsB, LB], [LsB])
                        if c0 < pc0:
                            cp("dve", Ls[:, c0:pc0], L[:, c0:pc0], [LB], [LsB])
                        ls = (Ls, LsB, c0)
                    else:
                        ls = None
                    st[i].update(L=L, LB=LB, ls=ls)

                def stage2(i):
                    c0 = blocks[i]["c0"]
                    d = st[i]
                    tb, tbB = bank(7)
                    mm(tb[:, c0:512], NEGTRI, d["L"][:, c0:512], True, i == 0, [d["LB"], B_cst], tbB, sg=True)
                    if i >= 1:
                        pLs, pLsB, pc0 = st[i - 1]["ls"]
                        mm(tb[:, pc0:512], NEGONES, pLs[:, pc0:512], False, True, [pLsB, B_cst], tbB, sg=True)
                    X, XB = Xr.next()
                    act(X[:, c0:512], tb[:, c0:512], AF.Exp, tbB, [XB])
                    w, wB = Wr.next()
                    tt("dve", w[:, c0:512], d["E"][:, c0:512], X[:, c0:512], ALU.mult, [d["EB"], XB], [wB])
                    d["w"] = (w, wB)

                def stage3(i):
                    w, wB = st[i]["w"]
                    for j, (v, a, b, rb) in enumerate(blocks[i]["pv"]):
                        mm(ob[:, a:b], v, w[:, a:b], i == 0 and j == 0, True, [wB] + rb, obB, sg=True)

                for i in range(n + 2):
                    if i < n:
                        stage1(i)
                    if 1 <= i <= n:
                        stage2(i - 1)
                    if i < n:
                        stage1b(i)
                    if i >= 2:
                        stage3(i - 2)
                    if i < n and blocks[i].get("post") is not None:
                        blocks[i]["post"]()
                    if bg is not None:
                        bg.step()
                evict(ob, obB)

            with ExitStack() as p2b:
                wpc = Ring([sb("wpc%d" % i, [128, KC, 512], BF16, p2b) for i in range(2)])
                wpc_ds = [dsem("d_wpc%d" % i) for i in range(2)]
                qT_s = sb("qT_s", [128, 8, NS_TOK], BF16, p2b)
                kT_s = sb("kT_s", [128, 8, NS_TOK], BF16, p2b)
                v_s = sb("v_s", [128, 2, D], BF16, p2b)
                qkB = bufs(16)
                v_sB = [bufs(2) for _ in range(2)]
                kvst = Ring([sb("kvst%d" % i, [128, 512], F32, p2b) for i in range(2)])
                kvst_ds = [dsem("d_kvst%d" % i) for i in range(2)]
                pcnt = 0
                import os
                KV = os.environ.get("KVAR", "")
                for piece in range(4):
                    if "nofm" in KV:
                        break
                    w, wB = wpc.next()
                    dma("pool", w[:], ws_sb_d[:, :, piece * 512:(piece + 1) * 512], wpc_ds[(wpc.i - 1) % 2], [], [wB])
                    for j in range(4):
                        ch = piece * 4 + j
                        pz, pzB = bank(5 + pcnt % 2)
                        pcnt += 1
                        for kc in range(KC):
                            mm(pz[:, 0:NS_TOK], w[:, kc, j * 128:(j + 1) * 128], hT_s[:, kc, :], kc == 0, kc == KC - 1,
                               [wB, hT_sB], pzB)
                        if ch < 8:
                            cp("act", qT_s[:, ch, :], pz[:, 0:NS_TOK], pzB, [qkB[ch]], scale=float(128.0 ** -0.5))
                        else:
                            cp("dve", kT_s[:, ch - 8, :], pz[:, 0:NS_TOK], pzB, [qkB[ch]])
                for piece in range(4):
                    if "notm" in KV:
                        break
                    w, wB = wpc.next()
                    dma("pool", w[:], ws_sb_d[:, :, 2048 + piece * 512:2048 + (piece + 1) * 512],
                        wpc_ds[(wpc.i - 1) % 2], [], [wB])
                    for ti in range(2):
                        pz, pzB = bank(5 + pcnt % 2)
                        pcnt += 1
                        for kc in range(KC):
                            mm(pz[:, :], hT_s[:, kc, ti * 128:(ti + 1) * 128], w[:, kc, :], kc == 0, kc == KC - 1,
                               [wB, hT_sB], pzB)
                        stg, stgB = kvst.next()
                        cp("act", stg[:], pz[:, :], pzB, [stgB])
                        dst = sbk_s_d if piece < 2 else sbv_s_d
                        if "nodma" not in KV:
                            dma("sp", dst[ti * 128:(ti + 1) * 128, (piece % 2) * 512:(piece % 2) * 512 + 512], stg[:],
                                kvst_ds[(kvst.i - 1) % 2], [stgB], [])
                        if piece >= 2 and "novs" not in KV:
                            cp("dve", v_s[:, ti, (piece - 2) * 512:(piece - 2) * 512 + 512], pz[:, :], pzB,
                               [v_sB[ti][piece - 2]])
                if stop == 1.7:
                    S.flush()
                    return nc
                sf = Ring([sb("s_sf%d" % i, [128, 256], F32, p2b) for i in range(4)])
                sf_ds = [dsem("d_sf%d" % i) for i in range(4)]
                def sample_gla_gen():
                    for hh in range(4):
                        W, wup, gnb = load_gla_w(ws_gla_d[hh], hh)
                        for ti in range(2):
                            sts = []
                            for j in range(2):
                                st = State()
                                f, fB = sf.next()
                                st.k = (sf.i - 1) % 4
                                dma("sp", f[:], state_s_d[2 * ti + j, hh], sf_ds[st.k], [], [fB])
                                st.f32, st.fB = f[:], fB
                                snapshot(st)
                                sts.append(st)
                            yield from gla_tile(W, hT_s[:, :, ti * 128:(ti + 1) * 128], [hT_sB], wup, gnb, sts,
                                                lambda hh=hh, ti=ti: (
                                                    oaT_s[:, 2 * hh:2 * hh + 2, ti * 128:(ti + 1) * 128],
                                                    [oaT_sB[2 * hh + ti]]))
                            for j in range(2):
                                dma("sp", st_s_d[2 * ti + j, hh], sts[j].f32, sf_ds[sts[j].k], [sts[j].fB], [])
                            yield
                bg_s = Replay(record(sample_gla_gen()), 4 * 35)
                kcr = Ring([sb("kc%d" % i, [128, 8, 1024], BF16, p2b) for i in range(2)])
                vcr = Ring([sb("vc%d" % i, [128, 8, D], BF16, p2b) for i in range(2)])
                kc_ds = [dsem("d_kc%d" % i) for i in range(2)]
                vc_ds = [dsem("d_vc%d" % i) for i in range(2)]
                def load_kv(s, gk):
                    kt, ktB = kcr.next()
                    for hq in range(4):
                        dma("pool", kt[:, 2 * hq:2 * hq + 2, :],
                            kcT_d[s, 2 * hq:2 * hq + 2].rearrange("h d t -> d h t")[:, :, gk * 1024:(gk + 1) * 1024],
                            "auto", [], [ktB])
                    vt, vtB = vcr.next()
                    for hq in range(4):
                        dma("pool", vt[:, 2 * hq:2 * hq + 2, :], vc_d[s, :, gk * 8 + 2 * hq:gk * 8 + 2 * hq + 2, :],
                            "auto", [], [vtB])
                    return kt, ktB, vt, vtB

                order = [(s, gk) for s in range(4) for gk in range(3, -1, -1)]
                loaded = {order[0]: load_kv(*order[0])}
                for s in range(4):
                    ti, par = s // 2, s % 2
                    qcols = slice(s * 64, (s + 1) * 64)
                    blocks = [dict(
                        c0=0, mask=(ident, cst2[:, par * 512:(par + 1) * 512], 0, 512), mask_first=True,
                        z=[(kT_s[:, h, ti * 128:(ti + 1) * 128], qT_s[:, h, qcols], h * 64, h * 64 + 64,
                            [qkB[h], qkB[8 + h]]) for h in range(8)],
                        pv=[(v_s[:, ti, h * 128:(h + 1) * 128], h * 64, h * 64 + 64, v_sB[ti]) for h in range(8)])]
                    for gk in range(3, -1, -1):
                        for kb in range(7, -1, -1):
                            def mk(s=s, gk=gk, kb=kb, qcols=qcols):
                                kt, ktB, vt, vtB = loaded[(s, gk)]
                                return dict(
                                    c0=0,
                                    z=[(kt[:, h, kb * 128:(kb + 1) * 128], qT_s[:, h, qcols], h * 64, h * 64 + 64,
                                        [ktB, qkB[h]]) for h in range(8)],
                                    pv=[(vt[:, kb, h * 128:(h + 1) * 128], h * 64, h * 64 + 64, [vtB])
                                        for h in range(8)])
                            blk = dict(lazy=mk)
                            if kb == 6:
                                nxt = order.index((s, gk)) + 1
                                if nxt < len(order):
                                    blk["post"] = (lambda nxt=nxt: loaded.__setitem__(order[nxt], load_kv(*order[nxt])))
                            blocks.append(blk)

                    def evict(ob, obB, s=s):
                        cp("act", obT_s[:, :, s * 64:(s + 1) * 64], ob.rearrange("p (h q) -> p h q", h=8), obB,
                           [obT_sB[s]])
                    sb_stream(blocks, evict, bg_s)
                bg_s.drain()
                S.flush()
            if test2b:
                dbg_oa = dout("dbg_oa", [128, KC, NS_TOK], BF16)
                dbg_ob = dout("dbg_ob", [128, KC, NS_TOK], BF16)
                dma("sp", dbg_oa, oaT_s[:], "auto", oaT_sB, [])
                dma("sp", dbg_ob, obT_s[:], "auto", obT_sB, [])
                S.flush()
            if stop == 2 or test2b:
                return nc

            with ExitStack() as p2a:
                wsb = sb("wsb", [128, KC, 1024], BF16, p2a)
                B_wsb = Buf()
                for i in range(2):
                    dma("pool", wsb[:, :, i * 512:(i + 1) * 512], wp_sb_d[:, :, i * 512:(i + 1) * 512], ds_c, [],
                        [B_wsb])
                W, wup, gnb = load_gla_w(wp_gla_d, 4)
                kT_all = sb("kT_all", [128, 2, SEQ], BF16, p2a)
                v_all = sb("v_all", [128, 64, 256], BF16, p2a)
                kvB = bufs(16)
                hTb = Ring([sb("hTb%d" % i, [128, KC, 512], BF16, p2a) for i in range(2)])
                hTb_ds = [dsem("d_hTb%d" % i) for i in range(2)]
                qTb = Ring([sb("qTb%d" % i, [128, 2, 512], BF16, p2a) for i in range(2)])
                kvst = Ring([sb("kvstp%d" % i, [128, 512], F32, p2a) for i in range(2)])
                kvst_ds = [dsem("d_kvstp%d" % i) for i in range(2)]
                oaTb = Ring([sb("oaTb%d" % i, [128, 2, 512], BF16, p2a) for i in range(2)])
                oaTb_ds = [dsem("d_oaTb%d" % i) for i in range(2)]
                obst = Ring([sb("obst%d" % i, [128, 512], BF16, p2a) for i in range(2)])
                obst_ds = [dsem("d_obst%d" % i) for i in range(2)]
                B_ag2s = bufs(16)
                stp = State()
                stp_t = sb("stp_f32", [128, 256], F32, p2a)
                stp.f32, stp.fB = stp_t[:], Buf()
                memset(stp_t[:], 0.0, [stp.fB])
                snapshot(stp)
                ag1v = ag1_d.rearrange("(j r kc p) t -> j r p kc t", j=4, r=4, p=128)
                ag2sv = ag2s_d.rearrange("(k c p) t -> k p c t", k=8, p=128)
                cc2 = dsem("cc2", 1)
                B_ag2 = bufs(8)
                pcnt = 0
                for bi in range(16):
                    hb, hbB = hTb.next()
                    for half in range(2):
                        dma("sp", hb[:, half * 4:half * 4 + 4, :],
                            ag1v[bi % 4, bi // 4, :, half * 4:half * 4 + 4, :],
                            hTb_ds[(hTb.i - 1) % 2], [B_ag1[bi % 4]], [hbB])
                    tok = slice(bi * 512, (bi + 1) * 512)
                    qt, qtB = qTb.next()
                    for j in range(4):
                        pz, pzB = bank(5 + pcnt % 2)
                        pcnt += 1
                        for kc in range(KC):
                            mm(pz[:, :], wsb[:, kc, j * 128:(j + 1) * 128], hb[:, kc, :], kc == 0, kc == KC - 1,
                               [B_wsb, hbB], pzB)
                        if j < 2:
                            cp("act", qt[:, j, :], pz[:, :], pzB, [qtB], scale=float(128.0 ** -0.5))
                        else:
                            cp("dve", kT_all[:, j - 2, tok], pz[:, :], pzB, [kvB[bi]])
                    for t in range(4):
                        pz, pzB = bank(5 + pcnt % 2)
                        pcnt += 1
                        for kc in range(KC):
                            mm(pz[:, :], hb[:, kc, t * 128:(t + 1) * 128], wsb[:, kc, 512:1024], kc == 0, kc == KC - 1,
                               [B_wsb, hbB], pzB)
                        stg, stgB = kvst.next()
                        k_ = (kvst.i - 1) % 2
                        cp("act", stg[:], pz[:, :], pzB, [stgB])
                        rows = slice(bi * 512 + t * 128, bi * 512 + (t + 1) * 128)
                        dma("sp", sbk_p_d[rows, :], stg[:, 0:256], kvst_ds[k_], [stgB], [])
                        dma("sp", sbv_p_d[rows, :], stg[:, 256:512], kvst_ds[k_], [stgB], [])
                        cp("dve", v_all[:, bi * 4 + t, :], pz[:, 256:512], pzB, [kvB[bi]])
                    tk2 = slice((bi % 2) * 512, (bi % 2) * 512 + 512)

                    def gla_block_gen(bi=bi, hb=hb, hbB=hbB, tk2=tk2):
                        ot, otB = oaTb.next()
                        for t in range(4):
                            yield from gla_tile(W, hb[:, :, t * 128:(t + 1) * 128], [hbB], wup, gnb, [stp, stp],
                                                lambda t=t: (ot[:, :, t * 128:(t + 1) * 128], [otB]))
                        dma("sp", ag2sv[bi // 2, :, 0:2, tk2], ot[:], "auto", [otB], [B_ag2s[bi]])
                    bg_p = Replay(record(gla_block_gen()), 2 * (4 * bi + 6))
                    for hh in range(2):
                        blocks = []
                        for j in range(3, -1, -1):
                            kb = 4 * bi + j
                            c0 = 128 * j
                            blocks.append(dict(
                                c0=c0, mask=(ident, MASKB, c0, c0 + 128), mask_first=False,
                                z=[(kT_all[:, hh, kb * 128:(kb + 1) * 128], qt[:, hh, c0:512], c0, 512,
                                    [kvB[bi], qtB])],
                                pv=[(v_all[:, kb, hh * 128:(hh + 1) * 128], c0, 512, [kvB[bi]])]))
                        for kb in range(4 * bi - 1, -1, -1):
                            blocks.append(dict(
                                c0=0,
                                z=[(kT_all[:, hh, kb * 128:(kb + 1) * 128], qt[:, hh, :], 0, 512,
                                    [kvB[kb // 4], qtB])],
                                pv=[(v_all[:, kb, hh * 128:(hh + 1) * 128], 0, 512, [kvB[kb // 4]])]))

                        def evict(ob, obB, hh=hh, tk2=tk2, bi=bi):
                            o, oB = obst.next()
                            cp("act", o[:], ob[:, :], obB, [oB])
                            dma("sp", ag2sv[bi // 2, :, 2 + hh, tk2], o[:], obst_ds[(obst.i - 1) % 2], [oB],
                                [B_ag2s[bi]])
                        sb_stream(blocks, evict, bg_p)
                    bg_p.drain()
                    if bi % 2 == 1:
                        k = bi // 2
                        S.add("pool", ag_fn(ag2s_d[k * 512:(k + 1) * 512, :], ag2_d[k * 2048:(k + 1) * 2048, :]),
                              R=[B_ag2s[bi - 1], B_ag2s[bi]], W=[B_ag2[k]], dsem=cc2)
                dma("sp", st_p_d, stp.f32, ds_c, [stp.fB], [])
                S.flush()

        if stop == 3:
            return nc

        B_own = [bufs(4) for _ in range(2)]
        ds_own = dsem("d_own")

        pid_cache = {}

        def own_fn(kk, r4):
            def fn(e):
                if "g" not in pid_cache:
                    pid_cache["g"] = e.partition_id() % 4
                g_ = pid_cache["g"]
                return e.dma_start(out=own_d[r4 * 512:(r4 + 1) * 512, kk * 1024:(kk + 1) * 1024],
                                   in_=ag2_d[bass.ds((g_ * 2 + kk) * 2048 + r4 * 512, 512), :])
            return fn
        for kk in range(2):
            for r4 in range(4):
                S.add("pool", own_fn(kk, r4), R=B_ag2, W=[B_own[kk][r4]], dsem=ds_own)

        G3 = 512
        groups3 = [(o, min(G3, NTOK - o)) for o in range(0, NTOK, G3)]
        B_h2 = bufs(len(groups3))
        with ExitStack() as p3:
            lnp, lnpB = load_lnp(p3, 1)
            wo = sb("wo_sb", [128, KC, D], BF16, p3)
            B_wo = Buf()
            for i in range(2):
                dma("pool", wo[:, :, i * 512:(i + 1) * 512], wo_d[:, :, i * 512:(i + 1) * 512], ds_c, [], [B_wo])
            wmr = Ring([sb("wm%d" % i, [128, KC, 512], BF16, p3) for i in range(2)])
            wm_ds = [dsem("d_wm%d" % i) for i in range(2)]
            hin = Ring([sb("m_h%d" % i, [128, KC, G3], BF16, p3) for i in range(2)])
            oain = Ring([sb("m_oa%d" % i, [128, KC, G3], BF16, p3) for i in range(2)])
            obin = Ring([sb("m_ob%d" % i, [128, KC, G3], BF16, p3) for i in range(2)])
            in_ds = [dsem("d_min%d" % i) for i in range(2)]
            sgr = Ring([sb("m_sg%d" % i, [128, G3], F32, p3) for i in range(4)])
            t12 = Ring([sb("m_t%d" % i, [128, G3], F32, p3) for i in range(2)])
            mT = sb("m_mT", [128, KC, G3], BF16, p3)
            mTB = bufs(KC)
            rr = Ring([sb("m_r%d" % i, [128, D], F32, p3) for i in range(3)])
            r_ds = [dsem("d_mr%d" % i) for i in range(3)]
            hbf = Ring([sb("m_hbf%d" % i, [128, D], BF16, p3) for i in range(2)])
            h2Tg = sb("m_h2Tg", [128, KC, G3], BF16, p3)
            h2TgB = [bufs(2) for _ in range(G3 // 128)]
            ds_h2T = dsem("d_h2T")
            h2Tv = h2T_d.rearrange("(kc p) t -> p kc t", p=128)
            for gi, (t0, n) in enumerate(groups3):
                if t0 < NP_TOK:
                    hi, hiB = hin.next()
                    oi, oiB = oain.next()
                    qi, qiB = obin.next()
                    k_ = (hin.i - 1) % 2
                    dma("sp", hi[:, :, 0:n], hTv[t0 // 512], in_ds[k_], B_hTp, [hiB])
                    for r4 in range(4):
                        for which, dstt in ((0, oi), (1, qi)):
                            r0 = r4 * 512 + which * 256
                            dma("sp", dstt[:, 2 * r4:2 * r4 + 2, 0:n],
                                own_d[r0:r0 + 256, t0:t0 + n].rearrange("(c p) t -> p c t", p=128), in_ds[k_],
                                [B_own[t0 // 1024][r4]], [oiB if which == 0 else qiB])
                    hT_g, hT_gB = hi, [hiB]
                    oa_g, oa_gB = oi, [oiB]
                    ob_g, ob_gB = qi, [qiB]
                    off = 0
                else:
                    hT_g, hT_gB = hT_s, [hT_sB]
                    oa_g, oa_gB = oaT_s, oaT_sB
                    ob_g, ob_gB = obT_s, obT_sB
                    off = t0 - NP_TOK
                for fc in range(KC):
                    w, wB = wmr.next()
                    dma("pool", w[:], wmix_d[fc], wm_ds[(wmr.i - 1) % 2], [], [wB])
                    srcs = ((hT_g, hT_gB), (hT_g, hT_gB), (oa_g, oa_gB), (ob_g, ob_gB))
                    for j in range(4):
                        pz, pzB = bank(j)
                        src, srcB = srcs[j]
                        for kc in range(KC):
                            mm(pz[:, 0:n], w[:, kc, j * 128:(j + 1) * 128], src[:, kc, off:off + n], kc == 0,
                               kc == KC - 1, [wB] + srcB, pzB)
                    sga, sgaB = sgr.next()
                    sgb, sgbB = sgr.next()
                    act(sga[:, 0:n], bank(0)[0][:, 0:n], AF.Sigmoid, bank(0)[1], [sgaB])
                    act(sgb[:, 0:n], bank(1)[0][:, 0:n], AF.Sigmoid, bank(1)[1], [sgbB])
                    t1, t1B = t12.next()
                    t2, t2B = t12.next()
                    tt("dve", t1[:, 0:n], sga[:, 0:n], bank(2)[0][:, 0:n], ALU.mult, [sgaB] + bank(2)[1], [t1B])
                    tt("dve", t2[:, 0:n], sgb[:, 0:n], bank(3)[0][:, 0:n], ALU.mult, [sgbB] + bank(3)[1], [t2B])
                    tt("dve", mT[:, fc, 0:n], t1[:, 0:n], t2[:, 0:n], ALU.add, [t1B, t2B], [mTB[fc]])
                for t in range(n // 128):
                    po = PS[2]
                    for fc in range(KC):
                        for hf in range(2):
                            mm(po[:, hf * 512:(hf + 1) * 512], mT[:, fc, t * 128:(t + 1) * 128],
                               wo[:, fc, hf * 512:(hf + 1) * 512], fc == 0, fc == KC - 1, [mTB[fc], B_wo], PSQ[4 + hf])
                    r, rB = rr.next()
                    k_ = (rr.i - 1) % 3
                    rows = slice(t0 + t * 128, t0 + (t + 1) * 128)
                    dma("sp", r[:], h_tm_d[rows, :], r_ds[k_], [], [rB])
                    act(r[:], r[:], AF.Copy, [rB], [rB], scale=ALPHA)
                    tt("dve", r[:], r[:], po[:], ALU.add, [rB] + PSQ[4] + PSQ[5], [rB])
                    layer_norm(r[:], rB, lnp, lnpB)
                    dma("sp", h2_tm_d[rows, :], r[:], r_ds[k_], [rB], [B_h2[gi]])
                    hb, hbB = hbf.next()
                    cp("act", hb[:], r[:], [rB], [hbB])
                    transpose_tile(hb, hbB, lambda half, t=t: (
                        h2Tg[:, half * 4:half * 4 + 4, t * 128:(t + 1) * 128], h2TgB[t][half]), t)
                dma("sp", h2Tv[:, :, t0:t0 + n], h2Tg[:, :, 0:n], ds_h2T,
                    [b for bb in h2TgB[:n // 128] for b in bb], [B_h2[gi]])
            S.flush()

        if stop == 4:
            return nc
        ffn_phase("f2", h2T_d, h2_tm_d, f2_win_d, f2_wout_d, 2, y_d, None)
        S.flush()
    return nc


def _consts():
    c = np.zeros((128, 8 * 128), np.float32)
    s = np.arange(128)[:, None]
    t = np.arange(128)[None, :]
    same = (s // 64) == (t // 64)
    c[:, 0:128] = np.eye(128, dtype=np.float32)
    c[:, 128:256] = np.where(s >= t, -1.0, 0.0)
    c[:, 256:384] = np.where(s < t, -1.0, 0.0)
    c[:, 384:512] = np.where(s < t, 0.0, NEG)
    c[:, 512:640] = np.where((s <= t) & same, -1.0 / 16, 0.0)
    c[:, 640:768] = np.where((s > t) & same, -1.0 / 16, 0.0)
    c[:, 768:896] = np.where((s <= t) & same, 1.0, 0.0)
    c[:, 896:1024] = -1.0
    c2 = np.full((128, 1024), NEG, np.float32)
    sk = np.arange(128)[:, None]
    tq = np.arange(64)[None, :]
    for par in range(2):
        m = np.where(((sk // 64) == par) & ((sk % 64) < tq), 0.0, NEG).astype(np.float32)
        for h in range(8):
            c2[:, par * 512 + h * 64:par * 512 + (h + 1) * 64] = m
    return c, c2


def _r(w):
    return np.ascontiguousarray(w.reshape(KC, 128, w.shape[1]).transpose(1, 0, 2))


def _ffn_layout(w_in, w_out):
    wi = w_in.reshape(KC, 128, 2, NFC, 128)
    wi = np.ascontiguousarray(wi.transpose(3, 1, 0, 2, 4)).reshape(NFC, 128, KC * 256)
    wo = np.ascontiguousarray(w_out.reshape(NFC, 128, D).transpose(1, 0, 2)).reshape(128, NFC * D)
    return wi, wo


O_GQ, O_GK, O_GV, O_GR, O_GLR, O_SQ, O_SK, O_SV, O_GA, O_GB = 0, 512, 1024, 2048, 3072, 3088, 4112, 5136, 6160, 7184


def make_in_maps(inp):
    f1_win, f1_wout = _ffn_layout(inp["ffn1_w_in"][0], inp["ffn1_w_out"][0])
    f2_win, f2_wout = _ffn_layout(inp["ffn2_w_in"][0], inp["ffn2_w_out"][0])
    lnp = np.ascontiguousarray(np.stack([inp["ln1_g"][0], inp["ln1_b"][0], inp["ln2_g"][0], inp["ln2_b"][0],
                                         inp["ln3_g"][0], inp["ln3_b"][0]]).astype(np.float32))
    consts, consts2 = _consts()
    w = inp["w_in"][0]

    def c(off, n):
        return w[:, off:off + n]

    def gla_cols(h):
        return np.concatenate([c(O_GQ + h * 128, 128), c(O_GK + h * 128, 128), c(O_GV + h * 256, 256),
                               c(O_GR + h * 256, 256), c(O_GK + h * 128, 128)], axis=1)

    ws_gla = np.stack([_r(gla_cols(h)) for h in range(4)])
    ws_sb = _r(np.concatenate([c(O_SQ, 1024), c(O_SK, 1024), c(O_SK, 1024), c(O_SV, 1024)], axis=1))
    wglr = _r(c(O_GLR, 16))
    wupa = np.zeros((4, 128, 128), np.float32)
    for h in range(4):
        wupa[h, 0:16] = inp["w_gla_gate_up"][0][:, h * 128:(h + 1) * 128]
        wupa[h, 32] = inp["b_gla_gate"][0][h * 128:(h + 1) * 128]
    gn = inp["g_gla_norm"][0]
    wmix = np.stack([_r(np.concatenate([c(O_GA + fc * 128, 128), c(O_GB + fc * 128, 128),
                                        inp["w_gla_o"][0][:, fc * 128:(fc + 1) * 128],
                                        inp["w_sb_o"][0][:, fc * 128:(fc + 1) * 128]], axis=1)) for fc in range(8)])
    wo = _r(inp["w_out"][0])
    maps = []
    for core in range(NCORES):
        b, g = core // 4, core % 4
        xtm = np.concatenate([inp["x_prompt"][b, g * NP_TOK:(g + 1) * NP_TOK],
                              inp["x_sample"][4 * core:4 * core + 4].reshape(NS_TOK, D)], axis=0)
        xtm = np.ascontiguousarray(xtm)
        wp_sb = _r(np.concatenate([c(O_SQ + 2 * g * 128, 256), c(O_SK + 2 * g * 128, 256),
                                   c(O_SK + 2 * g * 128, 256), c(O_SV + 2 * g * 128, 256)], axis=1))
        sl = slice(4 * core, 4 * core + 4)
        kcT = np.ascontiguousarray(inp["cache_sb_k"][0, sl].transpose(0, 2, 3, 1))
        vc = np.ascontiguousarray(inp["cache_sb_v"][0, sl].reshape(4, 32, 128, D).transpose(0, 2, 1, 3))
        maps.append(dict(
            xT=np.ascontiguousarray(xtm.T), xtm=xtm, f1_win=f1_win, f1_wout=f1_wout, f2_win=f2_win, f2_wout=f2_wout,
            lnp=lnp, consts=consts, consts2=consts2, wp_gla=ws_gla[g], wp_sb=wp_sb, ws_gla=ws_gla, ws_sb=ws_sb,
            wglr=wglr, wup=np.ascontiguousarray(np.concatenate([wupa, wupa[g:g + 1]])),
            gnorm=np.ascontiguousarray(np.concatenate([gn, gn[g:g + 1]])),
            state_s=np.ascontiguousarray(inp["state_gla"][0, sl]), kcT=kcT, vc=vc, wmix=wmix, wo=wo))
    return maps


_NC_CACHE = {}
_STOP = None


def kernel(**inputs):
    inp = {k: np.asarray(v) for k, v in inputs.items()}
    if "nc" not in _NC_CACHE:
        _NC_CACHE["nc"] = build_nc(stop=_STOP)
    nc = _NC_CACHE["nc"]
    maps = make_in_maps(inp)
    res = run_bass_kernel_spmd(nc, maps, core_ids=list(range(NCORES)))
    R = res.results
    y_p = np.zeros((2, SEQ, D), np.float32)
    y_s = np.zeros((32, 64, D), np.float32)
    st_p = np.zeros((1, 2, 4, 128, 256), np.float32)
    k_p = np.zeros((1, 2, SEQ, 8, 128), np.float32)
    v_p = np.zeros((1, 2, SEQ, 8, 128), np.float32)
    st_s = np.zeros((1, 32, 4, 128, 256), np.float32)
    k_s = np.zeros((1, 32, 64, 8, 128), np.float32)
    v_s = np.zeros((1, 32, 64, 8, 128), np.float32)
    for core in range(NCORES):
        b, g = core // 4, core % 4
        r = R[core]
        y = np.asarray(r["y"])
        y_p[b, g * NP_TOK:(g + 1) * NP_TOK] = y[:NP_TOK]
        y_s[4 * core:4 * core + 4] = y[NP_TOK:].reshape(4, 64, D)
        st_p[0, b, g] = np.asarray(r["st_p"])
        k_p[0, b, :, 2 * g:2 * g + 2, :] = np.asarray(r["sbk_p"]).reshape(SEQ, 2, 128)
        v_p[0, b, :, 2 * g:2 * g + 2, :] = np.asarray(r["sbv_p"]).reshape(SEQ, 2, 128)
        st_s[0, 4 * core:4 * core + 4] = np.asarray(r["st_s"])
        k_s[0, 4 * core:4 * core + 4] = np.asarray(r["sbk_s"]).reshape(4, 64, 8, 128)
        v_s[0, 4 * core:4 * core + 4] = np.asarray(r["sbv_s"]).reshape(4, 64, 8, 128)
    return (y_p, y_s, st_p, k_p, v_p, st_s, k_s, v_s)
```

```python
import numpy as np
from contextlib import ExitStack
import concourse.bass as bass
import concourse.mybir as mybir
from concourse.bass_utils import run_bass_kernel_spmd

F32 = mybir.dt.float32
BF16 = mybir.dt.bfloat16
AF = mybir.ActivationFunctionType
ALU = mybir.AluOpType

NCORES = 8
D = 1024
KC = 8
SEQ = 8192
NP_TOK = 2048
NS_TOK = 256
NTOK = NP_TOK + NS_TOK
PAST = 4096
DFF = 2816
NFC = 22
ALPHA = 2.0 ** 0.25
EPS = 1e-5
NEG = -30000.0
G = 768
NG = NTOK // G


class Buf:
    __slots__ = ("w", "r", "excl")

    def __init__(self, excl=False):
        self.w = None
        self.r = []
        self.excl = excl


def bufs(n):
    return [Buf() for _ in range(n)]


class DSem:
    def __init__(self, sem, inc=16):
        self.sem = sem
        self.inc = inc
        self.val = 0
        self.last = None


class Op:
    __slots__ = ("eng", "fn", "deps", "dsem", "dval", "needed", "count")


class Sched:
    ENG = ("pe", "act", "dve", "pool", "sp")

    def __init__(self, nc, esems):
        self.nc = nc
        self.esem = esems
        self.ops = []
        self.flushed = 0
        self.cnt = {e: 0 for e in self.ENG}
        self.last = {e: None for e in self.ENG}
        self.waited = {e: {} for e in self.ENG}
        self.dsems = []
        self.dq = {}
        self.dqi = {}

    def dsem(self, sem, inc=16):
        d = DSem(sem, inc)
        self.dsems.append(d)
        return d

    def add(self, eng, fn, R=(), W=(), dsem=None, extra=()):
        op = Op()
        deps = set(extra)
        if dsem is not None:
            key = "cc" if dsem == "cc" else eng
            i = self.dqi.get(key, 0)
            self.dqi[key] = i + 1
            dsem = self.dq[key][i % len(self.dq[key])]
            if dsem.last is not None:
                deps.add(dsem.last)
        op.eng, op.fn, op.dsem, op.needed, op.count, op.dval = eng, fn, dsem, False, 0, 0
        W = list(W) + [b for b in R if b.excl]
        R = [b for b in R if not b.excl]
        for b in R:
            if b.w is not None:
                deps.add(b.w)
        for b in W:
            if b.w is not None:
                deps.add(b.w)
            for r in b.r:
                deps.add(r)
        deps.discard(op)
        op.deps = deps
        for b in R:
            b.r.append(op)
        for b in W:
            b.w = op
            b.r = []
        if dsem is not None:
            dsem.val += dsem.inc
            op.dval = dsem.val
            dsem.last = op
        self.ops.append(op)
        self.last[eng] = op
        return op

    def barrier(self):
        deps = [o for o in self.last.values() if o is not None]
        deps += [d.last for d in self.dsems if d.last is not None]
        for e in self.ENG:
            self.add(e, None, extra=deps)

    def flush(self):
        self.barrier()
        ops = self.ops[self.flushed:]
        self.flushed = len(self.ops)
        for op in ops:
            for d in op.deps:
                if d.dsem is None and not (d.eng == "pe" and op.eng == "pe"):
                    d.needed = True
        for op in ops:
            if op.dsem is None and op.needed:
                self.cnt[op.eng] += 1
                op.count = self.cnt[op.eng]
        per = {e: [o for o in ops if o.eng == e] for e in self.ENG}

        def emit(e, name):
            waited = self.waited[name]
            for op in per[name]:
                w = {}
                for d in op.deps:
                    if d.dsem is not None:
                        key, so, val = ("d", id(d.dsem)), d.dsem.sem, d.dval
                    else:
                        if d.eng == "pe" and name == "pe":
                            continue
                        key, so, val = d.eng, self.esem[d.eng], d.count
                    if val > w.get(key, (None, 0))[1]:
                        w[key] = (so, val)
                for key, (so, val) in w.items():
                    if waited.get(key, 0) >= val:
                        continue
                    e.wait_ge(so, val)
                    waited[key] = val
                if op.fn is None:
                    continue
                ins = op.fn(e)
                if op.dsem is not None:
                    ins.then_inc(op.dsem.sem, op.dsem.inc)
                elif op.needed:
                    ins.then_inc(self.esem[name], 1)

        with self.nc.Block() as block:
            @block.tensor
            def _(e):
                emit(e, "pe")

            @block.scalar
            def _(e):
                emit(e, "act")

            @block.vector
            def _(e):
                emit(e, "dve")

            @block.gpsimd
            def _(e):
                emit(e, "pool")

            @block.sync
            def _(e):
                emit(e, "sp")


class Ring:
    def __init__(self, tiles):
        self.t = tiles
        self.b = bufs(len(tiles))
        self.i = 0

    def next(self):
        k = self.i % len(self.t)
        self.i += 1
        return self.t[k], self.b[k]


def build_nc(dbg=False, stop=None, test2b=False):
    nc = bass.Bass("TRN2", target_bir_lowering=False)

    T2B = ("consts", "consts2", "ws_gla", "ws_sb", "wglr", "wup", "gnorm", "state_s", "kcT", "vc")

    def din(name, shape, dt=F32):
        if test2b and name not in T2B:
            return None
        return nc.dram_tensor(name, list(shape), dt, kind="ExternalInput").ap()

    def dout(name, shape, dt=F32):
        return nc.dram_tensor(name, list(shape), dt, kind="ExternalOutput").ap()

    def dint(name, shape, dt):
        return nc.dram_tensor(name, list(shape), dt, kind="Internal").ap()

    xT_d = din("xT", [D, NTOK])
    xtm_d = din("xtm", [NTOK, D])
    f1_win_d = din("f1_win", [NFC, 128, KC * 256])
    f1_wout_d = din("f1_wout", [128, NFC * D])
    f2_win_d = din("f2_win", [NFC, 128, KC * 256])
    f2_wout_d = din("f2_wout", [128, NFC * D])
    lnp_d = din("lnp", [6, D])
    consts_d = din("consts", [128, 8 * 128])
    consts2_d = din("consts2", [128, 1024])
    wp_gla_d = din("wp_gla", [128, KC, 896])
    wp_sb_d = din("wp_sb", [128, KC, 1024])
    ws_gla_d = din("ws_gla", [4, 128, KC, 896])
    ws_sb_d = din("ws_sb", [128, KC, 4096])
    wglr_d = din("wglr", [128, KC, 16])
    wup_d = din("wup", [5, 128, 128])
    gnorm_d = din("gnorm", [5, 256])
    state_s_d = din("state_s", [4, 4, 128, 256])
    kcT_d = din("kcT", [4, 8, 128, PAST])
    vc_d = din("vc", [4, 128, 32, D])
    wmix_d = din("wmix", [8, 128, KC, 512])
    wo_d = din("wo", [128, KC, D])
    y_d = dout("y", [NTOK, D])
    st_p_d = dout("st_p", [128, 256])
    sbk_p_d = dout("sbk_p", [SEQ, 256])
    sbv_p_d = dout("sbv_p", [SEQ, 256])
    st_s_d = dout("st_s", [4, 4, 128, 256])
    sbk_s_d = dout("sbk_s", [NS_TOK, D])
    sbv_s_d = dout("sbv_s", [NS_TOK, D])
    h_tm_d = dint("h_tm", [NTOK, D], F32)
    hT_p_d = dint("hT_p", [4 * D, 512], BF16)
    ag1_d = dint("ag1", [4 * 4 * D, 512], BF16)
    ag2s_d = dint("ag2s", [8 * 512, 1024], BF16)
    ag2_d = dint("ag2", [8 * 2048, 1024], BF16)
    own_d = dint("ag2own", [4 * 512, NP_TOK], BF16)
    h2_tm_d = dint("h2_tm", [NTOK, D], F32)
    h2T_d = dint("h2T", [D, NTOK], BF16)
    GROUPS = [[0, 1, 2, 3], [4, 5, 6, 7]]

    es = ExitStack()
    with es:
        def sem(name):
            return es.enter_context(nc.semaphore(name))

        esems = {e: sem("s_" + e) for e in Sched.ENG}
        S = Sched(nc, esems)
        for q_, n_, inc_ in (("sp", 12, 16), ("pool", 12, 16), ("cc", 4, 1)):
            S.dq[q_] = [S.dsem(sem("dq_%s%d" % (q_, i)), inc_) for i in range(n_)]

        def sb(name, shape, dt, stack=es):
            return stack.enter_context(nc.sbuf_tensor(name, list(shape), dt))

        PS = [es.enter_context(nc.psum_tensor("ps%d" % i, [128, 1024], F32)) for i in range(4)]
        PSQ = [[Buf(excl=True)] * 4 for _ in range(8)]

        def bank(i):
            return PS[i // 2][:, (i % 2) * 512:(i % 2) * 512 + 512], PSQ[i]

        def mm(out, lhsT, rhs, start, stop, R, W, sg=False):
            if sg:
                return S.add("pe", lambda e: e.matmul(out, lhsT, rhs, start=start, stop=stop, skip_group_check=True),
                             R=R, W=W)
            return S.add("pe", lambda e: e.matmul(out, lhsT, rhs, start=start, stop=stop), R=R, W=W)

        def act(out, in_, func, R, W, bias=0.0, scale=1.0, accum_out=None):
            if accum_out is None:
                return S.add("act", lambda e: e.activation(out, in_, func, bias=bias, scale=scale), R=R, W=W)
            return S.add("act", lambda e: e.activation(out, in_, func, bias=bias, scale=scale,
                                                       accum_out=accum_out), R=R, W=W)

        def tt(eng, out, in0, in1, op, R, W):
            return S.add(eng, lambda e: e.tensor_tensor(out, in0, in1, op), R=R, W=W)

        def ts(eng, out, in0, s1, s2, op0, op1, R, W):
            if op1 is None:
                return S.add(eng, lambda e: e.tensor_scalar(out, in0, s1, None, op0), R=R, W=W)
            return S.add(eng, lambda e: e.tensor_scalar(out, in0, s1, s2, op0, op1), R=R, W=W)

        def stt(eng, out, in0, scalar, in1, op0, op1, R, W):
            return S.add(eng, lambda e: e.scalar_tensor_tensor(out, in0, scalar, in1, op0, op1), R=R, W=W)

        def recip(out, in_, R, W):
            return S.add("dve", lambda e: e.reciprocal(out, in_), R=R, W=W)

        def cp(eng, out, in_, R, W, scale=None):
            if eng == "act":
                if scale is None:
                    return S.add("act", lambda e: e.activation(out, in_, AF.Copy), R=R, W=W)
                return S.add("act", lambda e: e.activation(out, in_, AF.Copy, scale=scale), R=R, W=W)
            return S.add(eng, lambda e: e.tensor_copy(out, in_), R=R, W=W)

        def memset(out, val, W):
            return S.add("dve", lambda e: e.memset(out, val), R=[], W=W)

        def dma(q, out, in_, ds, R, W):
            return S.add(q, lambda e: e.dma_start(out=out, in_=in_), R=R, W=W, dsem=ds)

        def dsem(name, inc=16):
            return "cc" if inc == 1 else "auto"

        cst = sb("cst", [128, 8 * 128], BF16)
        cstf = sb("cstf", [128, 128], F32)
        B_cst = Buf()
        ds_c = dsem("d_c")
        dma("pool", cst[:], consts_d, ds_c, [], [B_cst])
        dma("sp", cstf[:], consts_d[:, 6 * 128:7 * 128], ds_c, [], [B_cst])
        ident = cst[:, 0:128]
        NEGTRI = cst[:, 128:256]
        NEGTRIC = cst[:, 256:384]
        MASKB = cst[:, 384:512]
        TRI_INCL = cst[:, 512:640]
        TRI_REV = cst[:, 640:768]
        NEGONES = cst[:, 896:1024]
        CAUSAL2 = cstf[:, 0:128]

        hT_s = sb("hT_s", [128, KC, NS_TOK], BF16)
        hT_sB = Buf()
        oaT_s = sb("oaT_s", [128, KC, NS_TOK], BF16)
        oaT_sB = bufs(8)
        obT_s = sb("obT_s", [128, KC, NS_TOK], BF16)
        obT_sB = bufs(4)
        lnw = dict(
            st=Ring([sb("ln_st%d" % i, [128, 12], F32) for i in range(2)]),
            mv=Ring([sb("ln_mv%d" % i, [128, 4], F32) for i in range(2)]),
        )

        def load_lnp(stack, li):
            t = sb("lnp_sb%d" % li, [128, 2 * D], F32, stack)
            b = Buf()
            for i in range(2):
                dma("sp", t[:, i * D:(i + 1) * D], lnp_d[2 * li + i:2 * li + i + 1, :].partition_broadcast(128),
                    ds_c, [], [b])
            return t, b

        def layer_norm(r_ap, rB, lnp, lnpB):
            st, stB = lnw["st"].next()
            mv, mvB = lnw["mv"].next()
            S.add("dve", lambda e: e.bn_stats(st[:, 0:6], r_ap[:, 0:512]), R=[rB], W=[stB])
            S.add("dve", lambda e: e.bn_stats(st[:, 6:12], r_ap[:, 512:1024]), R=[rB, stB], W=[stB])
            S.add("dve", lambda e: e.bn_aggr(mv[:, 0:2], st[:, 0:12]), R=[stB], W=[mvB])
            act(mv[:, 3:4], mv[:, 1:2], AF.Sqrt, [mvB], [mvB], bias=EPS)
            recip(mv[:, 2:3], mv[:, 3:4], [mvB], [mvB])
            ts("dve", r_ap, r_ap, mv[:, 0:1], mv[:, 2:3], ALU.subtract, ALU.mult, [rB, mvB], [rB])
            tt("dve", r_ap, r_ap, lnp[:, 0:D], ALU.mult, [rB, lnpB], [rB])
            tt("dve", r_ap, r_ap, lnp[:, D:2 * D], ALU.add, [rB, lnpB], [rB])

        def transpose_tile(hb, hbB, dst_fn, t):
            for half in range(2):
                pt, ptB = bank(6 + half)
                for j in range(4):
                    kc = half * 4 + j
                    mm(pt[:, j * 128:(j + 1) * 128], hb[:, kc * 128:(kc + 1) * 128], ident, True, True,
                       [hbB, B_cst], ptB[j:j + 1])
                dst, dB = dst_fn(half)
                cp("act" if half else "dve", dst, pt.rearrange("p (j n) -> p j n", j=4), ptB, [dB])

        def ffn_phase(tag, srcT_d, res_d, win_d, wout_d, li, out_tm_d, hT_out):
            with ExitStack() as p1:
                xT = sb(tag + "xT", [128, KC, NTOK], BF16, p1)
                xTB = bufs(NG)
                ds_x = dsem(tag + "d_x")
                xTv = srcT_d.rearrange("(kc p) t -> p kc t", p=128)
                for gi in range(NG):
                    for kc in range(KC):
                        dma("pool", xT[:, kc, gi * G:(gi + 1) * G], xTv[:, kc, gi * G:(gi + 1) * G], ds_x, [],
                            [xTB[gi]])
                wout = sb(tag + "wout", [128, NFC * D], BF16, p1)
                woutB = Buf()
                ds_wo = dsem(tag + "d_wo")
                for i in range(2):
                    dma("pool", wout[:, i * 11 * D:(i + 1) * 11 * D], wout_d[:, i * 11 * D:(i + 1) * 11 * D], ds_wo,
                        [], [woutB])
                lnp, lnpB = load_lnp(p1, li)
                wring = Ring([sb(tag + "w%d" % i, [128, KC * 256], BF16, p1) for i in range(3)])
                wds = [dsem(tag + "d_w%d" % i) for i in range(3)]
                sgr = Ring([sb(tag + "sg%d" % i, [128, 512], F32, p1) for i in range(2)])
                actT = sb(tag + "actT", [128, NFC, G], BF16, p1)
                actB = [bufs(2) for _ in range(NFC)]
                rr = Ring([sb(tag + "r%d" % i, [128, D], F32, p1) for i in range(3)])
                ds_ho = [dsem(tag + "d_ho%d" % i) for i in range(3)]
                xres = Ring([sb(tag + "x%d" % i, [128, D], F32, p1) for i in range(2)])
                xres_ds = [dsem(tag + "d_xr%d" % i) for i in range(2)]
                if hT_out is not None:
                    hbf = Ring([sb(tag + "hbf%d" % i, [128, D], BF16, p1) for i in range(2)])
                    hTg = sb(tag + "hTg", [128, KC, G], BF16, p1)
                    hTgB = [bufs(2) for _ in range(G // 128)]
                    hTgAll = [b for bb in hTgB for b in bb]
                nblk = [(o, min(512, G - o)) for o in range(0, G, 512)]
                gu = 0
                for gi in range(NG):
                    t0 = gi * G
                    for c in range(NFC):
                        w, wB = wring.next()
                        dma("pool", w[:], win_d[c], wds[(wring.i - 1) % 3], [], [wB])
                        for bi, (o, n) in enumerate(nblk):
                            pg, pgB = bank(2 * (gu % 2))
                            pu, puB = bank(2 * (gu % 2) + 1)
                            gu += 1
                            for kc in range(KC):
                                mm(pg[:, :n], w[:, kc * 256:kc * 256 + 128], xT[:, kc, t0 + o:t0 + o + n], kc == 0,
                                   kc == KC - 1, [wB, xTB[gi]], pgB)
                            for kc in range(KC):
                                mm(pu[:, :n], w[:, kc * 256 + 128:kc * 256 + 256], xT[:, kc, t0 + o:t0 + o + n],
                                   kc == 0, kc == KC - 1, [wB, xTB[gi]], puB)
                            sg, sgB = sgr.next()
                            act(sg[:, :n], pg[:, :n], AF.Silu, pgB, [sgB])
                            tt("dve", actT[:, c, o:o + n], sg[:, :n], pu[:, :n], ALU.mult, [sgB] + puB, [actB[c][bi]])
                    for t in range(G // 128):
                        po = PS[2]
                        for c in range(NFC):
                            for hf in range(2):
                                mm(po[:, hf * 512:(hf + 1) * 512], actT[:, c, t * 128:(t + 1) * 128],
                                   wout[:, c * D + hf * 512:c * D + hf * 512 + 512], c == 0, c == NFC - 1,
                                   [actB[c][(t * 128) // 512], woutB], PSQ[4 + hf])
                        x, xB = xres.next()
                        dma("sp", x[:], res_d[t0 + t * 128:t0 + (t + 1) * 128, :], xres_ds[(xres.i - 1) % 2], [], [xB])
                        r, rB = rr.next()
                        k = (rr.i - 1) % 3
                        act(r[:], x[:], AF.Copy, [xB], [rB], scale=ALPHA)
                        stt("dve", r[:], po[:], 0.5, r[:], ALU.mult, ALU.add, PSQ[4] + PSQ[5] + [rB], [rB])
                        layer_norm(r[:], rB, lnp, lnpB)
                        dma("sp", out_tm_d[t0 + t * 128:t0 + (t + 1) * 128, :], r[:], ds_ho[k], [rB], [])
                        if hT_out is not None:
                            hb, hbB = hbf.next()
                            cp("act", hb[:], r[:], [rB], [hbB])
                            transpose_tile(hb, hbB, lambda half, t=t: (
                                hTg[:, half * 4:half * 4 + 4, t * 128:(t + 1) * 128], hTgB[t][half]), t)
                    if hT_out is not None:
                        hT_out(gi, hTg, hTgAll)
                S.flush()

        ds_hT = dsem("d_hT")
        hTv = hT_p_d.rearrange("(j kc p) t -> j p kc t", j=4, p=128)
        B_hTp = bufs(4)

        def ship_h(gi, hTg, hB):
            t0 = gi * G
            npr = max(0, min(NP_TOK - t0, G))
            for j in range(4):
                a, b = max(t0, 512 * j), min(t0 + npr, 512 * (j + 1))
                if a < b:
                    dma("sp", hTv[j, :, :, a - 512 * j:b - 512 * j], hTg[:, :, a - t0:b - t0], ds_hT, hB, [B_hTp[j]])
            if npr < G:
                cp("dve", hT_s[:, :, :], hTg[:, :, npr:G], hB, [hT_sB])

        if test2b:
            hT_in_d = nc.dram_tensor("hT_s_in", [D, NS_TOK], F32, kind="ExternalInput").ap()
            dma("pool", hT_s[:], hT_in_d.rearrange("(kc p) t -> p kc t", p=128), ds_c, [], [hT_sB])
        else:
            ffn_phase("f1", xT_d, xtm_d, f1_win_d, f1_wout_d, 0, h_tm_d, ship_h)
        if stop == 1:
            return nc

        cc1 = dsem("cc1", 1)
        B_ag1 = bufs(4)

        def ag_fn(src, dst):
            return lambda e: e.collective_compute("AllGather", ALU.bypass, replica_groups=GROUPS, ins=[src], outs=[dst])
        for j in range(4):
            if test2b:
                break
            S.add("pool", ag_fn(hT_p_d[j * D:(j + 1) * D, :], ag1_d[j * 4 * D:(j + 1) * 4 * D, :]),
                  R=[B_hTp[j]], W=[B_ag1[j]], dsem=cc1)

        if stop == 1.5:
            S.flush()
            return nc
        LN_QS = float(np.log(128.0 ** -0.5))
        with ExitStack() as p2:
            cst2 = sb("cst2", [128, 1024], BF16, p2)
            dma("pool", cst2[:], consts2_d, ds_c, [], [B_cst])
            wglr = sb("wglr_sb", [128, KC, 128], BF16, p2)
            B_wglr = Buf()
            memset(wglr[:], 0.0, [B_wglr])
            dma("pool", wglr[:, :, 0:16], wglr_d, ds_c, [], [B_wglr])
            wgla = Ring([sb("wgla%d" % i, [128, KC, 896], BF16, p2) for i in range(2)])
            wgla_ds = [dsem("d_wgla%d" % i) for i in range(2)]
            wupr = Ring([sb("wup%d" % i, [128, 128], BF16, p2) for i in range(2)])
            gnbr = Ring([sb("gnb%d" % i, [128, 256], F32, p2) for i in range(2)])

            def load_gla_w(src_ap, hidx):
                w, wB = wgla.next()
                k = (wgla.i - 1) % 2
                dma("pool", w[:], src_ap, wgla_ds[k], [], [wB])
                wu, wuB = wupr.next()
                dma("pool", wu[:], wup_d[hidx], wgla_ds[k], [], [wuB])
                gn, gnB = gnbr.next()
                dma("sp", gn[:], gnorm_d[hidx:hidx + 1, :].partition_broadcast(128), wgla_ds[k], [], [gnB])
                W = dict(q=w[:, :, 0:128], k=w[:, :, 128:256], vr=w[:, :, 256:768], ktm=w[:, :, 768:896], B=wB)
                return W, (wu[:], wuB), (gn[:], gnB)

            def ring(name, shape, dt, n=2, zero=False, ones_row=False):
                tiles = [sb("%s%d" % (name, i), shape, dt, p2) for i in range(n)]
                r = Ring(tiles)
                if zero:
                    for t, b in zip(tiles, r.b):
                        memset(t[:], 0.0, [b])
                        if ones_row:
                            memset(t[32:33, :], 1.0, [b])
                return r

            R = dict(
                glrT=ring("g_glrT", [128, 128], BF16, zero=True, ones_row=True),
                e1=ring("g_e1", [128, 128], F32), la=ring("g_la", [128, 128], BF16),
                eb=ring("g_eb", [128, 128], F32), enb=ring("g_enb", [128, 128], F32),
                ek=ring("g_ek", [128, 128], F32), dec=ring("g_dec", [128, 2], F32),
                qA=ring("g_qA", [128, 128], BF16, zero=True), qB=ring("g_qB", [128, 128], BF16, zero=True),
                kg=ring("g_kg", [128, 128], BF16),
                kd0=ring("g_kd0", [128, 128], BF16, zero=True), kd1=ring("g_kd1", [128, 128], BF16, zero=True),
                vb=ring("g_vb", [128, 256], BF16), eg=ring("g_eg", [128, 256], F32),
                gg=ring("g_gg", [128, 256], F32), at=ring("g_at", [128, 128], BF16),
                ss=ring("g_ss", [128, 4], F32), oa=ring("g_oa", [128, 256], BF16),
                sbf=ring("g_sbf", [128, 256], BF16, n=6),
            )
            junk = sb("g_junk", [128, 256], F32, p2)
            junkB = Buf()

            class State:
                pass

            def snapshot(st):
                nb, nbB = R["sbf"].next()
                cp("act", nb[:], st.f32, [st.fB], [nbB])
                st.bf, st.bfB = nb[:], nbB

            def update(st, dec_ap, decB, U_ap, UB):
                stt("dve", st.f32, st.f32, dec_ap, U_ap, ALU.mult, ALU.add, [st.fB, decB] + UB, [st.fB])
                snapshot(st)

            def gla_tile(W, hT_t, hTB, wup, gnb, states, dst_fn):
                pA, Aq = bank(0)
                pB_, Bq = bank(1)
                pC, Cq = bank(2)
                pD, Dq = bank(3)
                wu, wuB = wup
                gn, gnB = gnb
                for j, key in enumerate(("q", "k", "glr")):
                    wk = wglr if key == "glr" else W[key]
                    wkB = B_wglr if key == "glr" else W["B"]
                    for kc in range(KC):
                        mm(pA[:, j * 128:(j + 1) * 128], wk[:, kc, :], hT_t[:, kc, :], kc == 0, kc == KC - 1,
                           [wkB] + hTB, Aq[j:j + 1])
                for kc in range(KC):
                    mm(pB_[:, 0:512], hT_t[:, kc, :], W["vr"][:, kc, :], kc == 0, kc == KC - 1, [W["B"]] + hTB, Bq)
                for kc in range(KC):
                    mm(pC[:, 0:128], hT_t[:, kc, :], W["ktm"][:, kc, :], kc == 0, kc == KC - 1, [W["B"]] + hTB,
                       Cq[0:1])
                yield
                gp, gpB = R["glrT"].next()
                cp("dve", gp[0:32, :], pA[0:32, 256:384], Aq[2:3], [gpB])
                mm(pC[:, 128:256], gp[:], wu, True, True, [gpB, wuB], Cq[1:2])
                yield
                e1, e1B = R["e1"].next()
                la, laB = R["la"].next()
                act(e1[:], pC[:, 128:256], AF.Exp, Cq[1:2], [e1B], scale=-1.0)
                act(la[:], e1[:], AF.Ln, [e1B], [laB], bias=1.0)
                mm(pC[:, 256:384], la[:], TRI_INCL, True, True, [laB, B_cst], Cq[2:3])
                mm(pC[:, 384:512], TRI_REV, la[:], True, True, [laB, B_cst], Cq[3:4])
                yield
                eb, ebB = R["eb"].next()
                enb, enbB = R["enb"].next()
                ek, ekB = R["ek"].next()
                dec, decB = R["dec"].next()
                act(eb[:], pC[:, 256:384], AF.Exp, Cq[2:3], [ebB], bias=LN_QS)
                act(enb[:], pC[:, 256:384], AF.Exp, Cq[2:3], [enbB], scale=-1.0)
                act(ek[:], pC[:, 384:512], AF.Exp, Cq[3:4], [ekB])
                act(dec[:, 0:1], pC[:, 256 + 63:256 + 64], AF.Exp, Cq[2:3], [decB])
                act(dec[:, 1:2], pC[:, 256 + 127:256 + 128], AF.Exp, Cq[2:3], [decB])
                yield
                qA, qAB = R["qA"].next()
                qB, qBB = R["qB"].next()
                tt("dve", qA[:, 0:64], pA[:, 0:64], eb[:, 0:64], ALU.mult, Aq[0:1] + [ebB], [qAB])
                tt("dve", qB[:, 64:128], pA[:, 64:128], eb[:, 64:128], ALU.mult, Aq[0:1] + [ebB], [qBB])
                kg, kgB = R["kg"].next()
                tt("dve", kg[:], pA[:, 128:256], enb[:], ALU.mult, Aq[1:2] + [enbB], [kgB])
                kd0, kd0B = R["kd0"].next()
                kd1, kd1B = R["kd1"].next()
                tt("dve", kd0[0:64, :], pC[0:64, 0:128], ek[0:64, :], ALU.mult, Cq[0:1] + [ekB], [kd0B])
                tt("dve", kd1[64:128, :], pC[64:128, 0:128], ek[64:128, :], ALU.mult, Cq[0:1] + [ekB], [kd1B])
                yield
                vb, vbB = R["vb"].next()
                cp("act", vb[:], pB_[:, 0:256], Bq[0:2], [vbB])
                eg, egB = R["eg"].next()
                gg, ggB = R["gg"].next()
                act(eg[:], pB_[:, 256:512], AF.Exp, Bq[2:4], [egB], scale=-1.0)
                act(eg[:], eg[:], AF.Ln, [egB], [egB], bias=1.0)
                act(eg[:], eg[:], AF.Exp, [egB], [egB], scale=-1.0)
                tt("dve", gg[:], pB_[:, 256:512], eg[:], ALU.mult, Bq[2:4] + [egB], [ggB])
                tt("dve", gg[:], gg[:], gn, ALU.mult, [ggB, gnB], [ggB])
                yield
                mm(pA[:, 384:512], kg[:], qA[:], True, False, [kgB, qAB], Aq[3:4])
                mm(pA[:, 384:512], kg[:], qB[:], False, True, [kgB, qBB], Aq[3:4])
                at, atB = R["at"].next()
                tt("dve", at[:], pA[:, 384:512], CAUSAL2, ALU.mult, Aq[3:4] + [B_cst], [atB])
                yield
                st0, st1 = states
                mm(pD[:, 0:256], at[:], vb[:], True, False, [atB, vbB], Dq[0:2])
                mm(pD[:, 0:256], qA[:], st0.bf, False, False, [qAB, st0.bfB], Dq[0:2])
                mm(pB_[:, 0:256], kd0[:], vb[:], True, True, [kd0B, vbB], Bq[0:2])
                mm(pB_[:, 256:512], kd1[:], vb[:], True, True, [kd1B, vbB], Bq[2:4])
                update(st0, dec[:, 0:1], decB, pB_[:, 0:256], Bq[0:2])
                yield
                mm(pD[:, 0:256], qB[:], st1.bf, False, True, [qBB, st1.bfB], Dq[0:2])
                update(st1, dec[:, 1:2], decB, pB_[:, 256:512], Bq[2:4])
                yield
                ss, ssB = R["ss"].next()
                act(junk[:], pD[:, 0:256], AF.Square, Dq[0:2], [junkB, ssB], accum_out=ss[:, 0:1])
                act(ss[:, 1:2], ss[:, 0:1], AF.Ln, [ssB], [ssB], scale=1.0 / 256, bias=EPS)
                act(ss[:, 2:3], ss[:, 1:2], AF.Exp, [ssB], [ssB], scale=-0.5)
                oa, oaB = R["oa"].next()
                stt("dve", oa[:], pD[:, 0:256], ss[:, 2:3], gg[:], ALU.mult, ALU.mult, Dq[0:2] + [ssB, ggB], [oaB])
                yield
                for c in range(2):
                    mm(pD[:, 256 + c * 128:256 + (c + 1) * 128], oa[:, c * 128:(c + 1) * 128], ident, True, True,
                       [oaB, B_cst], Dq[2 + c:3 + c])
                dst, dstB = dst_fn()
                cp("act", dst, pD[:, 256:512].rearrange("p (c n) -> p c n", c=2), Dq[2:4], dstB)

            Er = ring("s_E", [128, 512], BF16, 4)
            Lr = ring("s_L", [128, 512], BF16, 4)
            LSr = ring("s_LS", [128, 512], BF16, 3)
            Xr = ring("s_X", [128, 512], BF16, 3)
            Wr = ring("s_W", [128, 512], BF16, 4)
            zcnt = [0]

            def record(gen):
                rec = []
                S.add = lambda eng, fn, R=(), W=(), dsem=None, extra=(): rec.append((eng, fn, R, W, dsem, extra))
                try:
                    for _ in gen:
                        pass
                finally:
                    del S.add
                return rec

            class Replay:
                def __init__(self, rec, iters):
                    self.rec = rec
                    self.i = 0
                    self.k = max(1, -(-len(rec) // max(1, iters)))

                def step(self):
                    for _ in range(self.k):
                        if self.i < len(self.rec):
                            S.add(*self.rec[self.i])
                            self.i += 1

                def drain(self):
                    while self.i < len(self.rec):
                        S.add(*self.rec[self.i])
                        self.i += 1

            def sb_stream(blocks, evict, bg=None):
                ob, obB = bank(4)
                n = len(blocks)
                st = [None] * n

                def stage1(i):
                    if "lazy" in blocks[i]:
                        post = blocks[i].get("post")
                        blocks[i] = blocks[i]["lazy"]()
                        if post is not None:
                            blocks[i]["post"] = post
                    blk = blocks[i]
                    c0 = blk["c0"]
                    zb, zbB = bank(5 + zcnt[0] % 2)
                    zcnt[0] += 1
                    mk = blk.get("mask")
                    mfirst = blk.get("mask_first", False)
                    if mk is not None and mfirst:
                        mm(zb[:, mk[2]:mk[3]], mk[0], mk[1], True, False, [B_cst], zbB, sg=True)
                    for (kT, q, a, b, rb) in blk["z"]:
                        mm(zb[:, a:b], kT, q, not (mk is not None and mfirst), True, rb, zbB, sg=True)
                    if mk is not None and not mfirst:
                        mm(zb[:, mk[2]:mk[3]], mk[0], mk[1], False, True, [B_cst], zbB, sg=True)
                    E, EB = Er.next()
                    act(E[:, c0:512], zb[:, c0:512], AF.Exp, zbB, [EB])
                    st[i] = dict(E=E, EB=EB)

                def stage1b(i):
                    c0 = blocks[i]["c0"]
                    E, EB = st[i]["E"], st[i]["EB"]
                    L, LB = Lr.next()
                    act(L[:, c0:512], E[:, c0:512], AF.Ln, [EB], [LB], bias=1.0)
                    if i == 0:
                        ls = (L, LB, c0)
                    elif i < n - 1:
                        pLs, pLsB, pc0 = st[i - 1]["ls"]
                        Ls, LsB = LSr.next()
                        tt("dve", Ls[:, pc0:512], pLs[:, pc0:512], L[:, pc0:512], ALU.add, [pLsB, LB], [LsB])
                        if c0 < pc0:
                            cp("dve", Ls[:, c0:pc0], L[:, c0:pc0], [LB], [LsB])
                        ls = (Ls, LsB, c0)
                    else:
                        ls = None
                    st[i].update(L=L, LB=LB, ls=ls)

                def stage2(i):
                    c0 = blocks[i]["c0"]
                    d = st[i]
                    tb, tbB = bank(7)
                    mm(tb[:, c0:512], NEGTRI, d["L"][:, c0:512], True, i == 0, [d["LB"], B_cst], tbB, sg=True)
                    if i >= 1:
                        pLs, pLsB, pc0 = st[i - 1]["ls"]
                        mm(tb[:, pc0:512], NEGONES, pLs[:, pc0:512], False, True, [pLsB, B_cst], tbB, sg=True)
                    X, XB = Xr.next()
                    act(X[:, c0:512], tb[:, c0:512], AF.Exp, tbB, [XB])
                    w, wB = Wr.next()
                    tt("dve", w[:, c0:512], d["E"][:, c0:512], X[:, c0:512], ALU.mult, [d["EB"], XB], [wB])
                    d["w"] = (w, wB)

                def stage3(i):
                    w, wB = st[i]["w"]
                    for j, (v, a, b, rb) in enumerate(blocks[i]["pv"]):
                        mm(ob[:, a:b], v, w[:, a:b], i == 0 and j == 0, True, [wB] + rb, obB, sg=True)

                for i in range(n + 2):
                    if i < n:
                        stage1(i)
                        stage1b(i)
                    if 1 <= i <= n:
                        stage2(i - 1)
                    if i >= 2:
                        stage3(i - 2)
                    if i < n and blocks[i].get("post") is not None:
                        blocks[i]["post"]()
                    if bg is not None:
                        bg.step()
                evict(ob, obB)

            with ExitStack() as p2b:
                wpc = Ring([sb("wpc%d" % i, [128, KC, 512], BF16, p2b) for i in range(2)])
                wpc_ds = [dsem("d_wpc%d" % i) for i in range(2)]
                qT_s = sb("qT_s", [128, 8, NS_TOK], BF16, p2b)
                kT_s = sb("kT_s", [128, 8, NS_TOK], BF16, p2b)
                v_s = sb("v_s", [128, 2, D], BF16, p2b)
                qkB = bufs(16)
                v_sB = [bufs(2) for _ in range(2)]
                kvst = Ring([sb("kvst%d" % i, [128, 512], F32, p2b) for i in range(2)])
                kvst_ds = [dsem("d_kvst%d" % i) for i in range(2)]
                pcnt = 0
                import os
                KV = os.environ.get("KVAR", "")
                for piece in range(4):
                    if "nofm" in KV:
                        break
                    w, wB = wpc.next()
                    dma("pool", w[:], ws_sb_d[:, :, piece * 512:(piece + 1) * 512], wpc_ds[(wpc.i - 1) % 2], [], [wB])
                    for j in range(4):
                        ch = piece * 4 + j
                        pz, pzB = bank(5 + pcnt % 2)
                        pcnt += 1
                        for kc in range(KC):
                            mm(pz[:, 0:NS_TOK], w[:, kc, j * 128:(j + 1) * 128], hT_s[:, kc, :], kc == 0, kc == KC - 1,
                               [wB, hT_sB], pzB)
                        if ch < 8:
                            cp("act", qT_s[:, ch, :], pz[:, 0:NS_TOK], pzB, [qkB[ch]], scale=float(128.0 ** -0.5))
                        else:
                            cp("dve", kT_s[:, ch - 8, :], pz[:, 0:NS_TOK], pzB, [qkB[ch]])
                for piece in range(4):
                    if "notm" in KV:
                        break
                    w, wB = wpc.next()
                    dma("pool", w[:], ws_sb_d[:, :, 2048 + piece * 512:2048 + (piece + 1) * 512],
                        wpc_ds[(wpc.i - 1) % 2], [], [wB])
                    for ti in range(2):
                        pz, pzB = bank(5 + pcnt % 2)
                        pcnt += 1
                        for kc in range(KC):
                            mm(pz[:, :], hT_s[:, kc, ti * 128:(ti + 1) * 128], w[:, kc, :], kc == 0, kc == KC - 1,
                               [wB, hT_sB], pzB)
                        stg, stgB = kvst.next()
                        cp("act", stg[:], pz[:, :], pzB, [stgB])
                        dst = sbk_s_d if piece < 2 else sbv_s_d
                        if "nodma" not in KV:
                            dma("sp", dst[ti * 128:(ti + 1) * 128, (piece % 2) * 512:(piece % 2) * 512 + 512], stg[:],
                                kvst_ds[(kvst.i - 1) % 2], [stgB], [])
                        if piece >= 2 and "novs" not in KV:
                            cp("dve", v_s[:, ti, (piece - 2) * 512:(piece - 2) * 512 + 512], pz[:, :], pzB,
                               [v_sB[ti][piece - 2]])
                if stop == 1.7:
                    S.flush()
                    return nc
                sf = Ring([sb("s_sf%d" % i, [128, 256], F32, p2b) for i in range(4)])
                sf_ds = [dsem("d_sf%d" % i) for i in range(4)]
                def sample_gla_gen():
                    for hh in range(4):
                        W, wup, gnb = load_gla_w(ws_gla_d[hh], hh)
                        for ti in range(2):
                            sts = []
                            for j in range(2):
                                st = State()
                                f, fB = sf.next()
                                st.k = (sf.i - 1) % 4
                                dma("sp", f[:], state_s_d[2 * ti + j, hh], sf_ds[st.k], [], [fB])
                                st.f32, st.fB = f[:], fB
                                snapshot(st)
                                sts.append(st)
                            yield from gla_tile(W, hT_s[:, :, ti * 128:(ti + 1) * 128], [hT_sB], wup, gnb, sts,
                                                lambda hh=hh, ti=ti: (
                                                    oaT_s[:, 2 * hh:2 * hh + 2, ti * 128:(ti + 1) * 128],
                                                    [oaT_sB[2 * hh + ti]]))
                            for j in range(2):
                                dma("sp", st_s_d[2 * ti + j, hh], sts[j].f32, sf_ds[sts[j].k], [sts[j].fB], [])
                            yield
                bg_s = Replay(record(sample_gla_gen()), 4 * 35)
                kcr = Ring([sb("kc%d" % i, [128, 8, 1024], BF16, p2b) for i in range(2)])
                vcr = Ring([sb("vc%d" % i, [128, 8, D], BF16, p2b) for i in range(2)])
                kc_ds = [dsem("d_kc%d" % i) for i in range(2)]
                vc_ds = [dsem("d_vc%d" % i) for i in range(2)]
                def load_kv(s, gk):
                    kt, ktB = kcr.next()
                    for hq in range(4):
                        dma("pool", kt[:, 2 * hq:2 * hq + 2, :],
                            kcT_d[s, 2 * hq:2 * hq + 2].rearrange("h d t -> d h t")[:, :, gk * 1024:(gk + 1) * 1024],
                            "auto", [], [ktB])
                    vt, vtB = vcr.next()
                    for hq in range(4):
                        dma("pool", vt[:, 2 * hq:2 * hq + 2, :], vc_d[s, :, gk * 8 + 2 * hq:gk * 8 + 2 * hq + 2, :],
                            "auto", [], [vtB])
                    return kt, ktB, vt, vtB

                order = [(s, gk) for s in range(4) for gk in range(3, -1, -1)]
                loaded = {order[0]: load_kv(*order[0])}
                for s in range(4):
                    ti, par = s // 2, s % 2
                    qcols = slice(s * 64, (s + 1) * 64)
                    blocks = [dict(
                        c0=0, mask=(ident, cst2[:, par * 512:(par + 1) * 512], 0, 512), mask_first=True,
                        z=[(kT_s[:, h, ti * 128:(ti + 1) * 128], qT_s[:, h, qcols], h * 64, h * 64 + 64,
                            [qkB[h], qkB[8 + h]]) for h in range(8)],
                        pv=[(v_s[:, ti, h * 128:(h + 1) * 128], h * 64, h * 64 + 64, v_sB[ti]) for h in range(8)])]
                    for gk in range(3, -1, -1):
                        for kb in range(7, -1, -1):
                            def mk(s=s, gk=gk, kb=kb, qcols=qcols):
                                kt, ktB, vt, vtB = loaded[(s, gk)]
                                return dict(
                                    c0=0,
                                    z=[(kt[:, h, kb * 128:(kb + 1) * 128], qT_s[:, h, qcols], h * 64, h * 64 + 64,
                                        [ktB, qkB[h]]) for h in range(8)],
                                    pv=[(vt[:, kb, h * 128:(h + 1) * 128], h * 64, h * 64 + 64, [vtB])
                                        for h in range(8)])
                            blk = dict(lazy=mk)
                            if kb == 6:
                                nxt = order.index((s, gk)) + 1
                                if nxt < len(order):
                                    blk["post"] = (lambda nxt=nxt: loaded.__setitem__(order[nxt], load_kv(*order[nxt])))
                            blocks.append(blk)

                    def evict(ob, obB, s=s):
                        cp("act", obT_s[:, :, s * 64:(s + 1) * 64], ob.rearrange("p (h q) -> p h q", h=8), obB,
                           [obT_sB[s]])
                    sb_stream(blocks, evict, bg_s)
                bg_s.drain()
                S.flush()
            if test2b:
                dbg_oa = dout("dbg_oa", [128, KC, NS_TOK], BF16)
                dbg_ob = dout("dbg_ob", [128, KC, NS_TOK], BF16)
                dma("sp", dbg_oa, oaT_s[:], "auto", oaT_sB, [])
                dma("sp", dbg_ob, obT_s[:], "auto", obT_sB, [])
                S.flush()
            if stop == 2 or test2b:
                return nc

            with ExitStack() as p2a:
                wsb = sb("wsb", [128, KC, 1024], BF16, p2a)
                B_wsb = Buf()
                for i in range(2):
                    dma("pool", wsb[:, :, i * 512:(i + 1) * 512], wp_sb_d[:, :, i * 512:(i + 1) * 512], ds_c, [],
                        [B_wsb])
                W, wup, gnb = load_gla_w(wp_gla_d, 4)
                kT_all = sb("kT_all", [128, 2, SEQ], BF16, p2a)
                v_all = sb("v_all", [128, 64, 256], BF16, p2a)
                kvB = bufs(16)
                hTb = Ring([sb("hTb%d" % i, [128, KC, 512], BF16, p2a) for i in range(2)])
                hTb_ds = [dsem("d_hTb%d" % i) for i in range(2)]
                qTb = Ring([sb("qTb%d" % i, [128, 2, 512], BF16, p2a) for i in range(2)])
                kvst = Ring([sb("kvstp%d" % i, [128, 512], F32, p2a) for i in range(2)])
                kvst_ds = [dsem("d_kvstp%d" % i) for i in range(2)]
                oaTb = Ring([sb("oaTb%d" % i, [128, 2, 512], BF16, p2a) for i in range(2)])
                oaTb_ds = [dsem("d_oaTb%d" % i) for i in range(2)]
                obst = Ring([sb("obst%d" % i, [128, 512], BF16, p2a) for i in range(2)])
                obst_ds = [dsem("d_obst%d" % i) for i in range(2)]
                B_ag2s = bufs(16)
                stp = State()
                stp_t = sb("stp_f32", [128, 256], F32, p2a)
                stp.f32, stp.fB = stp_t[:], Buf()
                memset(stp_t[:], 0.0, [stp.fB])
                snapshot(stp)
                ag1v = ag1_d.rearrange("(j r kc p) t -> j r p kc t", j=4, r=4, p=128)
                ag2sv = ag2s_d.rearrange("(k c p) t -> k p c t", k=8, p=128)
                cc2 = dsem("cc2", 1)
                B_ag2 = bufs(8)
                pcnt = 0
                for bi in range(16):
                    hb, hbB = hTb.next()
                    for half in range(2):
                        dma("sp", hb[:, half * 4:half * 4 + 4, :],
                            ag1v[bi % 4, bi // 4, :, half * 4:half * 4 + 4, :],
                            hTb_ds[(hTb.i - 1) % 2], [B_ag1[bi % 4]], [hbB])
                    tok = slice(bi * 512, (bi + 1) * 512)
                    qt, qtB = qTb.next()
                    for j in range(4):
                        pz, pzB = bank(5 + pcnt % 2)
                        pcnt += 1
                        for kc in range(KC):
                            mm(pz[:, :], wsb[:, kc, j * 128:(j + 1) * 128], hb[:, kc, :], kc == 0, kc == KC - 1,
                               [B_wsb, hbB], pzB)
                        if j < 2:
                            cp("act", qt[:, j, :], pz[:, :], pzB, [qtB], scale=float(128.0 ** -0.5))
                        else:
                            cp("dve", kT_all[:, j - 2, tok], pz[:, :], pzB, [kvB[bi]])
                    for t in range(4):
                        pz, pzB = bank(5 + pcnt % 2)
                        pcnt += 1
                        for kc in range(KC):
                            mm(pz[:, :], hb[:, kc, t * 128:(t + 1) * 128], wsb[:, kc, 512:1024], kc == 0, kc == KC - 1,
                               [B_wsb, hbB], pzB)
                        stg, stgB = kvst.next()
                        k_ = (kvst.i - 1) % 2
                        cp("act", stg[:], pz[:, :], pzB, [stgB])
                        rows = slice(bi * 512 + t * 128, bi * 512 + (t + 1) * 128)
                        dma("sp", sbk_p_d[rows, :], stg[:, 0:256], kvst_ds[k_], [stgB], [])
                        dma("sp", sbv_p_d[rows, :], stg[:, 256:512], kvst_ds[k_], [stgB], [])
                        cp("dve", v_all[:, bi * 4 + t, :], pz[:, 256:512], pzB, [kvB[bi]])
                    tk2 = slice((bi % 2) * 512, (bi % 2) * 512 + 512)

                    def gla_block_gen(bi=bi, hb=hb, hbB=hbB, tk2=tk2):
                        ot, otB = oaTb.next()
                        for t in range(4):
                            yield from gla_tile(W, hb[:, :, t * 128:(t + 1) * 128], [hbB], wup, gnb, [stp, stp],
                                                lambda t=t: (ot[:, :, t * 128:(t + 1) * 128], [otB]))
                        dma("sp", ag2sv[bi // 2, :, 0:2, tk2], ot[:], "auto", [otB], [B_ag2s[bi]])
                    bg_p = Replay(record(gla_block_gen()), 2 * (4 * bi + 6))
                    for hh in range(2):
                        blocks = []
                        for j in range(3, -1, -1):
                            kb = 4 * bi + j
                            c0 = 128 * j
                            blocks.append(dict(
                                c0=c0, mask=(ident, MASKB, c0, c0 + 128), mask_first=False,
                                z=[(kT_all[:, hh, kb * 128:(kb + 1) * 128], qt[:, hh, c0:512], c0, 512,
                                    [kvB[bi], qtB])],
                                pv=[(v_all[:, kb, hh * 128:(hh + 1) * 128], c0, 512, [kvB[bi]])]))
                        for kb in range(4 * bi - 1, -1, -1):
                            blocks.append(dict(
                                c0=0,
                                z=[(kT_all[:, hh, kb * 128:(kb + 1) * 128], qt[:, hh, :], 0, 512,
                                    [kvB[kb // 4], qtB])],
                                pv=[(v_all[:, kb, hh * 128:(hh + 1) * 128], 0, 512, [kvB[kb // 4]])]))

                        def evict(ob, obB, hh=hh, tk2=tk2, bi=bi):
                            o, oB = obst.next()
                            cp("act", o[:], ob[:, :], obB, [oB])
                            dma("sp", ag2sv[bi // 2, :, 2 + hh, tk2], o[:], obst_ds[(obst.i - 1) % 2], [oB],
                                [B_ag2s[bi]])
                        sb_stream(blocks, evict, bg_p)
                    bg_p.drain()
                    if bi % 2 == 1:
                        k = bi // 2
                        S.add("pool", ag_fn(ag2s_d[k * 512:(k + 1) * 512, :], ag2_d[k * 2048:(k + 1) * 2048, :]),
                              R=[B_ag2s[bi - 1], B_ag2s[bi]], W=[B_ag2[k]], dsem=cc2)
                dma("sp", st_p_d, stp.f32, ds_c, [stp.fB], [])
                S.flush()

        if stop == 3:
            return nc

        B_own = [bufs(4) for _ in range(2)]
        ds_own = dsem("d_own")

        pid_cache = {}

        def own_fn(kk, r4):
            def fn(e):
                if "g" not in pid_cache:
                    pid_cache["g"] = e.partition_id() % 4
                g_ = pid_cache["g"]
                return e.dma_start(out=own_d[r4 * 512:(r4 + 1) * 512, kk * 1024:(kk + 1) * 1024],
                                   in_=ag2_d[bass.ds((g_ * 2 + kk) * 2048 + r4 * 512, 512), :])
            return fn
        for kk in range(2):
            for r4 in range(4):
                S.add("pool", own_fn(kk, r4), R=B_ag2, W=[B_own[kk][r4]], dsem=ds_own)

        G3 = 512
        groups3 = [(o, min(G3, NTOK - o)) for o in range(0, NTOK, G3)]
        B_h2 = bufs(len(groups3))
        with ExitStack() as p3:
            lnp, lnpB = load_lnp(p3, 1)
            wo = sb("wo_sb", [128, KC, D], BF16, p3)
            B_wo = Buf()
            for i in range(2):
                dma("pool", wo[:, :, i * 512:(i + 1) * 512], wo_d[:, :, i * 512:(i + 1) * 512], ds_c, [], [B_wo])
            wmr = Ring([sb("wm%d" % i, [128, KC, 512], BF16, p3) for i in range(2)])
            wm_ds = [dsem("d_wm%d" % i) for i in range(2)]
            hin = Ring([sb("m_h%d" % i, [128, KC, G3], BF16, p3) for i in range(2)])
            oain = Ring([sb("m_oa%d" % i, [128, KC, G3], BF16, p3) for i in range(2)])
            obin = Ring([sb("m_ob%d" % i, [128, KC, G3], BF16, p3) for i in range(2)])
            in_ds = [dsem("d_min%d" % i) for i in range(2)]
            sgr = Ring([sb("m_sg%d" % i, [128, G3], F32, p3) for i in range(4)])
            t12 = Ring([sb("m_t%d" % i, [128, G3], F32, p3) for i in range(2)])
            mT = sb("m_mT", [128, KC, G3], BF16, p3)
            mTB = bufs(KC)
            rr = Ring([sb("m_r%d" % i, [128, D], F32, p3) for i in range(3)])
            r_ds = [dsem("d_mr%d" % i) for i in range(3)]
            hbf = Ring([sb("m_hbf%d" % i, [128, D], BF16, p3) for i in range(2)])
            h2Tg = sb("m_h2Tg", [128, KC, G3], BF16, p3)
            h2TgB = [bufs(2) for _ in range(G3 // 128)]
            ds_h2T = dsem("d_h2T")
            h2Tv = h2T_d.rearrange("(kc p) t -> p kc t", p=128)
            for gi, (t0, n) in enumerate(groups3):
                if t0 < NP_TOK:
                    hi, hiB = hin.next()
                    oi, oiB = oain.next()
                    qi, qiB = obin.next()
                    k_ = (hin.i - 1) % 2
                    dma("sp", hi[:, :, 0:n], hTv[t0 // 512], in_ds[k_], B_hTp, [hiB])
                    for r4 in range(4):
                        for which, dstt in ((0, oi), (1, qi)):
                            r0 = r4 * 512 + which * 256
                            dma("sp", dstt[:, 2 * r4:2 * r4 + 2, 0:n],
                                own_d[r0:r0 + 256, t0:t0 + n].rearrange("(c p) t -> p c t", p=128), in_ds[k_],
                                [B_own[t0 // 1024][r4]], [oiB if which == 0 else qiB])
                    hT_g, hT_gB = hi, [hiB]
                    oa_g, oa_gB = oi, [oiB]
                    ob_g, ob_gB = qi, [qiB]
                    off = 0
                else:
                    hT_g, hT_gB = hT_s, [hT_sB]
                    oa_g, oa_gB = oaT_s, oaT_sB
                    ob_g, ob_gB = obT_s, obT_sB
                    off = t0 - NP_TOK
                for fc in range(KC):
                    w, wB = wmr.next()
                    dma("pool", w[:], wmix_d[fc], wm_ds[(wmr.i - 1) % 2], [], [wB])
                    srcs = ((hT_g, hT_gB), (hT_g, hT_gB), (oa_g, oa_gB), (ob_g, ob_gB))
                    for j in range(4):
                        pz, pzB = bank(j)
                        src, srcB = srcs[j]
                        for kc in range(KC):
                            mm(pz[:, 0:n], w[:, kc, j * 128:(j + 1) * 128], src[:, kc, off:off + n], kc == 0,
                               kc == KC - 1, [wB] + srcB, pzB)
                    sga, sgaB = sgr.next()
                    sgb, sgbB = sgr.next()
                    act(sga[:, 0:n], bank(0)[0][:, 0:n], AF.Sigmoid, bank(0)[1], [sgaB])
                    act(sgb[:, 0:n], bank(1)[0][:, 0:n], AF.Sigmoid, bank(1)[1], [sgbB])
                    t1, t1B = t12.next()
                    t2, t2B = t12.next()
                    tt("dve", t1[:, 0:n], sga[:, 0:n], bank(2)[0][:, 0:n], ALU.mult, [sgaB] + bank(2)[1], [t1B])
                    tt("dve", t2[:, 0:n], sgb[:, 0:n], bank(3)[0][:, 0:n], ALU.mult, [sgbB] + bank(3)[1], [t2B])
                    tt("dve", mT[:, fc, 0:n], t1[:, 0:n], t2[:, 0:n], ALU.add, [t1B, t2B], [mTB[fc]])
                for t in range(n // 128):
                    po = PS[2]
                    for fc in range(KC):
                        for hf in range(2):
                            mm(po[:, hf * 512:(hf + 1) * 512], mT[:, fc, t * 128:(t + 1) * 128],
                               wo[:, fc, hf * 512:(hf + 1) * 512], fc == 0, fc == KC - 1, [mTB[fc], B_wo], PSQ[4 + hf])
                    r, rB = rr.next()
                    k_ = (rr.i - 1) % 3
                    rows = slice(t0 + t * 128, t0 + (t + 1) * 128)
                    dma("sp", r[:], h_tm_d[rows, :], r_ds[k_], [], [rB])
                    act(r[:], r[:], AF.Copy, [rB], [rB], scale=ALPHA)
                    tt("dve", r[:], r[:], po[:], ALU.add, [rB] + PSQ[4] + PSQ[5], [rB])
                    layer_norm(r[:], rB, lnp, lnpB)
                    dma("sp", h2_tm_d[rows, :], r[:], r_ds[k_], [rB], [B_h2[gi]])
                    hb, hbB = hbf.next()
                    cp("act", hb[:], r[:], [rB], [hbB])
                    transpose_tile(hb, hbB, lambda half, t=t: (
                        h2Tg[:, half * 4:half * 4 + 4, t * 128:(t + 1) * 128], h2TgB[t][half]), t)
                dma("sp", h2Tv[:, :, t0:t0 + n], h2Tg[:, :, 0:n], ds_h2T,
                    [b for bb in h2TgB[:n // 128] for b in bb], [B_h2[gi]])
            S.flush()

        if stop == 4:
            return nc
        ffn_phase("f2", h2T_d, h2_tm_d, f2_win_d, f2_wout_d, 2, y_d, None)
        S.flush()
    return nc


def _consts():
    c = np.zeros((128, 8 * 128), np.float32)
    s = np.arange(128)[:, None]
    t = np.arange(128)[None, :]
    same = (s // 64) == (t // 64)
    c[:, 0:128] = np.eye(128, dtype=np.float32)
    c[:, 128:256] = np.where(s >= t, -1.0, 0.0)
    c[:, 256:384] = np.where(s < t, -1.0, 0.0)
    c[:, 384:512] = np.where(s < t, 0.0, NEG)
    c[:, 512:640] = np.where((s <= t) & same, -1.0 / 16, 0.0)
    c[:, 640:768] = np.where((s > t) & same, -1.0 / 16, 0.0)
    c[:, 768:896] = np.where((s <= t) & same, 1.0, 0.0)
    c[:, 896:1024] = -1.0
    c2 = np.full((128, 1024), NEG, np.float32)
    sk = np.arange(128)[:, None]
    tq = np.arange(64)[None, :]
    for par in range(2):
        m = np.where(((sk // 64) == par) & ((sk % 64) < tq), 0.0, NEG).astype(np.float32)
        for h in range(8):
            c2[:, par * 512 + h * 64:par * 512 + (h + 1) * 64] = m
    return c, c2


def _r(w):
    return np.ascontiguousarray(w.reshape(KC, 128, w.shape[1]).transpose(1, 0, 2))


def _ffn_layout(w_in, w_out):
    wi = w_in.reshape(KC, 128, 2, NFC, 128)
    wi = np.ascontiguousarray(wi.transpose(3, 1, 0, 2, 4)).reshape(NFC, 128, KC * 256)
    wo = np.ascontiguousarray(w_out.reshape(NFC, 128, D).transpose(1, 0, 2)).reshape(128, NFC * D)
    return wi, wo


O_GQ, O_GK, O_GV, O_GR, O_GLR, O_SQ, O_SK, O_SV, O_GA, O_GB = 0, 512, 1024, 2048, 3072, 3088, 4112, 5136, 6160, 7184


def make_in_maps(inp):
    f1_win, f1_wout = _ffn_layout(inp["ffn1_w_in"][0], inp["ffn1_w_out"][0])
    f2_win, f2_wout = _ffn_layout(inp["ffn2_w_in"][0], inp["ffn2_w_out"][0])
    lnp = np.ascontiguousarray(np.stack([inp["ln1_g"][0], inp["ln1_b"][0], inp["ln2_g"][0], inp["ln2_b"][0],
                                         inp["ln3_g"][0], inp["ln3_b"][0]]).astype(np.float32))
    consts, consts2 = _consts()
    w = inp["w_in"][0]

    def c(off, n):
        return w[:, off:off + n]

    def gla_cols(h):
        return np.concatenate([c(O_GQ + h * 128, 128), c(O_GK + h * 128, 128), c(O_GV + h * 256, 256),
                               c(O_GR + h * 256, 256), c(O_GK + h * 128, 128)], axis=1)

    ws_gla = np.stack([_r(gla_cols(h)) for h in range(4)])
    ws_sb = _r(np.concatenate([c(O_SQ, 1024), c(O_SK, 1024), c(O_SK, 1024), c(O_SV, 1024)], axis=1))
    wglr = _r(c(O_GLR, 16))
    wupa = np.zeros((4, 128, 128), np.float32)
    for h in range(4):
        wupa[h, 0:16] = inp["w_gla_gate_up"][0][:, h * 128:(h + 1) * 128]
        wupa[h, 32] = inp["b_gla_gate"][0][h * 128:(h + 1) * 128]
    gn = inp["g_gla_norm"][0]
    wmix = np.stack([_r(np.concatenate([c(O_GA + fc * 128, 128), c(O_GB + fc * 128, 128),
                                        inp["w_gla_o"][0][:, fc * 128:(fc + 1) * 128],
                                        inp["w_sb_o"][0][:, fc * 128:(fc + 1) * 128]], axis=1)) for fc in range(8)])
    wo = _r(inp["w_out"][0])
    maps = []
    for core in range(NCORES):
        b, g = core // 4, core % 4
        xtm = np.concatenate([inp["x_prompt"][b, g * NP_TOK:(g + 1) * NP_TOK],
                              inp["x_sample"][4 * core:4 * core + 4].reshape(NS_TOK, D)], axis=0)
        xtm = np.ascontiguousarray(xtm)
        wp_sb = _r(np.concatenate([c(O_SQ + 2 * g * 128, 256), c(O_SK + 2 * g * 128, 256),
                                   c(O_SK + 2 * g * 128, 256), c(O_SV + 2 * g * 128, 256)], axis=1))
        sl = slice(4 * core, 4 * core + 4)
        kcT = np.ascontiguousarray(inp["cache_sb_k"][0, sl].transpose(0, 2, 3, 1))
        vc = np.ascontiguousarray(inp["cache_sb_v"][0, sl].reshape(4, 32, 128, D).transpose(0, 2, 1, 3))
        maps.append(dict(
            xT=np.ascontiguousarray(xtm.T), xtm=xtm, f1_win=f1_win, f1_wout=f1_wout, f2_win=f2_win, f2_wout=f2_wout,
            lnp=lnp, consts=consts, consts2=consts2, wp_gla=ws_gla[g], wp_sb=wp_sb, ws_gla=ws_gla, ws_sb=ws_sb,
            wglr=wglr, wup=np.ascontiguousarray(np.concatenate([wupa, wupa[g:g + 1]])),
            gnorm=np.ascontiguousarray(np.concatenate([gn, gn[g:g + 1]])),
            state_s=np.ascontiguousarray(inp["state_gla"][0, sl]), kcT=kcT, vc=vc, wmix=wmix, wo=wo))
    return maps


_NC_CACHE = {}
_STOP = None


def kernel(**inputs):
    inp = {k: np.asarray(v) for k, v in inputs.items()}
    if "nc" not in _NC_CACHE:
        _NC_CACHE["nc"] = build_nc(stop=_STOP)
    nc = _NC_CACHE["nc"]
    maps = make_in_maps(inp)
    res = run_bass_kernel_spmd(nc, maps, core_ids=list(range(NCORES)))
    R = res.results
    y_p = np.zeros((2, SEQ, D), np.float32)
    y_s = np.zeros((32, 64, D), np.float32)
    st_p = np.zeros((1, 2, 4, 128, 256), np.float32)
    k_p = np.zeros((1, 2, SEQ, 8, 128), np.float32)
    v_p = np.zeros((1, 2, SEQ, 8, 128), np.float32)
    st_s = np.zeros((1, 32, 4, 128, 256), np.float32)
    k_s = np.zeros((1, 32, 64, 8, 128), np.float32)
    v_s = np.zeros((1, 32, 64, 8, 128), np.float32)
    for core in range(NCORES):
        b, g = core // 4, core % 4
        r = R[core]
        y = np.asarray(r["y"])
        y_p[b, g * NP_TOK:(g + 1) * NP_TOK] = y[:NP_TOK]
        y_s[4 * core:4 * core + 4] = y[NP_TOK:].reshape(4, 64, D)
        st_p[0, b, g] = np.asarray(r["st_p"])
        k_p[0, b, :, 2 * g:2 * g + 2, :] = np.asarray(r["sbk_p"]).reshape(SEQ, 2, 128)
        v_p[0, b, :, 2 * g:2 * g + 2, :] = np.asarray(r["sbv_p"]).reshape(SEQ, 2, 128)
        st_s[0, 4 * core:4 * core + 4] = np.asarray(r["st_s"])
        k_s[0, 4 * core:4 * core + 4] = np.asarray(r["sbk_s"]).reshape(4, 64, 8, 128)
        v_s[0, 4 * core:4 * core + 4] = np.asarray(r["sbv_s"]).reshape(4, 64, 8, 128)
    return (y_p, y_s, st_p, k_p, v_p, st_s, k_s, v_s)
```

```python
import numpy as np
from contextlib import ExitStack
import concourse.bass as bass
import concourse.mybir as mybir
from concourse.bass_utils import run_bass_kernel_spmd

F32 = mybir.dt.float32
BF16 = mybir.dt.bfloat16
AF = mybir.ActivationFunctionType
ALU = mybir.AluOpType

NCORES = 8
D = 1024
KC = 8
SEQ = 8192
NP_TOK = 2048
NS_TOK = 256
NTOK = NP_TOK + NS_TOK
PAST = 4096
DFF = 2816
NFC = 22
ALPHA = 2.0 ** 0.25
EPS = 1e-5
NEG = -30000.0
G = 768
NG = NTOK // G


class Buf:
    __slots__ = ("w", "r", "excl")

    def __init__(self, excl=False):
        self.w = None
        self.r = []
        self.excl = excl


def bufs(n):
    return [Buf() for _ in range(n)]


class DSem:
    def __init__(self, sem, inc=16):
        self.sem = sem
        self.inc = inc
        self.val = 0
        self.last = None


class Op:
    __slots__ = ("eng", "fn", "deps", "dsem", "dval", "needed", "count")


class Sched:
    ENG = ("pe", "act", "dve", "pool", "sp")

    def __init__(self, nc, esems):
        self.nc = nc
        self.esem = esems
        self.ops = []
        self.flushed = 0
        self.cnt = {e: 0 for e in self.ENG}
        self.last = {e: None for e in self.ENG}
        self.waited = {e: {} for e in self.ENG}
        self.dsems = []
        self.dq = {}
        self.dqi = {}

    def dsem(self, sem, inc=16):
        d = DSem(sem, inc)
        self.dsems.append(d)
        return d

    def add(self, eng, fn, R=(), W=(), dsem=None, extra=()):
        op = Op()
        deps = set(extra)
        if dsem is not None:
            key = "cc" if dsem == "cc" else eng
            i = self.dqi.get(key, 0)
            self.dqi[key] = i + 1
            dsem = self.dq[key][i % len(self.dq[key])]
            if dsem.last is not None:
                deps.add(dsem.last)
        op.eng, op.fn, op.dsem, op.needed, op.count, op.dval = eng, fn, dsem, False, 0, 0
        W = list(W) + [b for b in R if b.excl]
        R = [b for b in R if not b.excl]
        for b in R:
            if b.w is not None:
                deps.add(b.w)
        for b in W:
            if b.w is not None:
                deps.add(b.w)
            for r in b.r:
                deps.add(r)
        deps.discard(op)
        op.deps = deps
        for b in R:
            b.r.append(op)
        for b in W:
            b.w = op
            b.r = []
        if dsem is not None:
            dsem.val += dsem.inc
            op.dval = dsem.val
            dsem.last = op
        self.ops.append(op)
        self.last[eng] = op
        return op

    def barrier(self):
        deps = [o for o in self.last.values() if o is not None]
        deps += [d.last for d in self.dsems if d.last is not None]
        for e in self.ENG:
            self.add(e, None, extra=deps)

    def flush(self):
        self.barrier()
        ops = self.ops[self.flushed:]
        self.flushed = len(self.ops)
        for op in ops:
            for d in op.deps:
                if d.dsem is None and not (d.eng == "pe" and op.eng == "pe"):
                    d.needed = True
        for op in ops:
            if op.dsem is None and op.needed:
                self.cnt[op.eng] += 1
                op.count = self.cnt[op.eng]
        per = {e: [o for o in ops if o.eng == e] for e in self.ENG}

        def emit(e, name):
            waited = self.waited[name]
            for op in per[name]:
                w = {}
                for d in op.deps:
                    if d.dsem is not None:
                        key, so, val = ("d", id(d.dsem)), d.dsem.sem, d.dval
                    else:
                        if d.eng == "pe" and name == "pe":
                            continue
                        key, so, val = d.eng, self.esem[d.eng], d.count
                    if val > w.get(key, (None, 0))[1]:
                        w[key] = (so, val)
                for key, (so, val) in w.items():
                    if waited.get(key, 0) >= val:
                        continue
                    e.wait_ge(so, val)
                    waited[key] = val
                if op.fn is None:
                    continue
                ins = op.fn(e)
                if op.dsem is not None:
                    ins.then_inc(op.dsem.sem, op.dsem.inc)
                elif op.needed:
                    ins.then_inc(self.esem[name], 1)

        with self.nc.Block() as block:
            @block.tensor
            def _(e):
                emit(e, "pe")

            @block.scalar
            def _(e):
                emit(e, "act")

            @block.vector
            def _(e):
                emit(e, "dve")

            @block.gpsimd
            def _(e):
                emit(e, "pool")

            @block.sync
            def _(e):
                emit(e, "sp")


class Ring:
    def __init__(self, tiles):
        self.t = tiles
        self.b = bufs(len(tiles))
        self.i = 0

    def next(self):
        k = self.i % len(self.t)
        self.i += 1
        return self.t[k], self.b[k]


def build_nc(dbg=False, stop=None, test2b=False):
    nc = bass.Bass("TRN2", target_bir_lowering=False)

    T2B = ("consts", "consts2", "ws_gla", "ws_sb", "wglr", "wup", "gnorm", "state_s", "kcT", "vc")

    def din(name, shape, dt=F32):
        if test2b and name not in T2B:
            return None
        return nc.dram_tensor(name, list(shape), dt, kind="ExternalInput").ap()

    def dout(name, shape, dt=F32):
        return nc.dram_tensor(name, list(shape), dt, kind="ExternalOutput").ap()

    def dint(name, shape, dt):
        return nc.dram_tensor(name, list(shape), dt, kind="Internal").ap()

    xT_d = din("xT", [D, NTOK])
    xtm_d = din("xtm", [NTOK, D])
    f1_win_d = din("f1_win", [NFC, 128, KC * 256])
    f1_wout_d = din("f1_wout", [128, NFC * D])
    f2_win_d = din("f2_win", [NFC, 128, KC * 256])
    f2_wout_d = din("f2_wout", [128, NFC * D])
    lnp_d = din("lnp", [6, D])
    consts_d = din("consts", [128, 8 * 128])
    consts2_d = din("consts2", [128, 1024])
    wp_gla_d = din("wp_gla", [128, KC, 896])
    wp_sb_d = din("wp_sb", [128, KC, 1024])
    ws_gla_d = din("ws_gla", [4, 128, KC, 896])
    ws_sb_d = din("ws_sb", [128, KC, 4096])
    wglr_d = din("wglr", [128, KC, 16])
    wup_d = din("wup", [5, 128, 128])
    gnorm_d = din("gnorm", [5, 256])
    state_s_d = din("state_s", [4, 4, 128, 256])
    kcT_d = din("kcT", [4, 8, 128, PAST])
    vc_d = din("vc", [4, 128, 32, D])
    wmix_d = din("wmix", [8, 128, KC, 512])
    wo_d = din("wo", [128, KC, D])
    y_d = dout("y", [NTOK, D])
    st_p_d = dout("st_p", [128, 256])
    sbk_p_d = dout("sbk_p", [SEQ, 256])
    sbv_p_d = dout("sbv_p", [SEQ, 256])
    st_s_d = dout("st_s", [4, 4, 128, 256])
    sbk_s_d = dout("sbk_s", [NS_TOK, D])
    sbv_s_d = dout("sbv_s", [NS_TOK, D])
    h_tm_d = dint("h_tm", [NTOK, D], F32)
    hT_p_d = dint("hT_p", [4 * D, 512], BF16)
    ag1_d = dint("ag1", [4 * 4 * D, 512], BF16)
    ag2s_d = dint("ag2s", [8 * 512, 1024], BF16)
    ag2_d = dint("ag2", [8 * 2048, 1024], BF16)
    own_d = dint("ag2own", [4 * 512, NP_TOK], BF16)
    h2_tm_d = dint("h2_tm", [NTOK, D], F32)
    h2T_d = dint("h2T", [D, NTOK], BF16)
    GROUPS = [[0, 1, 2, 3], [4, 5, 6, 7]]

    es = ExitStack()
    with es:
        def sem(name):
            return es.enter_context(nc.semaphore(name))

        esems = {e: sem("s_" + e) for e in Sched.ENG}
        S = Sched(nc, esems)
        for q_, n_, inc_ in (("sp", 12, 16), ("pool", 12, 16), ("cc", 4, 1)):
            S.dq[q_] = [S.dsem(sem("dq_%s%d" % (q_, i)), inc_) for i in range(n_)]

        def sb(name, shape, dt, stack=es):
            return stack.enter_context(nc.sbuf_tensor(name, list(shape), dt))

        PS = [es.enter_context(nc.psum_tensor("ps%d" % i, [128, 1024], F32)) for i in range(4)]
        PSQ = [[Buf(excl=True)] * 4 for _ in range(8)]

        def bank(i):
            return PS[i // 2][:, (i % 2) * 512:(i % 2) * 512 + 512], PSQ[i]

        def mm(out, lhsT, rhs, start, stop, R, W, sg=False):
            if sg:
                return S.add("pe", lambda e: e.matmul(out, lhsT, rhs, start=start, stop=stop, skip_group_check=True),
                             R=R, W=W)
            return S.add("pe", lambda e: e.matmul(out, lhsT, rhs, start=start, stop=stop), R=R, W=W)

        def act(out, in_, func, R, W, bias=0.0, scale=1.0, accum_out=None):
            if accum_out is None:
                return S.add("act", lambda e: e.activation(out, in_, func, bias=bias, scale=scale), R=R, W=W)
            return S.add("act", lambda e: e.activation(out, in_, func, bias=bias, scale=scale,
                                                       accum_out=accum_out), R=R, W=W)

        def tt(eng, out, in0, in1, op, R, W):
            return S.add(eng, lambda e: e.tensor_tensor(out, in0, in1, op), R=R, W=W)

        def ts(eng, out, in0, s1, s2, op0, op1, R, W):
            if op1 is None:
                return S.add(eng, lambda e: e.tensor_scalar(out, in0, s1, None, op0), R=R, W=W)
            return S.add(eng, lambda e: e.tensor_scalar(out, in0, s1, s2, op0, op1), R=R, W=W)

        def stt(eng, out, in0, scalar, in1, op0, op1, R, W):
            return S.add(eng, lambda e: e.scalar_tensor_tensor(out, in0, scalar, in1, op0, op1), R=R, W=W)

        def recip(out, in_, R, W):
            return S.add("dve", lambda e: e.reciprocal(out, in_), R=R, W=W)

        def cp(eng, out, in_, R, W, scale=None):
            if eng == "act":
                if scale is None:
                    return S.add("act", lambda e: e.activation(out, in_, AF.Copy), R=R, W=W)
                return S.add("act", lambda e: e.activation(out, in_, AF.Copy, scale=scale), R=R, W=W)
            return S.add(eng, lambda e: e.tensor_copy(out, in_), R=R, W=W)

        def memset(out, val, W):
            return S.add("dve", lambda e: e.memset(out, val), R=[], W=W)

        def dma(q, out, in_, ds, R, W):
            return S.add(q, lambda e: e.dma_start(out=out, in_=in_), R=R, W=W, dsem=ds)

        def dsem(name, inc=16):
            return "cc" if inc == 1 else "auto"

        cst = sb("cst", [128, 8 * 128], BF16)
        cstf = sb("cstf", [128, 128], F32)
        B_cst = Buf()
        ds_c = dsem("d_c")
        dma("pool", cst[:], consts_d, ds_c, [], [B_cst])
        dma("sp", cstf[:], consts_d[:, 6 * 128:7 * 128], ds_c, [], [B_cst])
        ident = cst[:, 0:128]
        NEGTRI = cst[:, 128:256]
        NEGTRIC = cst[:, 256:384]
        MASKB = cst[:, 384:512]
        TRI_INCL = cst[:, 512:640]
        TRI_REV = cst[:, 640:768]
        NEGONES = cst[:, 896:1024]
        CAUSAL2 = cstf[:, 0:128]

        hT_s = sb("hT_s", [128, KC, NS_TOK], BF16)
        hT_sB = Buf()
        oaT_s = sb("oaT_s", [128, KC, NS_TOK], BF16)
        oaT_sB = bufs(8)
        obT_s = sb("obT_s", [128, KC, NS_TOK], BF16)
        obT_sB = bufs(4)
        lnw = dict(
            st=Ring([sb("ln_st%d" % i, [128, 12], F32) for i in range(2)]),
            mv=Ring([sb("ln_mv%d" % i, [128, 4], F32) for i in range(2)]),
        )

        def load_lnp(stack, li):
            t = sb("lnp_sb%d" % li, [128, 2 * D], F32, stack)
            b = Buf()
            for i in range(2):
                dma("sp", t[:, i * D:(i + 1) * D], lnp_d[2 * li + i:2 * li + i + 1, :].partition_broadcast(128),
                    ds_c, [], [b])
            return t, b

        def layer_norm(r_ap, rB, lnp, lnpB):
            st, stB = lnw["st"].next()
            mv, mvB = lnw["mv"].next()
            S.add("dve", lambda e: e.bn_stats(st[:, 0:6], r_ap[:, 0:512]), R=[rB], W=[stB])
            S.add("dve", lambda e: e.bn_stats(st[:, 6:12], r_ap[:, 512:1024]), R=[rB, stB], W=[stB])
            S.add("dve", lambda e: e.bn_aggr(mv[:, 0:2], st[:, 0:12]), R=[stB], W=[mvB])
            act(mv[:, 3:4], mv[:, 1:2], AF.Sqrt, [mvB], [mvB], bias=EPS)
            recip(mv[:, 2:3], mv[:, 3:4], [mvB], [mvB])
            ts("dve", r_ap, r_ap, mv[:, 0:1], mv[:, 2:3], ALU.subtract, ALU.mult, [rB, mvB], [rB])
            tt("dve", r_ap, r_ap, lnp[:, 0:D], ALU.mult, [rB, lnpB], [rB])
            tt("dve", r_ap, r_ap, lnp[:, D:2 * D], ALU.add, [rB, lnpB], [rB])

        def transpose_tile(hb, hbB, dst_fn, t):
            for half in range(2):
                pt, ptB = bank(6 + half)
                for j in range(4):
                    kc = half * 4 + j
                    mm(pt[:, j * 128:(j + 1) * 128], hb[:, kc * 128:(kc + 1) * 128], ident, True, True,
                       [hbB, B_cst], ptB[j:j + 1])
                dst, dB = dst_fn(half)
                cp("act" if half else "dve", dst, pt.rearrange("p (j n) -> p j n", j=4), ptB, [dB])

        def ffn_phase(tag, srcT_d, res_d, win_d, wout_d, li, out_tm_d, hT_out):
            with ExitStack() as p1:
                xT = sb(tag + "xT", [128, KC, NTOK], BF16, p1)
                xTB = bufs(NG)
                ds_x = dsem(tag + "d_x")
                xTv = srcT_d.rearrange("(kc p) t -> p kc t", p=128)
                for gi in range(NG):
                    for kc in range(KC):
                        dma("pool", xT[:, kc, gi * G:(gi + 1) * G], xTv[:, kc, gi * G:(gi + 1) * G], ds_x, [],
                            [xTB[gi]])
                wout = sb(tag + "wout", [128, NFC * D], BF16, p1)
                woutB = Buf()
                ds_wo = dsem(tag + "d_wo")
                for i in range(2):
                    dma("pool", wout[:, i * 11 * D:(i + 1) * 11 * D], wout_d[:, i * 11 * D:(i + 1) * 11 * D], ds_wo,
                        [], [woutB])
                lnp, lnpB = load_lnp(p1, li)
                wring = Ring([sb(tag + "w%d" % i, [128, KC * 256], BF16, p1) for i in range(3)])
                wds = [dsem(tag + "d_w%d" % i) for i in range(3)]
                sgr = Ring([sb(tag + "sg%d" % i, [128, 512], F32, p1) for i in range(2)])
                actT = sb(tag + "actT", [128, NFC, G], BF16, p1)
                actB = [bufs(2) for _ in range(NFC)]
                rr = Ring([sb(tag + "r%d" % i, [128, D], F32, p1) for i in range(3)])
                ds_ho = [dsem(tag + "d_ho%d" % i) for i in range(3)]
                xres = Ring([sb(tag + "x%d" % i, [128, D], F32, p1) for i in range(2)])
                xres_ds = [dsem(tag + "d_xr%d" % i) for i in range(2)]
                if hT_out is not None:
                    hbf = Ring([sb(tag + "hbf%d" % i, [128, D], BF16, p1) for i in range(2)])
                    hTg = sb(tag + "hTg", [128, KC, G], BF16, p1)
                    hTgB = [bufs(2) for _ in range(G // 128)]
                    hTgAll = [b for bb in hTgB for b in bb]
                nblk = [(o, min(512, G - o)) for o in range(0, G, 512)]
                gu = 0
                for gi in range(NG):
                    t0 = gi * G
                    for c in range(NFC):
                        w, wB = wring.next()
                        dma("pool", w[:], win_d[c], wds[(wring.i - 1) % 3], [], [wB])
                        for bi, (o, n) in enumerate(nblk):
                            pg, pgB = bank(2 * (gu % 2))
                            pu, puB = bank(2 * (gu % 2) + 1)
                            gu += 1
                            for kc in range(KC):
                                mm(pg[:, :n], w[:, kc * 256:kc * 256 + 128], xT[:, kc, t0 + o:t0 + o + n], kc == 0,
                                   kc == KC - 1, [wB, xTB[gi]], pgB)
                            for kc in range(KC):
                                mm(pu[:, :n], w[:, kc * 256 + 128:kc * 256 + 256], xT[:, kc, t0 + o:t0 + o + n],
                                   kc == 0, kc == KC - 1, [wB, xTB[gi]], puB)
                            sg, sgB = sgr.next()
                            act(sg[:, :n], pg[:, :n], AF.Silu, pgB, [sgB])
                            tt("dve", actT[:, c, o:o + n], sg[:, :n], pu[:, :n], ALU.mult, [sgB] + puB, [actB[c][bi]])
                    for t in range(G // 128):
                        po = PS[2]
                        for c in range(NFC):
                            for hf in range(2):
                                mm(po[:, hf * 512:(hf + 1) * 512], actT[:, c, t * 128:(t + 1) * 128],
                                   wout[:, c * D + hf * 512:c * D + hf * 512 + 512], c == 0, c == NFC - 1,
                                   [actB[c][(t * 128) // 512], woutB], PSQ[4 + hf])
                        x, xB = xres.next()
                        dma("sp", x[:], res_d[t0 + t * 128:t0 + (t + 1) * 128, :], xres_ds[(xres.i - 1) % 2], [], [xB])
                        r, rB = rr.next()
                        k = (rr.i - 1) % 3
                        act(r[:], x[:], AF.Copy, [xB], [rB], scale=ALPHA)
                        stt("dve", r[:], po[:], 0.5, r[:], ALU.mult, ALU.add, PSQ[4] + PSQ[5] + [rB], [rB])
                        layer_norm(r[:], rB, lnp, lnpB)
                        dma("sp", out_tm_d[t0 + t * 128:t0 + (t + 1) * 128, :], r[:], ds_ho[k], [rB], [])
                        if hT_out is not None:
                            hb, hbB = hbf.next()
                            cp("act", hb[:], r[:], [rB], [hbB])
                            transpose_tile(hb, hbB, lambda half, t=t: (
                                hTg[:, half * 4:half * 4 + 4, t * 128:(t + 1) * 128], hTgB[t][half]), t)
                    if hT_out is not None:
                        hT_out(gi, hTg, hTgAll)
                S.flush()

        ds_hT = dsem("d_hT")
        hTv = hT_p_d.rearrange("(j kc p) t -> j p kc t", j=4, p=128)
        B_hTp = bufs(4)

        def ship_h(gi, hTg, hB):
            t0 = gi * G
            npr = max(0, min(NP_TOK - t0, G))
            for j in range(4):
                a, b = max(t0, 512 * j), min(t0 + npr, 512 * (j + 1))
                if a < b:
                    dma("sp", hTv[j, :, :, a - 512 * j:b - 512 * j], hTg[:, :, a - t0:b - t0], ds_hT, hB, [B_hTp[j]])
            if npr < G:
                cp("dve", hT_s[:, :, :], hTg[:, :, npr:G], hB, [hT_sB])

        if test2b:
            hT_in_d = nc.dram_tensor("hT_s_in", [D, NS_TOK], F32, kind="ExternalInput").ap()
            dma("pool", hT_s[:], hT_in_d.rearrange("(kc p) t -> p kc t", p=128), ds_c, [], [hT_sB])
        else:
            ffn_phase("f1", xT_d, xtm_d, f1_win_d, f1_wout_d, 0, h_tm_d, ship_h)
        if stop == 1:
            return nc

        cc1 = dsem("cc1", 1)
        B_ag1 = bufs(4)

        def ag_fn(src, dst):
            return lambda e: e.collective_compute("AllGather", ALU.bypass, replica_groups=GROUPS, ins=[src], outs=[dst])

        if stop == 1.5:
            S.flush()
            return nc
        LN_QS = float(np.log(128.0 ** -0.5))
        with ExitStack() as p2:
            cst2 = sb("cst2", [128, 1024], BF16, p2)
            dma("pool", cst2[:], consts2_d, ds_c, [], [B_cst])
            wglr = sb("wglr_sb", [128, KC, 128], BF16, p2)
            B_wglr = Buf()
            memset(wglr[:], 0.0, [B_wglr])
            dma("pool", wglr[:, :, 0:16], wglr_d, ds_c, [], [B_wglr])
            wgla = Ring([sb("wgla%d" % i, [128, KC, 896], BF16, p2) for i in range(2)])
            wgla_ds = [dsem("d_wgla%d" % i) for i in range(2)]
            wupr = Ring([sb("wup%d" % i, [128, 128], BF16, p2) for i in range(2)])
            gnbr = Ring([sb("gnb%d" % i, [128, 256], F32, p2) for i in range(2)])

            def load_gla_w(src_ap, hidx):
                w, wB = wgla.next()
                k = (wgla.i - 1) % 2
                dma("pool", w[:], src_ap, wgla_ds[k], [], [wB])
                wu, wuB = wupr.next()
                dma("pool", wu[:], wup_d[hidx], wgla_ds[k], [], [wuB])
                gn, gnB = gnbr.next()
                dma("sp", gn[:], gnorm_d[hidx:hidx + 1, :].partition_broadcast(128), wgla_ds[k], [], [gnB])
                W = dict(q=w[:, :, 0:128], k=w[:, :, 128:256], vr=w[:, :, 256:768], ktm=w[:, :, 768:896], B=wB)
                return W, (wu[:], wuB), (gn[:], gnB)

            def ring(name, shape, dt, n=2, zero=False, ones_row=False):
                tiles = [sb("%s%d" % (name, i), shape, dt, p2) for i in range(n)]
                r = Ring(tiles)
                if zero:
                    for t, b in zip(tiles, r.b):
                        memset(t[:], 0.0, [b])
                        if ones_row:
                            memset(t[32:33, :], 1.0, [b])
                return r

            R = dict(
                glrT=ring("g_glrT", [128, 128], BF16, zero=True, ones_row=True),
                e1=ring("g_e1", [128, 128], F32), la=ring("g_la", [128, 128], BF16),
                eb=ring("g_eb", [128, 128], F32), enb=ring("g_enb", [128, 128], F32),
                ek=ring("g_ek", [128, 128], F32), dec=ring("g_dec", [128, 2], F32),
                qA=ring("g_qA", [128, 128], BF16, zero=True), qB=ring("g_qB", [128, 128], BF16, zero=True),
                kg=ring("g_kg", [128, 128], BF16),
                kd0=ring("g_kd0", [128, 128], BF16, zero=True), kd1=ring("g_kd1", [128, 128], BF16, zero=True),
                vb=ring("g_vb", [128, 256], BF16), eg=ring("g_eg", [128, 256], F32),
                gg=ring("g_gg", [128, 256], F32), at=ring("g_at", [128, 128], BF16),
                ss=ring("g_ss", [128, 4], F32), oa=ring("g_oa", [128, 256], BF16),
                sbf=ring("g_sbf", [128, 256], BF16, n=6),
            )
            junk = sb("g_junk", [128, 256], F32, p2)
            junkB = Buf()

            class State:
                pass

            def snapshot(st):
                nb, nbB = R["sbf"].next()
                cp("act", nb[:], st.f32, [st.fB], [nbB])
                st.bf, st.bfB = nb[:], nbB

            def update(st, dec_ap, decB, U_ap, UB):
                stt("dve", st.f32, st.f32, dec_ap, U_ap, ALU.mult, ALU.add, [st.fB, decB] + UB, [st.fB])
                snapshot(st)

            def gla_tile(W, hT_t, hTB, wup, gnb, states, dst_fn):
                pA, Aq = bank(0)
                pB_, Bq = bank(1)
                pC, Cq = bank(2)
                pD, Dq = bank(3)
                wu, wuB = wup
                gn, gnB = gnb
                for j, key in enumerate(("q", "k", "glr")):
                    wk = wglr if key == "glr" else W[key]
                    wkB = B_wglr if key == "glr" else W["B"]
                    for kc in range(KC):
                        mm(pA[:, j * 128:(j + 1) * 128], wk[:, kc, :], hT_t[:, kc, :], kc == 0, kc == KC - 1,
                           [wkB] + hTB, Aq[j:j + 1])
                for kc in range(KC):
                    mm(pB_[:, 0:512], hT_t[:, kc, :], W["vr"][:, kc, :], kc == 0, kc == KC - 1, [W["B"]] + hTB, Bq)
                for kc in range(KC):
                    mm(pC[:, 0:128], hT_t[:, kc, :], W["ktm"][:, kc, :], kc == 0, kc == KC - 1, [W["B"]] + hTB,
                       Cq[0:1])
                yield
                gp, gpB = R["glrT"].next()
                cp("dve", gp[0:32, :], pA[0:32, 256:384], Aq[2:3], [gpB])
                mm(pC[:, 128:256], gp[:], wu, True, True, [gpB, wuB], Cq[1:2])
                yield
                e1, e1B = R["e1"].next()
                la, laB = R["la"].next()
                act(e1[:], pC[:, 128:256], AF.Exp, Cq[1:2], [e1B], scale=-1.0)
                act(la[:], e1[:], AF.Ln, [e1B], [laB], bias=1.0)
                mm(pC[:, 256:384], la[:], TRI_INCL, True, True, [laB, B_cst], Cq[2:3])
                mm(pC[:, 384:512], TRI_REV, la[:], True, True, [laB, B_cst], Cq[3:4])
                yield
                eb, ebB = R["eb"].next()
                enb, enbB = R["enb"].next()
                ek, ekB = R["ek"].next()
                dec, decB = R["dec"].next()
                act(eb[:], pC[:, 256:384], AF.Exp, Cq[2:3], [ebB], bias=LN_QS)
                act(enb[:], pC[:, 256:384], AF.Exp, Cq[2:3], [enbB], scale=-1.0)
                act(ek[:], pC[:, 384:512], AF.Exp, Cq[3:4], [ekB])
                act(dec[:, 0:1], pC[:, 256 + 63:256 + 64], AF.Exp, Cq[2:3], [decB])
                act(dec[:, 1:2], pC[:, 256 + 127:256 + 128], AF.Exp, Cq[2:3], [decB])
                yield
                qA, qAB = R["qA"].next()
                qB, qBB = R["qB"].next()
                tt("dve", qA[:, 0:64], pA[:, 0:64], eb[:, 0:64], ALU.mult, Aq[0:1] + [ebB], [qAB])
                tt("dve", qB[:, 64:128], pA[:, 64:128], eb[:, 64:128], ALU.mult, Aq[0:1] + [ebB], [qBB])
                kg, kgB = R["kg"].next()
                tt("dve", kg[:], pA[:, 128:256], enb[:], ALU.mult, Aq[1:2] + [enbB], [kgB])
                kd0, kd0B = R["kd0"].next()
                kd1, kd1B = R["kd1"].next()
                tt("dve", kd0[0:64, :], pC[0:64, 0:128], ek[0:64, :], ALU.mult, Cq[0:1] + [ekB], [kd0B])
                tt("dve", kd1[64:128, :], pC[64:128, 0:128], ek[64:128, :], ALU.mult, Cq[0:1] + [ekB], [kd1B])
                yield
                vb, vbB = R["vb"].next()
                cp("act", vb[:], pB_[:, 0:256], Bq[0:2], [vbB])
                eg, egB = R["eg"].next()
                gg, ggB = R["gg"].next()
                act(eg[:], pB_[:, 256:512], AF.Exp, Bq[2:4], [egB], scale=-1.0)
                act(eg[:], eg[:], AF.Ln, [egB], [egB], bias=1.0)
                act(eg[:], eg[:], AF.Exp, [egB], [egB], scale=-1.0)
                tt("dve", gg[:], pB_[:, 256:512], eg[:], ALU.mult, Bq[2:4] + [egB], [ggB])
                tt("dve", gg[:], gg[:], gn, ALU.mult, [ggB, gnB], [ggB])
                yield
                mm(pA[:, 384:512], kg[:], qA[:], True, False, [kgB, qAB], Aq[3:4])
                mm(pA[:, 384:512], kg[:], qB[:], False, True, [kgB, qBB], Aq[3:4])
                at, atB = R["at"].next()
                tt("dve", at[:], pA[:, 384:512], CAUSAL2, ALU.mult, Aq[3:4] + [B_cst], [atB])
                yield
                st0, st1 = states
                mm(pD[:, 0:256], at[:], vb[:], True, False, [atB, vbB], Dq[0:2])
                mm(pD[:, 0:256], qA[:], st0.bf, False, False, [qAB, st0.bfB], Dq[0:2])
                mm(pB_[:, 0:256], kd0[:], vb[:], True, True, [kd0B, vbB], Bq[0:2])
                mm(pB_[:, 256:512], kd1[:], vb[:], True, True, [kd1B, vbB], Bq[2:4])
                update(st0, dec[:, 0:1], decB, pB_[:, 0:256], Bq[0:2])
                yield
                mm(pD[:, 0:256], qB[:], st1.bf, False, True, [qBB, st1.bfB], Dq[0:2])
                update(st1, dec[:, 1:2], decB, pB_[:, 256:512], Bq[2:4])
                yield
                ss, ssB = R["ss"].next()
                act(junk[:], pD[:, 0:256], AF.Square, Dq[0:2], [junkB, ssB], accum_out=ss[:, 0:1])
                act(ss[:, 1:2], ss[:, 0:1], AF.Ln, [ssB], [ssB], scale=1.0 / 256, bias=EPS)
                act(ss[:, 2:3], ss[:, 1:2], AF.Exp, [ssB], [ssB], scale=-0.5)
                oa, oaB = R["oa"].next()
                stt("dve", oa[:], pD[:, 0:256], ss[:, 2:3], gg[:], ALU.mult, ALU.mult, Dq[0:2] + [ssB, ggB], [oaB])
                yield
                for c in range(2):
                    mm(pD[:, 256 + c * 128:256 + (c + 1) * 128], oa[:, c * 128:(c + 1) * 128], ident, True, True,
                       [oaB, B_cst], Dq[2 + c:3 + c])
                dst, dstB = dst_fn()
                cp("act", dst, pD[:, 256:512].rearrange("p (c n) -> p c n", c=2), Dq[2:4], dstB)

            Er = ring("s_E", [128, 512], BF16, 4)
            Lr = ring("s_L", [128, 512], BF16, 4)
            LSr = ring("s_LS", [128, 512], BF16, 3)
            Xr = ring("s_X", [128, 512], BF16, 3)
            Wr = ring("s_W", [128, 512], BF16, 4)
            zcnt = [0]

            def record(gen):
                rec = []
                S.add = lambda eng, fn, R=(), W=(), dsem=None, extra=(): rec.append((eng, fn, R, W, dsem, extra))
                try:
                    for _ in gen:
                        pass
                finally:
                    del S.add
                return rec

            class Replay:
                def __init__(self, rec, iters):
                    self.rec = rec
                    self.i = 0
                    self.k = max(1, -(-len(rec) // max(1, iters)))

                def step(self):
                    for _ in range(self.k):
                        if self.i < len(self.rec):
                            S.add(*self.rec[self.i])
                            self.i += 1

                def drain(self):
                    while self.i < len(self.rec):
                        S.add(*self.rec[self.i])
                        self.i += 1

            def sb_stream(blocks, evict, bg=None):
                ob, obB = bank(4)
                n = len(blocks)
                st = [None] * n

                def stage1(i):
                    if "lazy" in blocks[i]:
                        post = blocks[i].get("post")
                        blocks[i] = blocks[i]["lazy"]()
                        if post is not None:
                            blocks[i]["post"] = post
                    blk = blocks[i]
                    c0 = blk["c0"]
                    zb, zbB = bank(5 + zcnt[0] % 2)
                    zcnt[0] += 1
                    mk = blk.get("mask")
                    mfirst = blk.get("mask_first", False)
                    if mk is not None and mfirst:
                        mm(zb[:, mk[2]:mk[3]], mk[0], mk[1], True, False, [B_cst], zbB, sg=True)
                    for (kT, q, a, b, rb) in blk["z"]:
                        mm(zb[:, a:b], kT, q, not (mk is not None and mfirst), True, rb, zbB, sg=True)
                    if mk is not None and not mfirst:
                        mm(zb[:, mk[2]:mk[3]], mk[0], mk[1], False, True, [B_cst], zbB, sg=True)
                    E, EB = Er.next()
                    act(E[:, c0:512], zb[:, c0:512], AF.Exp, zbB, [EB])
                    st[i] = dict(E=E, EB=EB)

                def stage1b(i):
                    c0 = blocks[i]["c0"]
                    E, EB = st[i]["E"], st[i]["EB"]
                    L, LB = Lr.next()
                    act(L[:, c0:512], E[:, c0:512], AF.Ln, [EB], [LB], bias=1.0)
                    if i == 0:
                        ls = (L, LB, c0)
                    elif i < n - 1:
                        pLs, pLsB, pc0 = st[i - 1]["ls"]
                        Ls, LsB = LSr.next()
                        tt("dve", Ls[:, pc0:512], pLs[:, pc0:512], L[:, pc0:512], ALU.add, [pLsB, LB], [LsB])
                        if c0 < pc0:
                            cp("dve", Ls[:, c0:pc0], L[:, c0:pc0], [LB], [LsB])
                        ls = (Ls, LsB, c0)
                    else:
                        ls = None
                    st[i].update(L=L, LB=LB, ls=ls)

                def stage2(i):
                    c0 = blocks[i]["c0"]
                    d = st[i]
                    tb, tbB = bank(7)
                    mm(tb[:, c0:512], NEGTRI, d["L"][:, c0:512], True, i == 0, [d["LB"], B_cst], tbB, sg=True)
                    if i >= 1:
                        pLs, pLsB, pc0 = st[i - 1]["ls"]
                        mm(tb[:, pc0:512], NEGONES, pLs[:, pc0:512], False, True, [pLsB, B_cst], tbB, sg=True)
                    X, XB = Xr.next()
                    act(X[:, c0:512], tb[:, c0:512], AF.Exp, tbB, [XB])
                    w, wB = Wr.next()
                    tt("dve", w[:, c0:512], d["E"][:, c0:512], X[:, c0:512], ALU.mult, [d["EB"], XB], [wB])
                    d["w"] = (w, wB)

                def stage3(i):
                    w, wB = st[i]["w"]
                    for j, (v, a, b, rb) in enumerate(blocks[i]["pv"]):
                        mm(ob[:, a:b], v, w[:, a:b], i == 0 and j == 0, True, [wB] + rb, obB, sg=True)

                for i in range(n + 2):
                    if i < n:
                        stage1(i)
                        stage1b(i)
                    if 1 <= i <= n:
                        stage2(i - 1)
                    if i >= 2:
                        stage3(i - 2)
                    if i < n and blocks[i].get("post") is not None:
                        blocks[i]["post"]()
                    if bg is not None:
                        bg.step()
                evict(ob, obB)

            with ExitStack() as p2b:
                wpc = Ring([sb("wpc%d" % i, [128, KC, 512], BF16, p2b) for i in range(2)])
                wpc_ds = [dsem("d_wpc%d" % i) for i in range(2)]
                qT_s = sb("qT_s", [128, 8, NS_TOK], BF16, p2b)
                kT_s = sb("kT_s", [128, 8, NS_TOK], BF16, p2b)
                v_s = sb("v_s", [128, 2, D], BF16, p2b)
                qkB = bufs(16)
                v_sB = [bufs(2) for _ in range(2)]
                kvst = Ring([sb("kvst%d" % i, [128, 512], F32, p2b) for i in range(2)])
                kvst_ds = [dsem("d_kvst%d" % i) for i in range(2)]
                pcnt = 0
                import os
                KV = os.environ.get("KVAR", "")
                for piece in range(4):
                    if "nofm" in KV:
                        break
                    w, wB = wpc.next()
                    dma("pool", w[:], ws_sb_d[:, :, piece * 512:(piece + 1) * 512], wpc_ds[(wpc.i - 1) % 2], [], [wB])
                    for j in range(4):
                        ch = piece * 4 + j
                        pz, pzB = bank(5 + pcnt % 2)
                        pcnt += 1
                        for kc in range(KC):
                            mm(pz[:, 0:NS_TOK], w[:, kc, j * 128:(j + 1) * 128], hT_s[:, kc, :], kc == 0, kc == KC - 1,
                               [wB, hT_sB], pzB)
                        if ch < 8:
                            cp("act", qT_s[:, ch, :], pz[:, 0:NS_TOK], pzB, [qkB[ch]], scale=float(128.0 ** -0.5))
                        else:
                            cp("dve", kT_s[:, ch - 8, :], pz[:, 0:NS_TOK], pzB, [qkB[ch]])
                for piece in range(4):
                    if "notm" in KV:
                        break
                    w, wB = wpc.next()
                    dma("pool", w[:], ws_sb_d[:, :, 2048 + piece * 512:2048 + (piece + 1) * 512],
                        wpc_ds[(wpc.i - 1) % 2], [], [wB])
                    for ti in range(2):
                        pz, pzB = bank(5 + pcnt % 2)
                        pcnt += 1
                        for kc in range(KC):
                            mm(pz[:, :], hT_s[:, kc, ti * 128:(ti + 1) * 128], w[:, kc, :], kc == 0, kc == KC - 1,
                               [wB, hT_sB], pzB)
                        stg, stgB = kvst.next()
                        cp("act", stg[:], pz[:, :], pzB, [stgB])
                        dst = sbk_s_d if piece < 2 else sbv_s_d
                        if "nodma" not in KV:
                            dma("sp", dst[ti * 128:(ti + 1) * 128, (piece % 2) * 512:(piece % 2) * 512 + 512], stg[:],
                                kvst_ds[(kvst.i - 1) % 2], [stgB], [])
                        if piece >= 2 and "novs" not in KV:
                            cp("dve", v_s[:, ti, (piece - 2) * 512:(piece - 2) * 512 + 512], pz[:, :], pzB,
                               [v_sB[ti][piece - 2]])
                if stop == 1.7:
                    S.flush()
                    return nc
                sf = Ring([sb("s_sf%d" % i, [128, 256], F32, p2b) for i in range(4)])
                sf_ds = [dsem("d_sf%d" % i) for i in range(4)]
                for j in range(4):
                    if test2b:
                        break
                    S.add("pool", ag_fn(hT_p_d[j * D:(j + 1) * D, :], ag1_d[j * 4 * D:(j + 1) * 4 * D, :]),
                          R=[B_hTp[j]], W=[B_ag1[j]], dsem=cc1)

                def sample_gla_gen():
                    for hh in range(4):
                        W, wup, gnb = load_gla_w(ws_gla_d[hh], hh)
                        for ti in range(2):
                            sts = []
                            for j in range(2):
                                st = State()
                                f, fB = sf.next()
                                st.k = (sf.i - 1) % 4
                                dma("sp", f[:], state_s_d[2 * ti + j, hh], sf_ds[st.k], [], [fB])
                                st.f32, st.fB = f[:], fB
                                snapshot(st)
                                sts.append(st)
                            yield from gla_tile(W, hT_s[:, :, ti * 128:(ti + 1) * 128], [hT_sB], wup, gnb, sts,
                                                lambda hh=hh, ti=ti: (
                                                    oaT_s[:, 2 * hh:2 * hh + 2, ti * 128:(ti + 1) * 128],
                                                    [oaT_sB[2 * hh + ti]]))
                            for j in range(2):
                                dma("sp", st_s_d[2 * ti + j, hh], sts[j].f32, sf_ds[sts[j].k], [sts[j].fB], [])
                            yield
                bg_s = Replay(record(sample_gla_gen()), 4 * 35)
                kcr = Ring([sb("kc%d" % i, [128, 8, 1024], BF16, p2b) for i in range(2)])
                vcr = Ring([sb("vc%d" % i, [128, 8, D], BF16, p2b) for i in range(2)])
                kc_ds = [dsem("d_kc%d" % i) for i in range(2)]
                vc_ds = [dsem("d_vc%d" % i) for i in range(2)]
                def load_kv(s, gk):
                    kt, ktB = kcr.next()
                    for hq in range(4):
                        dma("pool", kt[:, 2 * hq:2 * hq + 2, :],
                            kcT_d[s, 2 * hq:2 * hq + 2].rearrange("h d t -> d h t")[:, :, gk * 1024:(gk + 1) * 1024],
                            "auto", [], [ktB])
                    vt, vtB = vcr.next()
                    for hq in range(4):
                        dma("pool", vt[:, 2 * hq:2 * hq + 2, :], vc_d[s, :, gk * 8 + 2 * hq:gk * 8 + 2 * hq + 2, :],
                            "auto", [], [vtB])
                    return kt, ktB, vt, vtB

                order = [(s, gk) for s in range(4) for gk in range(3, -1, -1)]
                loaded = {order[0]: load_kv(*order[0])}
                for s in range(4):
                    ti, par = s // 2, s % 2
                    qcols = slice(s * 64, (s + 1) * 64)
                    blocks = [dict(
                        c0=0, mask=(ident, cst2[:, par * 512:(par + 1) * 512], 0, 512), mask_first=True,
                        z=[(kT_s[:, h, ti * 128:(ti + 1) * 128], qT_s[:, h, qcols], h * 64, h * 64 + 64,
                            [qkB[h], qkB[8 + h]]) for h in range(8)],
                        pv=[(v_s[:, ti, h * 128:(h + 1) * 128], h * 64, h * 64 + 64, v_sB[ti]) for h in range(8)])]
                    for gk in range(3, -1, -1):
                        for kb in range(7, -1, -1):
                            def mk(s=s, gk=gk, kb=kb, qcols=qcols):
                                kt, ktB, vt, vtB = loaded[(s, gk)]
                                return dict(
                                    c0=0,
                                    z=[(kt[:, h, kb * 128:(kb + 1) * 128], qT_s[:, h, qcols], h * 64, h * 64 + 64,
                                        [ktB, qkB[h]]) for h in range(8)],
                                    pv=[(vt[:, kb, h * 128:(h + 1) * 128], h * 64, h * 64 + 64, [vtB])
                                        for h in range(8)])
                            blk = dict(lazy=mk)
                            if kb == 6:
                                nxt = order.index((s, gk)) + 1
                                if nxt < len(order):
                                    blk["post"] = (lambda nxt=nxt: loaded.__setitem__(order[nxt], load_kv(*order[nxt])))
                            blocks.append(blk)

                    def evict(ob, obB, s=s):
                        cp("act", obT_s[:, :, s * 64:(s + 1) * 64], ob.rearrange("p (h q) -> p h q", h=8), obB,
                           [obT_sB[s]])
                    sb_stream(blocks, evict, bg_s)
                bg_s.drain()
                S.flush()
            if test2b:
                dbg_oa = dout("dbg_oa", [128, KC, NS_TOK], BF16)
                dbg_ob = dout("dbg_ob", [128, KC, NS_TOK], BF16)
                dma("sp", dbg_oa, oaT_s[:], "auto", oaT_sB, [])
                dma("sp", dbg_ob, obT_s[:], "auto", obT_sB, [])
                S.flush()
            if stop == 2 or test2b:
                return nc

            with ExitStack() as p2a:
                wsb = sb("wsb", [128, KC, 1024], BF16, p2a)
                B_wsb = Buf()
                for i in range(2):
                    dma("pool", wsb[:, :, i * 512:(i + 1) * 512], wp_sb_d[:, :, i * 512:(i + 1) * 512], ds_c, [],
                        [B_wsb])
                W, wup, gnb = load_gla_w(wp_gla_d, 4)
                kT_all = sb("kT_all", [128, 2, SEQ], BF16, p2a)
                v_all = sb("v_all", [128, 64, 256], BF16, p2a)
                kvB = bufs(16)
                hTb = Ring([sb("hTb%d" % i, [128, KC, 512], BF16, p2a) for i in range(2)])
                hTb_ds = [dsem("d_hTb%d" % i) for i in range(2)]
                qTb = Ring([sb("qTb%d" % i, [128, 2, 512], BF16, p2a) for i in range(2)])
                kvst = Ring([sb("kvstp%d" % i, [128, 512], F32, p2a) for i in range(2)])
                kvst_ds = [dsem("d_kvstp%d" % i) for i in range(2)]
                oaTb = Ring([sb("oaTb%d" % i, [128, 2, 512], BF16, p2a) for i in range(2)])
                oaTb_ds = [dsem("d_oaTb%d" % i) for i in range(2)]
                obst = Ring([sb("obst%d" % i, [128, 512], BF16, p2a) for i in range(2)])
                obst_ds = [dsem("d_obst%d" % i) for i in range(2)]
                B_ag2s = bufs(16)
                stp = State()
                stp_t = sb("stp_f32", [128, 256], F32, p2a)
                stp.f32, stp.fB = stp_t[:], Buf()
                memset(stp_t[:], 0.0, [stp.fB])
                snapshot(stp)
                ag1v = ag1_d.rearrange("(j r kc p) t -> j r p kc t", j=4, r=4, p=128)
                ag2sv = ag2s_d.rearrange("(k c p) t -> k p c t", k=8, p=128)
                cc2 = dsem("cc2", 1)
                B_ag2 = bufs(8)
                pcnt = 0
                for bi in range(16):
                    hb, hbB = hTb.next()
                    for half in range(2):
                        dma("sp", hb[:, half * 4:half * 4 + 4, :],
                            ag1v[bi % 4, bi // 4, :, half * 4:half * 4 + 4, :],
                            hTb_ds[(hTb.i - 1) % 2], [B_ag1[bi % 4]], [hbB])
                    tok = slice(bi * 512, (bi + 1) * 512)
                    qt, qtB = qTb.next()
                    for j in range(4):
                        pz, pzB = bank(5 + pcnt % 2)
                        pcnt += 1
                        for kc in range(KC):
                            mm(pz[:, :], wsb[:, kc, j * 128:(j + 1) * 128], hb[:, kc, :], kc == 0, kc == KC - 1,
                               [B_wsb, hbB], pzB)
                        if j < 2:
                            cp("act", qt[:, j, :], pz[:, :], pzB, [qtB], scale=float(128.0 ** -0.5))
                        else:
                            cp("dve", kT_all[:, j - 2, tok], pz[:, :], pzB, [kvB[bi]])
                    for t in range(4):
                        pz, pzB = bank(5 + pcnt % 2)
                        pcnt += 1
                        for kc in range(KC):
                            mm(pz[:, :], hb[:, kc, t * 128:(t + 1) * 128], wsb[:, kc, 512:1024], kc == 0, kc == KC - 1,
                               [B_wsb, hbB], pzB)
                        stg, stgB = kvst.next()
                        k_ = (kvst.i - 1) % 2
                        cp("act", stg[:], pz[:, :], pzB, [stgB])
                        rows = slice(bi * 512 + t * 128, bi * 512 + (t + 1) * 128)
                        dma("sp", sbk_p_d[rows, :], stg[:, 0:256], kvst_ds[k_], [stgB], [])
                        dma("sp", sbv_p_d[rows, :], stg[:, 256:512], kvst_ds[k_], [stgB], [])
                        cp("dve", v_all[:, bi * 4 + t, :], pz[:, 256:512], pzB, [kvB[bi]])
                    tk2 = slice((bi % 2) * 512, (bi % 2) * 512 + 512)

                    def gla_block_gen(bi=bi, hb=hb, hbB=hbB, tk2=tk2):
                        ot, otB = oaTb.next()
                        for t in range(4):
                            yield from gla_tile(W, hb[:, :, t * 128:(t + 1) * 128], [hbB], wup, gnb, [stp, stp],
                                                lambda t=t: (ot[:, :, t * 128:(t + 1) * 128], [otB]))
                        dma("sp", ag2sv[bi // 2, :, 0:2, tk2], ot[:], "auto", [otB], [B_ag2s[bi]])
                    bg_p = Replay(record(gla_block_gen()), 2 * (4 * bi + 6))
                    for hh in range(2):
                        blocks = []
                        for j in range(3, -1, -1):
                            kb = 4 * bi + j
                            c0 = 128 * j
                            blocks.append(dict(
                                c0=c0, mask=(ident, MASKB, c0, c0 + 128), mask_first=False,
                                z=[(kT_all[:, hh, kb * 128:(kb + 1) * 128], qt[:, hh, c0:512], c0, 512,
                                    [kvB[bi], qtB])],
                                pv=[(v_all[:, kb, hh * 128:(hh + 1) * 128], c0, 512, [kvB[bi]])]))
                        for kb in range(4 * bi - 1, -1, -1):
                            blocks.append(dict(
                                c0=0,
                                z=[(kT_all[:, hh, kb * 128:(kb + 1) * 128], qt[:, hh, :], 0, 512,
                                    [kvB[kb // 4], qtB])],
                                pv=[(v_all[:, kb, hh * 128:(hh + 1) * 128], 0, 512, [kvB[kb // 4]])]))

                        def evict(ob, obB, hh=hh, tk2=tk2, bi=bi):
                            o, oB = obst.next()
                            cp("act", o[:], ob[:, :], obB, [oB])
                            dma("sp", ag2sv[bi // 2, :, 2 + hh, tk2], o[:], obst_ds[(obst.i - 1) % 2], [oB],
                                [B_ag2s[bi]])
                        sb_stream(blocks, evict, bg_p)
                    bg_p.drain()
                    if bi % 2 == 1:
                        k = bi // 2
                        S.add("pool", ag_fn(ag2s_d[k * 512:(k + 1) * 512, :], ag2_d[k * 2048:(k + 1) * 2048, :]),
                              R=[B_ag2s[bi - 1], B_ag2s[bi]], W=[B_ag2[k]], dsem=cc2)
                dma("sp", st_p_d, stp.f32, ds_c, [stp.fB], [])
                S.flush()

        if stop == 3:
            return nc

        B_own = [bufs(4) for _ in range(2)]
        ds_own = dsem("d_own")

        pid_cache = {}

        def own_fn(kk, r4):
            def fn(e):
                if "g" not in pid_cache:
                    pid_cache["g"] = e.partition_id() % 4
                g_ = pid_cache["g"]
                return e.dma_start(out=own_d[r4 * 512:(r4 + 1) * 512, kk * 1024:(kk + 1) * 1024],
                                   in_=ag2_d[bass.ds((g_ * 2 + kk) * 2048 + r4 * 512, 512), :])
            return fn
        for kk in range(2):
            for r4 in range(4):
                S.add("pool", own_fn(kk, r4), R=B_ag2, W=[B_own[kk][r4]], dsem=ds_own)

        G3 = 512
        groups3 = [(o, min(G3, NTOK - o)) for o in range(0, NTOK, G3)]
        B_h2 = bufs(len(groups3))
        with ExitStack() as p3:
            lnp, lnpB = load_lnp(p3, 1)
            wo = sb("wo_sb", [128, KC, D], BF16, p3)
            B_wo = Buf()
            for i in range(2):
                dma("pool", wo[:, :, i * 512:(i + 1) * 512], wo_d[:, :, i * 512:(i + 1) * 512], ds_c, [], [B_wo])
            wmr = Ring([sb("wm%d" % i, [128, KC, 512], BF16, p3) for i in range(2)])
            wm_ds = [dsem("d_wm%d" % i) for i in range(2)]
            hin = Ring([sb("m_h%d" % i, [128, KC, G3], BF16, p3) for i in range(2)])
            oain = Ring([sb("m_oa%d" % i, [128, KC, G3], BF16, p3) for i in range(2)])
            obin = Ring([sb("m_ob%d" % i, [128, KC, G3], BF16, p3) for i in range(2)])
            in_ds = [dsem("d_min%d" % i) for i in range(2)]
            sgr = Ring([sb("m_sg%d" % i, [128, G3], F32, p3) for i in range(4)])
            t12 = Ring([sb("m_t%d" % i, [128, G3], F32, p3) for i in range(2)])
            mT = sb("m_mT", [128, KC, G3], BF16, p3)
            mTB = bufs(KC)
            rr = Ring([sb("m_r%d" % i, [128, D], F32, p3) for i in range(3)])
            r_ds = [dsem("d_mr%d" % i) for i in range(3)]
            hbf = Ring([sb("m_hbf%d" % i, [128, D], BF16, p3) for i in range(2)])
            h2Tg = sb("m_h2Tg", [128, KC, G3], BF16, p3)
            h2TgB = [bufs(2) for _ in range(G3 // 128)]
            ds_h2T = dsem("d_h2T")
            h2Tv = h2T_d.rearrange("(kc p) t -> p kc t", p=128)
            for gi, (t0, n) in enumerate(groups3):
                if t0 < NP_TOK:
                    hi, hiB = hin.next()
                    oi, oiB = oain.next()
                    qi, qiB = obin.next()
                    k_ = (hin.i - 1) % 2
                    dma("sp", hi[:, :, 0:n], hTv[t0 // 512], in_ds[k_], B_hTp, [hiB])
                    for r4 in range(4):
                        for which, dstt in ((0, oi), (1, qi)):
                            r0 = r4 * 512 + which * 256
                            dma("sp", dstt[:, 2 * r4:2 * r4 + 2, 0:n],
                                own_d[r0:r0 + 256, t0:t0 + n].rearrange("(c p) t -> p c t", p=128), in_ds[k_],
                                [B_own[t0 // 1024][r4]], [oiB if which == 0 else qiB])
                    hT_g, hT_gB = hi, [hiB]
                    oa_g, oa_gB = oi, [oiB]
                    ob_g, ob_gB = qi, [qiB]
                    off = 0
                else:
                    hT_g, hT_gB = hT_s, [hT_sB]
                    oa_g, oa_gB = oaT_s, oaT_sB
                    ob_g, ob_gB = obT_s, obT_sB
                    off = t0 - NP_TOK
                for fc in range(KC):
                    w, wB = wmr.next()
                    dma("pool", w[:], wmix_d[fc], wm_ds[(wmr.i - 1) % 2], [], [wB])
                    srcs = ((hT_g, hT_gB), (hT_g, hT_gB), (oa_g, oa_gB), (ob_g, ob_gB))
                    for j in range(4):
                        pz, pzB = bank(j)
                        src, srcB = srcs[j]
                        for kc in range(KC):
                            mm(pz[:, 0:n], w[:, kc, j * 128:(j + 1) * 128], src[:, kc, off:off + n], kc == 0,
                               kc == KC - 1, [wB] + srcB, pzB)
                    sga, sgaB = sgr.next()
                    sgb, sgbB = sgr.next()
                    act(sga[:, 0:n], bank(0)[0][:, 0:n], AF.Sigmoid, bank(0)[1], [sgaB])
                    act(sgb[:, 0:n], bank(1)[0][:, 0:n], AF.Sigmoid, bank(1)[1], [sgbB])
                    t1, t1B = t12.next()
                    t2, t2B = t12.next()
                    tt("dve", t1[:, 0:n], sga[:, 0:n], bank(2)[0][:, 0:n], ALU.mult, [sgaB] + bank(2)[1], [t1B])
                    tt("dve", t2[:, 0:n], sgb[:, 0:n], bank(3)[0][:, 0:n], ALU.mult, [sgbB] + bank(3)[1], [t2B])
                    tt("dve", mT[:, fc, 0:n], t1[:, 0:n], t2[:, 0:n], ALU.add, [t1B, t2B], [mTB[fc]])
                for t in range(n // 128):
                    po = PS[2]
                    for fc in range(KC):
                        for hf in range(2):
                            mm(po[:, hf * 512:(hf + 1) * 512], mT[:, fc, t * 128:(t + 1) * 128],
                               wo[:, fc, hf * 512:(hf + 1) * 512], fc == 0, fc == KC - 1, [mTB[fc], B_wo], PSQ[4 + hf])
                    r, rB = rr.next()
                    k_ = (rr.i - 1) % 3
                    rows = slice(t0 + t * 128, t0 + (t + 1) * 128)
                    dma("sp", r[:], h_tm_d[rows, :], r_ds[k_], [], [rB])
                    act(r[:], r[:], AF.Copy, [rB], [rB], scale=ALPHA)
                    tt("dve", r[:], r[:], po[:], ALU.add, [rB] + PSQ[4] + PSQ[5], [rB])
                    layer_norm(r[:], rB, lnp, lnpB)
                    dma("sp", h2_tm_d[rows, :], r[:], r_ds[k_], [rB], [B_h2[gi]])
                    hb, hbB = hbf.next()
                    cp("act", hb[:], r[:], [rB], [hbB])
                    transpose_tile(hb, hbB, lambda half, t=t: (
                        h2Tg[:, half * 4:half * 4 + 4, t * 128:(t + 1) * 128], h2TgB[t][half]), t)
                dma("sp", h2Tv[:, :, t0:t0 + n], h2Tg[:, :, 0:n], ds_h2T,
                    [b for bb in h2TgB[:n // 128] for b in bb], [B_h2[gi]])
            S.flush()

        if stop == 4:
            return nc
        ffn_phase("f2", h2T_d, h2_tm_d, f2_win_d, f2_wout_d, 2, y_d, None)
        S.flush()
    return nc


def _consts():
    c = np.zeros((128, 8 * 128), np.float32)
    s = np.arange(128)[:, None]
    t = np.arange(128)[None, :]
    same = (s // 64) == (t // 64)
    c[:, 0:128] = np.eye(128, dtype=np.float32)
    c[:, 128:256] = np.where(s >= t, -1.0, 0.0)
    c[:, 256:384] = np.where(s < t, -1.0, 0.0)
    c[:, 384:512] = np.where(s < t, 0.0, NEG)
    c[:, 512:640] = np.where((s <= t) & same, -1.0 / 16, 0.0)
    c[:, 640:768] = np.where((s > t) & same, -1.0 / 16, 0.0)
    c[:, 768:896] = np.where((s <= t) & same, 1.0, 0.0)
    c[:, 896:1024] = -1.0
    c2 = np.full((128, 1024), NEG, np.float32)
    sk = np.arange(128)[:, None]
    tq = np.arange(64)[None, :]
    for par in range(2):
        m = np.where(((sk // 64) == par) & ((sk % 64) < tq), 0.0, NEG).astype(np.float32)
        for h in range(8):
            c2[:, par * 512 + h * 64:par * 512 + (h + 1) * 64] = m
    return c, c2


def _r(w):
    return np.ascontiguousarray(w.reshape(KC, 128, w.shape[1]).transpose(1, 0, 2))


def _ffn_layout(w_in, w_out):
    wi = w_in.reshape(KC, 128, 2, NFC, 128)
    wi = np.ascontiguousarray(wi.transpose(3, 1, 0, 2, 4)).reshape(NFC, 128, KC * 256)
    wo = np.ascontiguousarray(w_out.reshape(NFC, 128, D).transpose(1, 0, 2)).reshape(128, NFC * D)
    return wi, wo


O_GQ, O_GK, O_GV, O_GR, O_GLR, O_SQ, O_SK, O_SV, O_GA, O_GB = 0, 512, 1024, 2048, 3072, 3088, 4112, 5136, 6160, 7184


def make_in_maps(inp):
    f1_win, f1_wout = _ffn_layout(inp["ffn1_w_in"][0], inp["ffn1_w_out"][0])
    f2_win, f2_wout = _ffn_layout(inp["ffn2_w_in"][0], inp["ffn2_w_out"][0])
    lnp = np.ascontiguousarray(np.stack([inp["ln1_g"][0], inp["ln1_b"][0], inp["ln2_g"][0], inp["ln2_b"][0],
                                         inp["ln3_g"][0], inp["ln3_b"][0]]).astype(np.float32))
    consts, consts2 = _consts()
    w = inp["w_in"][0]

    def c(off, n):
        return w[:, off:off + n]

    def gla_cols(h):
        return np.concatenate([c(O_GQ + h * 128, 128), c(O_GK + h * 128, 128), c(O_GV + h * 256, 256),
                               c(O_GR + h * 256, 256), c(O_GK + h * 128, 128)], axis=1)

    ws_gla = np.stack([_r(gla_cols(h)) for h in range(4)])
    ws_sb = _r(np.concatenate([c(O_SQ, 1024), c(O_SK, 1024), c(O_SK, 1024), c(O_SV, 1024)], axis=1))
    wglr = _r(c(O_GLR, 16))
    wupa = np.zeros((4, 128, 128), np.float32)
    for h in range(4):
        wupa[h, 0:16] = inp["w_gla_gate_up"][0][:, h * 128:(h + 1) * 128]
        wupa[h, 32] = inp["b_gla_gate"][0][h * 128:(h + 1) * 128]
    gn = inp["g_gla_norm"][0]
    wmix = np.stack([_r(np.concatenate([c(O_GA + fc * 128, 128), c(O_GB + fc * 128, 128),
                                        inp["w_gla_o"][0][:, fc * 128:(fc + 1) * 128],
                                        inp["w_sb_o"][0][:, fc * 128:(fc + 1) * 128]], axis=1)) for fc in range(8)])
    wo = _r(inp["w_out"][0])
    maps = []
    for core in range(NCORES):
        b, g = core // 4, core % 4
        xtm = np.concatenate([inp["x_prompt"][b, g * NP_TOK:(g + 1) * NP_TOK],
                              inp["x_sample"][4 * core:4 * core + 4].reshape(NS_TOK, D)], axis=0)
        xtm = np.ascontiguousarray(xtm)
        wp_sb = _r(np.concatenate([c(O_SQ + 2 * g * 128, 256), c(O_SK + 2 * g * 128, 256),
                                   c(O_SK + 2 * g * 128, 256), c(O_SV + 2 * g * 128, 256)], axis=1))
        sl = slice(4 * core, 4 * core + 4)
        kcT = np.ascontiguousarray(inp["cache_sb_k"][0, sl].transpose(0, 2, 3, 1))
        vc = np.ascontiguousarray(inp["cache_sb_v"][0, sl].reshape(4, 32, 128, D).transpose(0, 2, 1, 3))
        maps.append(dict(
            xT=np.ascontiguousarray(xtm.T), xtm=xtm, f1_win=f1_win, f1_wout=f1_wout, f2_win=f2_win, f2_wout=f2_wout,
            lnp=lnp, consts=consts, consts2=consts2, wp_gla=ws_gla[g], wp_sb=wp_sb, ws_gla=ws_gla, ws_sb=ws_sb,
            wglr=wglr, wup=np.ascontiguousarray(np.concatenate([wupa, wupa[g:g + 1]])),
            gnorm=np.ascontiguousarray(np.concatenate([gn, gn[g:g + 1]])),
            state_s=np.ascontiguousarray(inp["state_gla"][0, sl]), kcT=kcT, vc=vc, wmix=wmix, wo=wo))
    return maps


_NC_CACHE = {}
_STOP = None


def kernel(**inputs):
    inp = {k: np.asarray(v) for k, v in inputs.items()}
    if "nc" not in _NC_CACHE:
        _NC_CACHE["nc"] = build_nc(stop=_STOP)
    nc = _NC_CACHE["nc"]
    maps = make_in_maps(inp)
    res = run_bass_kernel_spmd(nc, maps, core_ids=list(range(NCORES)))
    R = res.results
    y_p = np.zeros((2, SEQ, D), np.float32)
    y_s = np.zeros((32, 64, D), np.float32)
    st_p = np.zeros((1, 2, 4, 128, 256), np.float32)
    k_p = np.zeros((1, 2, SEQ, 8, 128), np.float32)
    v_p = np.zeros((1, 2, SEQ, 8, 128), np.float32)
    st_s = np.zeros((1, 32, 4, 128, 256), np.float32)
    k_s = np.zeros((1, 32, 64, 8, 128), np.float32)
    v_s = np.zeros((1, 32, 64, 8, 128), np.float32)
    for core in range(NCORES):
        b, g = core // 4, core % 4
        r = R[core]
        y = np.asarray(r["y"])
        y_p[b, g * NP_TOK:(g + 1) * NP_TOK] = y[:NP_TOK]
        y_s[4 * core:4 * core + 4] = y[NP_TOK:].reshape(4, 64, D)
        st_p[0, b, g] = np.asarray(r["st_p"])
        k_p[0, b, :, 2 * g:2 * g + 2, :] = np.asarray(r["sbk_p"]).reshape(SEQ, 2, 128)
        v_p[0, b, :, 2 * g:2 * g + 2, :] = np.asarray(r["sbv_p"]).reshape(SEQ, 2, 128)
        st_s[0, 4 * core:4 * core + 4] = np.asarray(r["st_s"])
        k_s[0, 4 * core:4 * core + 4] = np.asarray(r["sbk_s"]).reshape(4, 64, 8, 128)
        v_s[0, 4 * core:4 * core + 4] = np.asarray(r["sbv_s"]).reshape(4, 64, 8, 128)
    return (y_p, y_s, st_p, k_p, v_p, st_s, k_s, v_s)
```

```python
import numpy as np
from contextlib import ExitStack
import concourse.bass as bass
import concourse.mybir as mybir
from concourse.bass_utils import run_bass_kernel_spmd

F32 = mybir.dt.float32
BF16 = mybir.dt.bfloat16
AF = mybir.ActivationFunctionType
ALU = mybir.AluOpType

NCORES = 8
D = 1024
KC = 8
SEQ = 8192
NP_TOK = 2048
NS_TOK = 256
NTOK = NP_TOK + NS_TOK
PAST = 4096
DFF = 2816
NFC = 22
ALPHA = 2.0 ** 0.25
EPS = 1e-5
NEG = -30000.0
G = 768
NG = NTOK // G


class Buf:
    __slots__ = ("w", "r", "excl")

    def __init__(self, excl=False):
        self.w = None
        self.r = []
        self.excl = excl


def bufs(n):
    return [Buf() for _ in range(n)]


class DSem:
    def __init__(self, sem, inc=16):
        self.sem = sem
        self.inc = inc
        self.val = 0
        self.last = None


class Op:
    __slots__ = ("eng", "fn", "deps", "dsem", "dval", "needed", "count")


class Sched:
    ENG = ("pe", "act", "dve", "pool", "sp")

    def __init__(self, nc, esems):
        self.nc = nc
        self.esem = esems
        self.ops = []
        self.flushed = 0
        self.cnt = {e: 0 for e in self.ENG}
        self.last = {e: None for e in self.ENG}
        self.waited = {e: {} for e in self.ENG}
        self.dsems = []
        self.dq = {}
        self.dqi = {}

    def dsem(self, sem, inc=16):
        d = DSem(sem, inc)
        self.dsems.append(d)
        return d

    def add(self, eng, fn, R=(), W=(), dsem=None, extra=()):
        op = Op()
        deps = set(extra)
        if dsem is not None:
            key = "cc" if dsem == "cc" else eng
            i = self.dqi.get(key, 0)
            self.dqi[key] = i + 1
            dsem = self.dq[key][i % len(self.dq[key])]
            if dsem.last is not None:
                deps.add(dsem.last)
        op.eng, op.fn, op.dsem, op.needed, op.count, op.dval = eng, fn, dsem, False, 0, 0
        W = list(W) + [b for b in R if b.excl]
        R = [b for b in R if not b.excl]
        for b in R:
            if b.w is not None:
                deps.add(b.w)
        for b in W:
            if b.w is not None:
                deps.add(b.w)
            for r in b.r:
                deps.add(r)
        deps.discard(op)
        op.deps = deps
        for b in R:
            b.r.append(op)
        for b in W:
            b.w = op
            b.r = []
        if dsem is not None:
            dsem.val += dsem.inc
            op.dval = dsem.val
            dsem.last = op
        self.ops.append(op)
        self.last[eng] = op
        return op

    def barrier(self):
        deps = [o for o in self.last.values() if o is not None]
        deps += [d.last for d in self.dsems if d.last is not None]
        for e in self.ENG:
            self.add(e, None, extra=deps)

    def flush(self):
        self.barrier()
        ops = self.ops[self.flushed:]
        self.flushed = len(self.ops)
        for op in ops:
            for d in op.deps:
                if d.dsem is None and not (d.eng == "pe" and op.eng == "pe"):
                    d.needed = True
        for op in ops:
            if op.dsem is None and op.needed:
                self.cnt[op.eng] += 1
                op.count = self.cnt[op.eng]
        per = {e: [o for o in ops if o.eng == e] for e in self.ENG}

        def emit(e, name):
            waited = self.waited[name]
            for op in per[name]:
                w = {}
                for d in op.deps:
                    if d.dsem is not None:
                        key, so, val = ("d", id(d.dsem)), d.dsem.sem, d.dval
                    else:
                        if d.eng == "pe" and name == "pe":
                            continue
                        key, so, val = d.eng, self.esem[d.eng], d.count
                    if val > w.get(key, (None, 0))[1]:
                        w[key] = (so, val)
                for key, (so, val) in w.items():
                    if waited.get(key, 0) >= val:
                        continue
                    e.wait_ge(so, val)
                    waited[key] = val
                if op.fn is None:
                    continue
                ins = op.fn(e)
                if op.dsem is not None:
                    ins.then_inc(op.dsem.sem, op.dsem.inc)
                elif op.needed:
                    ins.then_inc(self.esem[name], 1)

        with self.nc.Block() as block:
            @block.tensor
            def _(e):
                emit(e, "pe")

            @block.scalar
            def _(e):
                emit(e, "act")

            @block.vector
            def _(e):
                emit(e, "dve")

            @block.gpsimd
            def _(e):
                emit(e, "pool")

            @block.sync
            def _(e):
                emit(e, "sp")


class Ring:
    def __init__(self, tiles):
        self.t = tiles
        self.b = bufs(len(tiles))
        self.i = 0

    def next(self):
        k = self.i % len(self.t)
        self.i += 1
        return self.t[k], self.b[k]


def build_nc(dbg=False, stop=None, test2b=False):
    nc = bass.Bass("TRN2", target_bir_lowering=False)

    T2B = ("consts", "consts2", "ws_gla", "ws_sb", "wglr", "wup", "gnorm", "state_s", "kcT", "vc")

    def din(name, shape, dt=F32):
        if test2b and name not in T2B:
            return None
        return nc.dram_tensor(name, list(shape), dt, kind="ExternalInput").ap()

    def dout(name, shape, dt=F32):
        return nc.dram_tensor(name, list(shape), dt, kind="ExternalOutput").ap()

    def dint(name, shape, dt):
        return nc.dram_tensor(name, list(shape), dt, kind="Internal").ap()

    xT_d = din("xT", [D, NTOK])
    xtm_d = din("xtm", [NTOK, D])
    f1_win_d = din("f1_win", [NFC, 128, KC * 256])
    f1_wout_d = din("f1_wout", [128, NFC * D])
    f2_win_d = din("f2_win", [NFC, 128, KC * 256])
    f2_wout_d = din("f2_wout", [128, NFC * D])
    lnp_d = din("lnp", [6, D])
    consts_d = din("consts", [128, 8 * 128])
    consts2_d = din("consts2", [128, 1024])
    wp_gla_d = din("wp_gla", [128, KC, 896])
    wp_sb_d = din("wp_sb", [128, KC, 1024])
    ws_gla_d = din("ws_gla", [4, 128, KC, 896])
    ws_sb_d = din("ws_sb", [128, KC, 4096])
    wglr_d = din("wglr", [128, KC, 16])
    wup_d = din("wup", [5, 128, 128])
    gnorm_d = din("gnorm", [5, 256])
    state_s_d = din("state_s", [4, 4, 128, 256])
    kcT_d = din("kcT", [4, 8, 128, PAST])
    vc_d = din("vc", [4, 128, 32, D])
    wmix_d = din("wmix", [8, 128, KC, 512])
    wo_d = din("wo", [128, KC, D])
    y_d = dout("y", [NTOK, D])
    st_p_d = dout("st_p", [128, 256])
    sbk_p_d = dout("sbk_p", [SEQ, 256])
    sbv_p_d = dout("sbv_p", [SEQ, 256])
    st_s_d = dout("st_s", [4, 4, 128, 256])
    sbk_s_d = dout("sbk_s", [NS_TOK, D])
    sbv_s_d = dout("sbv_s", [NS_TOK, D])
    h_tm_d = dint("h_tm", [NTOK, D], F32)
    hT_p_d = dint("hT_p", [4 * D, 512], BF16)
    ag1_d = dint("ag1", [4 * 4 * D, 512], BF16)
    ag2s_d = dint("ag2s", [8 * 512, 1024], BF16)
    ag2_d = dint("ag2", [8 * 2048, 1024], BF16)
    own_d = dint("ag2own", [4 * 512, NP_TOK], BF16)
    h2_tm_d = dint("h2_tm", [NTOK, D], F32)
    h2T_d = dint("h2T", [D, NTOK], BF16)
    GROUPS = [[0, 1, 2, 3], [4, 5, 6, 7]]

    es = ExitStack()
    with es:
        def sem(name):
            return es.enter_context(nc.semaphore(name))

        esems = {e: sem("s_" + e) for e in Sched.ENG}
        S = Sched(nc, esems)
        for q_, n_, inc_ in (("sp", 12, 16), ("pool", 12, 16), ("cc", 4, 1)):
            S.dq[q_] = [S.dsem(sem("dq_%s%d" % (q_, i)), inc_) for i in range(n_)]

        def sb(name, shape, dt, stack=es):
            return stack.enter_context(nc.sbuf_tensor(name, list(shape), dt))

        PS = [es.enter_context(nc.psum_tensor("ps%d" % i, [128, 1024], F32)) for i in range(4)]
        PSQ = [[Buf(excl=True)] * 4 for _ in range(8)]

        def bank(i):
            return PS[i // 2][:, (i % 2) * 512:(i % 2) * 512 + 512], PSQ[i]

        def mm(out, lhsT, rhs, start, stop, R, W, sg=False):
            if sg:
                return S.add("pe", lambda e: e.matmul(out, lhsT, rhs, start=start, stop=stop, skip_group_check=True),
                             R=R, W=W)
            return S.add("pe", lambda e: e.matmul(out, lhsT, rhs, start=start, stop=stop), R=R, W=W)

        def act(out, in_, func, R, W, bias=0.0, scale=1.0, accum_out=None):
            if accum_out is None:
                return S.add("act", lambda e: e.activation(out, in_, func, bias=bias, scale=scale), R=R, W=W)
            return S.add("act", lambda e: e.activation(out, in_, func, bias=bias, scale=scale,
                                                       accum_out=accum_out), R=R, W=W)

        def tt(eng, out, in0, in1, op, R, W):
            return S.add(eng, lambda e: e.tensor_tensor(out, in0, in1, op), R=R, W=W)

        def ts(eng, out, in0, s1, s2, op0, op1, R, W):
            if op1 is None:
                return S.add(eng, lambda e: e.tensor_scalar(out, in0, s1, None, op0), R=R, W=W)
            return S.add(eng, lambda e: e.tensor_scalar(out, in0, s1, s2, op0, op1), R=R, W=W)

        def stt(eng, out, in0, scalar, in1, op0, op1, R, W):
            return S.add(eng, lambda e: e.scalar_tensor_tensor(out, in0, scalar, in1, op0, op1), R=R, W=W)

        def recip(out, in_, R, W):
            return S.add("dve", lambda e: e.reciprocal(out, in_), R=R, W=W)

        def cp(eng, out, in_, R, W, scale=None):
            if eng == "act":
                if scale is None:
                    return S.add("act", lambda e: e.activation(out, in_, AF.Copy), R=R, W=W)
                return S.add("act", lambda e: e.activation(out, in_, AF.Copy, scale=scale), R=R, W=W)
            return S.add(eng, lambda e: e.tensor_copy(out, in_), R=R, W=W)

        def memset(out, val, W):
            return S.add("dve", lambda e: e.memset(out, val), R=[], W=W)

        def dma(q, out, in_, ds, R, W):
            return S.add(q, lambda e: e.dma_start(out=out, in_=in_), R=R, W=W, dsem=ds)

        def dsem(name, inc=16):
            return "cc" if inc == 1 else "auto"

        cst = sb("cst", [128, 8 * 128], BF16)
        cstf = sb("cstf", [128, 128], F32)
        B_cst = Buf()
        ds_c = dsem("d_c")
        dma("pool", cst[:], consts_d, ds_c, [], [B_cst])
        dma("sp", cstf[:], consts_d[:, 6 * 128:7 * 128], ds_c, [], [B_cst])
        ident = cst[:, 0:128]
        NEGTRI = cst[:, 128:256]
        NEGTRIC = cst[:, 256:384]
        MASKB = cst[:, 384:512]
        TRI_INCL = cst[:, 512:640]
        TRI_REV = cst[:, 640:768]
        NEGONES = cst[:, 896:1024]
        CAUSAL2 = cstf[:, 0:128]

        hT_s = sb("hT_s", [128, KC, NS_TOK], BF16)
        hT_sB = Buf()
        oaT_s = sb("oaT_s", [128, KC, NS_TOK], BF16)
        oaT_sB = bufs(8)
        obT_s = sb("obT_s", [128, KC, NS_TOK], BF16)
        obT_sB = bufs(4)
        lnw = dict(
            st=Ring([sb("ln_st%d" % i, [128, 12], F32) for i in range(2)]),
            mv=Ring([sb("ln_mv%d" % i, [128, 4], F32) for i in range(2)]),
        )

        def load_lnp(stack, li):
            t = sb("lnp_sb%d" % li, [128, 2 * D], F32, stack)
            b = Buf()
            for i in range(2):
                dma("sp", t[:, i * D:(i + 1) * D], lnp_d[2 * li + i:2 * li + i + 1, :].partition_broadcast(128),
                    ds_c, [], [b])
            return t, b

        def layer_norm(r_ap, rB, lnp, lnpB):
            st, stB = lnw["st"].next()
            mv, mvB = lnw["mv"].next()
            S.add("dve", lambda e: e.bn_stats(st[:, 0:6], r_ap[:, 0:512]), R=[rB], W=[stB])
            S.add("dve", lambda e: e.bn_stats(st[:, 6:12], r_ap[:, 512:1024]), R=[rB, stB], W=[stB])
            S.add("dve", lambda e: e.bn_aggr(mv[:, 0:2], st[:, 0:12]), R=[stB], W=[mvB])
            act(mv[:, 3:4], mv[:, 1:2], AF.Sqrt, [mvB], [mvB], bias=EPS)
            recip(mv[:, 2:3], mv[:, 3:4], [mvB], [mvB])
            ts("dve", r_ap, r_ap, mv[:, 0:1], mv[:, 2:3], ALU.subtract, ALU.mult, [rB, mvB], [rB])
            tt("dve", r_ap, r_ap, lnp[:, 0:D], ALU.mult, [rB, lnpB], [rB])
            tt("dve", r_ap, r_ap, lnp[:, D:2 * D], ALU.add, [rB, lnpB], [rB])

        def transpose_tile(hb, hbB, dst_fn, t):
            for half in range(2):
                pt, ptB = bank(6 + half)
                for j in range(4):
                    kc = half * 4 + j
                    mm(pt[:, j * 128:(j + 1) * 128], hb[:, kc * 128:(kc + 1) * 128], ident, True, True,
                       [hbB, B_cst], ptB[j:j + 1])
                dst, dB = dst_fn(half)
                cp("act" if half else "dve", dst, pt.rearrange("p (j n) -> p j n", j=4), ptB, [dB])

        def ffn_phase(tag, srcT_d, res_d, win_d, wout_d, li, out_tm_d, hT_out):
            with ExitStack() as p1:
                xT = sb(tag + "xT", [128, KC, NTOK], BF16, p1)
                xTB = bufs(NG)
                ds_x = dsem(tag + "d_x")
                xTv = srcT_d.rearrange("(kc p) t -> p kc t", p=128)
                for gi in range(NG):
                    for kc in range(KC):
                        dma("pool", xT[:, kc, gi * G:(gi + 1) * G], xTv[:, kc, gi * G:(gi + 1) * G], ds_x, [],
                            [xTB[gi]])
                wout = sb(tag + "wout", [128, NFC * D], BF16, p1)
                woutB = Buf()
                ds_wo = dsem(tag + "d_wo")
                for i in range(2):
                    dma("pool", wout[:, i * 11 * D:(i + 1) * 11 * D], wout_d[:, i * 11 * D:(i + 1) * 11 * D], ds_wo,
                        [], [woutB])
                lnp, lnpB = load_lnp(p1, li)
                wring = Ring([sb(tag + "w%d" % i, [128, KC * 256], BF16, p1) for i in range(3)])
                wds = [dsem(tag + "d_w%d" % i) for i in range(3)]
                sgr = Ring([sb(tag + "sg%d" % i, [128, 512], F32, p1) for i in range(2)])
                actT = sb(tag + "actT", [128, NFC, G], BF16, p1)
                actB = [bufs(2) for _ in range(NFC)]
                rr = Ring([sb(tag + "r%d" % i, [128, D], F32, p1) for i in range(3)])
                ds_ho = [dsem(tag + "d_ho%d" % i) for i in range(3)]
                xres = Ring([sb(tag + "x%d" % i, [128, D], F32, p1) for i in range(2)])
                xres_ds = [dsem(tag + "d_xr%d" % i) for i in range(2)]
                if hT_out is not None:
                    hbf = Ring([sb(tag + "hbf%d" % i, [128, D], BF16, p1) for i in range(2)])
                    hTg = sb(tag + "hTg", [128, KC, G], BF16, p1)
                    hTgB = [bufs(2) for _ in range(G // 128)]
                    hTgAll = [b for bb in hTgB for b in bb]
                nblk = [(o, min(512, G - o)) for o in range(0, G, 512)]
                gu = 0
                for gi in range(NG):
                    t0 = gi * G
                    for c in range(NFC):
                        w, wB = wring.next()
                        dma("pool", w[:], win_d[c], wds[(wring.i - 1) % 3], [], [wB])
                        for bi, (o, n) in enumerate(nblk):
                            pg, pgB = bank(2 * (gu % 2))
                            pu, puB = bank(2 * (gu % 2) + 1)
                            gu += 1
                            for kc in range(KC):
                                mm(pg[:, :n], w[:, kc * 256:kc * 256 + 128], xT[:, kc, t0 + o:t0 + o + n], kc == 0,
                                   kc == KC - 1, [wB, xTB[gi]], pgB)
                            for kc in range(KC):
                                mm(pu[:, :n], w[:, kc * 256 + 128:kc * 256 + 256], xT[:, kc, t0 + o:t0 + o + n],
                                   kc == 0, kc == KC - 1, [wB, xTB[gi]], puB)
                            sg, sgB = sgr.next()
                            act(sg[:, :n], pg[:, :n], AF.Silu, pgB, [sgB])
                            tt("dve", actT[:, c, o:o + n], sg[:, :n], pu[:, :n], ALU.mult, [sgB] + puB, [actB[c][bi]])
                    for t in range(G // 128):
                        po = PS[2]
                        for c in range(NFC):
                            for hf in range(2):
                                mm(po[:, hf * 512:(hf + 1) * 512], actT[:, c, t * 128:(t + 1) * 128],
                                   wout[:, c * D + hf * 512:c * D + hf * 512 + 512], c == 0, c == NFC - 1,
                                   [actB[c][(t * 128) // 512], woutB], PSQ[4 + hf])
                        x, xB = xres.next()
                        dma("sp", x[:], res_d[t0 + t * 128:t0 + (t + 1) * 128, :], xres_ds[(xres.i - 1) % 2], [], [xB])
                        r, rB = rr.next()
                        k = (rr.i - 1) % 3
                        act(r[:], x[:], AF.Copy, [xB], [rB], scale=ALPHA)
                        stt("dve", r[:], po[:], 0.5, r[:], ALU.mult, ALU.add, PSQ[4] + PSQ[5] + [rB], [rB])
                        layer_norm(r[:], rB, lnp, lnpB)
                        dma("sp", out_tm_d[t0 + t * 128:t0 + (t + 1) * 128, :], r[:], ds_ho[k], [rB], [])
                        if hT_out is not None:
                            hb, hbB = hbf.next()
                            cp("act", hb[:], r[:], [rB], [hbB])
                            transpose_tile(hb, hbB, lambda half, t=t: (
                                hTg[:, half * 4:half * 4 + 4, t * 128:(t + 1) * 128], hTgB[t][half]), t)
                    if hT_out is not None:
                        hT_out(gi, hTg, hTgAll)
                S.flush()

        ds_hT = dsem("d_hT")
        hTv = hT_p_d.rearrange("(j kc p) t -> j p kc t", j=4, p=128)
        B_hTp = bufs(4)

        def ship_h(gi, hTg, hB):
            t0 = gi * G
            npr = max(0, min(NP_TOK - t0, G))
            for j in range(4):
                a, b = max(t0, 512 * j), min(t0 + npr, 512 * (j + 1))
                if a < b:
                    dma("sp", hTv[j, :, :, a - 512 * j:b - 512 * j], hTg[:, :, a - t0:b - t0], ds_hT, hB, [B_hTp[j]])
            if npr < G:
                cp("dve", hT_s[:, :, :], hTg[:, :, npr:G], hB, [hT_sB])

        if test2b:
            hT_in_d = nc.dram_tensor("hT_s_in", [D, NS_TOK], F32, kind="ExternalInput").ap()
            dma("pool", hT_s[:], hT_in_d.rearrange("(kc p) t -> p kc t", p=128), ds_c, [], [hT_sB])
        else:
            ffn_phase("f1", xT_d, xtm_d, f1_win_d, f1_wout_d, 0, h_tm_d, ship_h)
        if stop == 1:
            return nc

        cc1 = dsem("cc1", 1)
        B_ag1 = bufs(4)

        def ag_fn(src, dst):
            return lambda e: e.collective_compute("AllGather", ALU.bypass, replica_groups=GROUPS, ins=[src], outs=[dst])
        for j in range(4):
            if test2b:
                break
            S.add("pool", ag_fn(hT_p_d[j * D:(j + 1) * D, :], ag1_d[j * 4 * D:(j + 1) * 4 * D, :]),
                  R=[B_hTp[j]], W=[B_ag1[j]], dsem=cc1)

        if stop == 1.5:
            S.flush()
            return nc
        LN_QS = float(np.log(128.0 ** -0.5))
        with ExitStack() as p2:
            cst2 = sb("cst2", [128, 1024], BF16, p2)
            dma("pool", cst2[:], consts2_d, ds_c, [], [B_cst])
            wglr = sb("wglr_sb", [128, KC, 128], BF16, p2)
            B_wglr = Buf()
            memset(wglr[:], 0.0, [B_wglr])
            dma("pool", wglr[:, :, 0:16], wglr_d, ds_c, [], [B_wglr])
            wgla = Ring([sb("wgla%d" % i, [128, KC, 896], BF16, p2) for i in range(2)])
            wgla_ds = [dsem("d_wgla%d" % i) for i in range(2)]
            wupr = Ring([sb("wup%d" % i, [128, 128], BF16, p2) for i in range(2)])
            gnbr = Ring([sb("gnb%d" % i, [128, 256], F32, p2) for i in range(2)])

            def load_gla_w(src_ap, hidx):
                w, wB = wgla.next()
                k = (wgla.i - 1) % 2
                dma("pool", w[:], src_ap, wgla_ds[k], [], [wB])
                wu, wuB = wupr.next()
                dma("pool", wu[:], wup_d[hidx], wgla_ds[k], [], [wuB])
                gn, gnB = gnbr.next()
                dma("sp", gn[:], gnorm_d[hidx:hidx + 1, :].partition_broadcast(128), wgla_ds[k], [], [gnB])
                W = dict(q=w[:, :, 0:128], k=w[:, :, 128:256], vr=w[:, :, 256:768], ktm=w[:, :, 768:896], B=wB)
                return W, (wu[:], wuB), (gn[:], gnB)

            def ring(name, shape, dt, n=2, zero=False, ones_row=False):
                tiles = [sb("%s%d" % (name, i), shape, dt, p2) for i in range(n)]
                r = Ring(tiles)
                if zero:
                    for t, b in zip(tiles, r.b):
                        memset(t[:], 0.0, [b])
                        if ones_row:
                            memset(t[32:33, :], 1.0, [b])
                return r

            R = dict(
                glrT=ring("g_glrT", [128, 128], BF16, zero=True, ones_row=True),
                e1=ring("g_e1", [128, 128], F32), la=ring("g_la", [128, 128], BF16),
                eb=ring("g_eb", [128, 128], F32), enb=ring("g_enb", [128, 128], F32),
                ek=ring("g_ek", [128, 128], F32), dec=ring("g_dec", [128, 2], F32),
                qA=ring("g_qA", [128, 128], BF16, zero=True), qB=ring("g_qB", [128, 128], BF16, zero=True),
                kg=ring("g_kg", [128, 128], BF16),
                kd0=ring("g_kd0", [128, 128], BF16, zero=True), kd1=ring("g_kd1", [128, 128], BF16, zero=True),
                vb=ring("g_vb", [128, 256], BF16), eg=ring("g_eg", [128, 256], F32),
                gg=ring("g_gg", [128, 256], F32), at=ring("g_at", [128, 128], BF16),
                ss=ring("g_ss", [128, 4], F32), oa=ring("g_oa", [128, 256], BF16),
                sbf=ring("g_sbf", [128, 256], BF16, n=6),
            )
            junk = sb("g_junk", [128, 256], F32, p2)
            junkB = Buf()

            class State:
                pass

            def snapshot(st):
                nb, nbB = R["sbf"].next()
                cp("act", nb[:], st.f32, [st.fB], [nbB])
                st.bf, st.bfB = nb[:], nbB

            def update(st, dec_ap, decB, U_ap, UB):
                stt("dve", st.f32, st.f32, dec_ap, U_ap, ALU.mult, ALU.add, [st.fB, decB] + UB, [st.fB])
                snapshot(st)

            def gla_tile(W, hT_t, hTB, wup, gnb, states, dst_fn):
                pA, Aq = bank(0)
                pB_, Bq = bank(1)
                pC, Cq = bank(2)
                pD, Dq = bank(3)
                wu, wuB = wup
                gn, gnB = gnb
                for j, key in enumerate(("q", "k", "glr")):
                    wk = wglr if key == "glr" else W[key]
                    wkB = B_wglr if key == "glr" else W["B"]
                    for kc in range(KC):
                        mm(pA[:, j * 128:(j + 1) * 128], wk[:, kc, :], hT_t[:, kc, :], kc == 0, kc == KC - 1,
                           [wkB] + hTB, Aq[j:j + 1])
                for kc in range(KC):
                    mm(pB_[:, 0:512], hT_t[:, kc, :], W["vr"][:, kc, :], kc == 0, kc == KC - 1, [W["B"]] + hTB, Bq)
                for kc in range(KC):
                    mm(pC[:, 0:128], hT_t[:, kc, :], W["ktm"][:, kc, :], kc == 0, kc == KC - 1, [W["B"]] + hTB,
                       Cq[0:1])
                yield
                gp, gpB = R["glrT"].next()
                cp("dve", gp[0:32, :], pA[0:32, 256:384], Aq[2:3], [gpB])
                mm(pC[:, 128:256], gp[:], wu, True, True, [gpB, wuB], Cq[1:2])
                yield
                e1, e1B = R["e1"].next()
                la, laB = R["la"].next()
                act(e1[:], pC[:, 128:256], AF.Exp, Cq[1:2], [e1B], scale=-1.0)
                act(la[:], e1[:], AF.Ln, [e1B], [laB], bias=1.0)
                mm(pC[:, 256:384], la[:], TRI_INCL, True, True, [laB, B_cst], Cq[2:3])
                mm(pC[:, 384:512], TRI_REV, la[:], True, True, [laB, B_cst], Cq[3:4])
                yield
                eb, ebB = R["eb"].next()
                enb, enbB = R["enb"].next()
                ek, ekB = R["ek"].next()
                dec, decB = R["dec"].next()
                act(eb[:], pC[:, 256:384], AF.Exp, Cq[2:3], [ebB], bias=LN_QS)
                act(enb[:], pC[:, 256:384], AF.Exp, Cq[2:3], [enbB], scale=-1.0)
                act(ek[:], pC[:, 384:512], AF.Exp, Cq[3:4], [ekB])
                act(dec[:, 0:1], pC[:, 256 + 63:256 + 64], AF.Exp, Cq[2:3], [decB])
                act(dec[:, 1:2], pC[:, 256 + 127:256 + 128], AF.Exp, Cq[2:3], [decB])
                yield
                qA, qAB = R["qA"].next()
                qB, qBB = R["qB"].next()
                tt("dve", qA[:, 0:64], pA[:, 0:64], eb[:, 0:64], ALU.mult, Aq[0:1] + [ebB], [qAB])
                tt("dve", qB[:, 64:128], pA[:, 64:128], eb[:, 64:128], ALU.mult, Aq[0:1] + [ebB], [qBB])
                kg, kgB = R["kg"].next()
                tt("dve", kg[:], pA[:, 128:256], enb[:], ALU.mult, Aq[1:2] + [enbB], [kgB])
                kd0, kd0B = R["kd0"].next()
                kd1, kd1B = R["kd1"].next()
                tt("dve", kd0[0:64, :], pC[0:64, 0:128], ek[0:64, :], ALU.mult, Cq[0:1] + [ekB], [kd0B])
                tt("dve", kd1[64:128, :], pC[64:128, 0:128], ek[64:128, :], ALU.mult, Cq[0:1] + [ekB], [kd1B])
                yield
                vb, vbB = R["vb"].next()
                cp("act", vb[:], pB_[:, 0:256], Bq[0:2], [vbB])
                eg, egB = R["eg"].next()
                gg, ggB = R["gg"].next()
                act(eg[:], pB_[:, 256:512], AF.Exp, Bq[2:4], [egB], scale=-1.0)
                act(eg[:], eg[:], AF.Ln, [egB], [egB], bias=1.0)
                act(eg[:], eg[:], AF.Exp, [egB], [egB], scale=-1.0)
                tt("dve", gg[:], pB_[:, 256:512], eg[:], ALU.mult, Bq[2:4] + [egB], [ggB])
                tt("dve", gg[:], gg[:], gn, ALU.mult, [ggB, gnB], [ggB])
                yield
                mm(pA[:, 384:512], kg[:], qA[:], True, False, [kgB, qAB], Aq[3:4])
                mm(pA[:, 384:512], kg[:], qB[:], False, True, [kgB, qBB], Aq[3:4])
                at, atB = R["at"].next()
                tt("dve", at[:], pA[:, 384:512], CAUSAL2, ALU.mult, Aq[3:4] + [B_cst], [atB])
                yield
                st0, st1 = states
                mm(pD[:, 0:256], at[:], vb[:], True, False, [atB, vbB], Dq[0:2])
                mm(pD[:, 0:256], qA[:], st0.bf, False, False, [qAB, st0.bfB], Dq[0:2])
                mm(pB_[:, 0:256], kd0[:], vb[:], True, True, [kd0B, vbB], Bq[0:2])
                mm(pB_[:, 256:512], kd1[:], vb[:], True, True, [kd1B, vbB], Bq[2:4])
                update(st0, dec[:, 0:1], decB, pB_[:, 0:256], Bq[0:2])
                yield
                mm(pD[:, 0:256], qB[:], st1.bf, False, True, [qBB, st1.bfB], Dq[0:2])
                update(st1, dec[:, 1:2], decB, pB_[:, 256:512], Bq[2:4])
                yield
                ss, ssB = R["ss"].next()
                act(junk[:], pD[:, 0:256], AF.Square, Dq[0:2], [junkB, ssB], accum_out=ss[:, 0:1])
                act(ss[:, 1:2], ss[:, 0:1], AF.Ln, [ssB], [ssB], scale=1.0 / 256, bias=EPS)
                act(ss[:, 2:3], ss[:, 1:2], AF.Exp, [ssB], [ssB], scale=-0.5)
                oa, oaB = R["oa"].next()
                stt("dve", oa[:], pD[:, 0:256], ss[:, 2:3], gg[:], ALU.mult, ALU.mult, Dq[0:2] + [ssB, ggB], [oaB])
                yield
                for c in range(2):
                    mm(pD[:, 256 + c * 128:256 + (c + 1) * 128], oa[:, c * 128:(c + 1) * 128], ident, True, True,
                       [oaB, B_cst], Dq[2 + c:3 + c])
                dst, dstB = dst_fn()
                cp("act", dst, pD[:, 256:512].rearrange("p (c n) -> p c n", c=2), Dq[2:4], dstB)

            Er = ring("s_E", [128, 512], BF16, 4)
            Lr = ring("s_L", [128, 512], BF16, 4)
            LSr = ring("s_LS", [128, 512], BF16, 3)
            Xr = ring("s_X", [128, 512], BF16, 3)
            Wr = ring("s_W", [128, 512], BF16, 4)
            zcnt = [0]

            def record(gen):
                rec = []
                S.add = lambda eng, fn, R=(), W=(), dsem=None, extra=(): rec.append((eng, fn, R, W, dsem, extra))
                try:
                    for _ in gen:
                        pass
                finally:
                    del S.add
                return rec

            class Replay:
                def __init__(self, rec, iters):
                    self.rec = rec
                    self.i = 0
                    self.k = max(1, -(-len(rec) // max(1, iters)))

                def step(self):
                    for _ in range(self.k):
                        if self.i < len(self.rec):
                            S.add(*self.rec[self.i])
                            self.i += 1

                def drain(self):
                    while self.i < len(self.rec):
                        S.add(*self.rec[self.i])
                        self.i += 1

            def sb_stream(blocks, evict, bg=None):
                ob, obB = bank(4)
                n = len(blocks)
                st = [None] * n

                def stage1(i):
                    if "lazy" in blocks[i]:
                        post = blocks[i].get("post")
                        blocks[i] = blocks[i]["lazy"]()
                        if post is not None:
                            blocks[i]["post"] = post
                    blk = blocks[i]
                    c0 = blk["c0"]
                    zb, zbB = bank(5 + zcnt[0] % 2)
                    zcnt[0] += 1
                    mk = blk.get("mask")
                    mfirst = blk.get("mask_first", False)
                    if mk is not None and mfirst:
                        mm(zb[:, mk[2]:mk[3]], mk[0], mk[1], True, False, [B_cst], zbB, sg=True)
                    for (kT, q, a, b, rb) in blk["z"]:
                        mm(zb[:, a:b], kT, q, not (mk is not None and mfirst), True, rb, zbB, sg=True)
                    if mk is not None and not mfirst:
                        mm(zb[:, mk[2]:mk[3]], mk[0], mk[1], False, True, [B_cst], zbB, sg=True)
                    E, EB = Er.next()
                    L, LB = Lr.next()
                    act(E[:, c0:512], zb[:, c0:512], AF.Exp, zbB, [EB])
                    act(L[:, c0:512], E[:, c0:512], AF.Ln, [EB], [LB], bias=1.0)
                    if i == 0:
                        ls = (L, LB, c0)
                    elif i < n - 1:
                        pLs, pLsB, pc0 = st[i - 1]["ls"]
                        Ls, LsB = LSr.next()
                        tt("dve", Ls[:, pc0:512], pLs[:, pc0:512], L[:, pc0:512], ALU.add, [pLsB, LB], [LsB])
                        if c0 < pc0:
                            cp("dve", Ls[:, c0:pc0], L[:, c0:pc0], [LB], [LsB])
                        ls = (Ls, LsB, c0)
                    else:
                        ls = None
                    st[i] = dict(E=E, EB=EB, L=L, LB=LB, ls=ls)

                def stage2(i):
                    c0 = blocks[i]["c0"]
                    d = st[i]
                    tb, tbB = bank(7)
                    mm(tb[:, c0:512], NEGTRI, d["L"][:, c0:512], True, i == 0, [d["LB"], B_cst], tbB, sg=True)
                    if i >= 1:
                        pLs, pLsB, pc0 = st[i - 1]["ls"]
                        mm(tb[:, pc0:512], NEGONES, pLs[:, pc0:512], False, True, [pLsB, B_cst], tbB, sg=True)
                    X, XB = Xr.next()
                    act(X[:, c0:512], tb[:, c0:512], AF.Exp, tbB, [XB])
                    w, wB = Wr.next()
                    tt("dve", w[:, c0:512], d["E"][:, c0:512], X[:, c0:512], ALU.mult, [d["EB"], XB], [wB])
                    d["w"] = (w, wB)

                def stage3(i):
                    w, wB = st[i]["w"]
                    for j, (v, a, b, rb) in enumerate(blocks[i]["pv"]):
                        mm(ob[:, a:b], v, w[:, a:b], i == 0 and j == 0, True, [wB] + rb, obB, sg=True)

                for i in range(n + 2):
                    if i < n:
                        stage1(i)
                    if 1 <= i <= n:
                        stage2(i - 1)
                    if i >= 2:
                        stage3(i - 2)
                    if i < n and blocks[i].get("post") is not None:
                        blocks[i]["post"]()
                    if bg is not None:
                        bg.step()
                evict(ob, obB)

            with ExitStack() as p2b:
                wpc = Ring([sb("wpc%d" % i, [128, KC, 512], BF16, p2b) for i in range(2)])
                wpc_ds = [dsem("d_wpc%d" % i) for i in range(2)]
                qT_s = sb("qT_s", [128, 8, NS_TOK], BF16, p2b)
                kT_s = sb("kT_s", [128, 8, NS_TOK], BF16, p2b)
                v_s = sb("v_s", [128, 2, D], BF16, p2b)
                qkB = bufs(16)
                v_sB = [bufs(2) for _ in range(2)]
                kvst = Ring([sb("kvst%d" % i, [128, 512], F32, p2b) for i in range(2)])
                kvst_ds = [dsem("d_kvst%d" % i) for i in range(2)]
                pcnt = 0
                import os
                KV = os.environ.get("KVAR", "")
                for piece in range(4):
                    if "nofm" in KV:
                        break
                    w, wB = wpc.next()
                    dma("pool", w[:], ws_sb_d[:, :, piece * 512:(piece + 1) * 512], wpc_ds[(wpc.i - 1) % 2], [], [wB])
                    for j in range(4):
                        ch = piece * 4 + j
                        pz, pzB = bank(5 + pcnt % 2)
                        pcnt += 1
                        for kc in range(KC):
                            mm(pz[:, 0:NS_TOK], w[:, kc, j * 128:(j + 1) * 128], hT_s[:, kc, :], kc == 0, kc == KC - 1,
                               [wB, hT_sB], pzB)
                        if ch < 8:
                            cp("act", qT_s[:, ch, :], pz[:, 0:NS_TOK], pzB, [qkB[ch]], scale=float(128.0 ** -0.5))
                        else:
                            cp("dve", kT_s[:, ch - 8, :], pz[:, 0:NS_TOK], pzB, [qkB[ch]])
                for piece in range(4):
                    if "notm" in KV:
                        break
                    w, wB = wpc.next()
                    dma("pool", w[:], ws_sb_d[:, :, 2048 + piece * 512:2048 + (piece + 1) * 512],
                        wpc_ds[(wpc.i - 1) % 2], [], [wB])
                    for ti in range(2):
                        pz, pzB = bank(5 + pcnt % 2)
                        pcnt += 1
                        for kc in range(KC):
                            mm(pz[:, :], hT_s[:, kc, ti * 128:(ti + 1) * 128], w[:, kc, :], kc == 0, kc == KC - 1,
                               [wB, hT_sB], pzB)
                        stg, stgB = kvst.next()
                        cp("act", stg[:], pz[:, :], pzB, [stgB])
                        dst = sbk_s_d if piece < 2 else sbv_s_d
                        if "nodma" not in KV:
                            dma("sp", dst[ti * 128:(ti + 1) * 128, (piece % 2) * 512:(piece % 2) * 512 + 512], stg[:],
                                kvst_ds[(kvst.i - 1) % 2], [stgB], [])
                        if piece >= 2 and "novs" not in KV:
                            cp("dve", v_s[:, ti, (piece - 2) * 512:(piece - 2) * 512 + 512], pz[:, :], pzB,
                               [v_sB[ti][piece - 2]])
                if stop == 1.7:
                    S.flush()
                    return nc
                sf = Ring([sb("s_sf%d" % i, [128, 256], F32, p2b) for i in range(4)])
                sf_ds = [dsem("d_sf%d" % i) for i in range(4)]
                def sample_gla_gen():
                    for hh in range(4):
                        W, wup, gnb = load_gla_w(ws_gla_d[hh], hh)
                        for ti in range(2):
                            sts = []
                            for j in range(2):
                                st = State()
                                f, fB = sf.next()
                                st.k = (sf.i - 1) % 4
                                dma("sp", f[:], state_s_d[2 * ti + j, hh], sf_ds[st.k], [], [fB])
                                st.f32, st.fB = f[:], fB
                                snapshot(st)
                                sts.append(st)
                            yield from gla_tile(W, hT_s[:, :, ti * 128:(ti + 1) * 128], [hT_sB], wup, gnb, sts,
                                                lambda hh=hh, ti=ti: (
                                                    oaT_s[:, 2 * hh:2 * hh + 2, ti * 128:(ti + 1) * 128],
                                                    [oaT_sB[2 * hh + ti]]))
                            for j in range(2):
                                dma("sp", st_s_d[2 * ti + j, hh], sts[j].f32, sf_ds[sts[j].k], [sts[j].fB], [])
                            yield
                bg_s = Replay(record(sample_gla_gen()), 4 * 35)
                kcr = Ring([sb("kc%d" % i, [128, 8, 1024], BF16, p2b) for i in range(2)])
                vcr = Ring([sb("vc%d" % i, [128, 8, D], BF16, p2b) for i in range(2)])
                kc_ds = [dsem("d_kc%d" % i) for i in range(2)]
                vc_ds = [dsem("d_vc%d" % i) for i in range(2)]
                def load_kv(s, gk):
                    kt, ktB = kcr.next()
                    for hq in range(4):
                        dma("pool", kt[:, 2 * hq:2 * hq + 2, :],
                            kcT_d[s, 2 * hq:2 * hq + 2].rearrange("h d t -> d h t")[:, :, gk * 1024:(gk + 1) * 1024],
                            "auto", [], [ktB])
                    vt, vtB = vcr.next()
                    for hq in range(4):
                        dma("pool", vt[:, 2 * hq:2 * hq + 2, :], vc_d[s, :, gk * 8 + 2 * hq:gk * 8 + 2 * hq + 2, :],
                            "auto", [], [vtB])
                    return kt, ktB, vt, vtB

                order = [(s, gk) for s in range(4) for gk in range(3, -1, -1)]
                loaded = {order[0]: load_kv(*order[0])}
                for s in range(4):
                    ti, par = s // 2, s % 2
                    qcols = slice(s * 64, (s + 1) * 64)
                    blocks = [dict(
                        c0=0, mask=(ident, cst2[:, par * 512:(par + 1) * 512], 0, 512), mask_first=True,
                        z=[(kT_s[:, h, ti * 128:(ti + 1) * 128], qT_s[:, h, qcols], h * 64, h * 64 + 64,
                            [qkB[h], qkB[8 + h]]) for h in range(8)],
                        pv=[(v_s[:, ti, h * 128:(h + 1) * 128], h * 64, h * 64 + 64, v_sB[ti]) for h in range(8)])]
                    for gk in range(3, -1, -1):
                        for kb in range(7, -1, -1):
                            def mk(s=s, gk=gk, kb=kb, qcols=qcols):
                                kt, ktB, vt, vtB = loaded[(s, gk)]
                                return dict(
                                    c0=0,
                                    z=[(kt[:, h, kb * 128:(kb + 1) * 128], qT_s[:, h, qcols], h * 64, h * 64 + 64,
                                        [ktB, qkB[h]]) for h in range(8)],
                                    pv=[(vt[:, kb, h * 128:(h + 1) * 128], h * 64, h * 64 + 64, [vtB])
                                        for h in range(8)])
                            blk = dict(lazy=mk)
                            if kb == 6:
                                nxt = order.index((s, gk)) + 1
                                if nxt < len(order):
                                    blk["post"] = (lambda nxt=nxt: loaded.__setitem__(order[nxt], load_kv(*order[nxt])))
                            blocks.append(blk)

                    def evict(ob, obB, s=s):
                        cp("act", obT_s[:, :, s * 64:(s + 1) * 64], ob.rearrange("p (h q) -> p h q", h=8), obB,
                           [obT_sB[s]])
                    sb_stream(blocks, evict, bg_s)
                bg_s.drain()
                S.flush()
            if test2b:
                dbg_oa = dout("dbg_oa", [128, KC, NS_TOK], BF16)
                dbg_ob = dout("dbg_ob", [128, KC, NS_TOK], BF16)
                dma("sp", dbg_oa, oaT_s[:], "auto", oaT_sB, [])
                dma("sp", dbg_ob, obT_s[:], "auto", obT_sB, [])
                S.flush()
            if stop == 2 or test2b:
                return nc

            with ExitStack() as p2a:
                wsb = sb("wsb", [128, KC, 1024], BF16, p2a)
                B_wsb = Buf()
                for i in range(2):
                    dma("pool", wsb[:, :, i * 512:(i + 1) * 512], wp_sb_d[:, :, i * 512:(i + 1) * 512], ds_c, [],
                        [B_wsb])
                W, wup, gnb = load_gla_w(wp_gla_d, 4)
                kT_all = sb("kT_all", [128, 2, SEQ], BF16, p2a)
                v_all = sb("v_all", [128, 64, 256], BF16, p2a)
                kvB = bufs(16)
                hTb = Ring([sb("hTb%d" % i, [128, KC, 512], BF16, p2a) for i in range(2)])
                hTb_ds = [dsem("d_hTb%d" % i) for i in range(2)]
                qTb = Ring([sb("qTb%d" % i, [128, 2, 512], BF16, p2a) for i in range(2)])
                kvst = Ring([sb("kvstp%d" % i, [128, 512], F32, p2a) for i in range(2)])
                kvst_ds = [dsem("d_kvstp%d" % i) for i in range(2)]
                oaTb = Ring([sb("oaTb%d" % i, [128, 2, 512], BF16, p2a) for i in range(2)])
                oaTb_ds = [dsem("d_oaTb%d" % i) for i in range(2)]
                obst = Ring([sb("obst%d" % i, [128, 512], BF16, p2a) for i in range(2)])
                obst_ds = [dsem("d_obst%d" % i) for i in range(2)]
                B_ag2s = bufs(16)
                stp = State()
                stp_t = sb("stp_f32", [128, 256], F32, p2a)
                stp.f32, stp.fB = stp_t[:], Buf()
                memset(stp_t[:], 0.0, [stp.fB])
                snapshot(stp)
                ag1v = ag1_d.rearrange("(j r kc p) t -> j r p kc t", j=4, r=4, p=128)
                ag2sv = ag2s_d.rearrange("(k c p) t -> k p c t", k=8, p=128)
                cc2 = dsem("cc2", 1)
                B_ag2 = bufs(8)
                pcnt = 0
                for bi in range(16):
                    hb, hbB = hTb.next()
                    for half in range(2):
                        dma("sp", hb[:, half * 4:half * 4 + 4, :],
                            ag1v[bi % 4, bi // 4, :, half * 4:half * 4 + 4, :],
                            hTb_ds[(hTb.i - 1) % 2], [B_ag1[bi % 4]], [hbB])
                    tok = slice(bi * 512, (bi + 1) * 512)
                    qt, qtB = qTb.next()
                    for j in range(4):
                        pz, pzB = bank(5 + pcnt % 2)
                        pcnt += 1
                        for kc in range(KC):
                            mm(pz[:, :], wsb[:, kc, j * 128:(j + 1) * 128], hb[:, kc, :], kc == 0, kc == KC - 1,
                               [B_wsb, hbB], pzB)
                        if j < 2:
                            cp("act", qt[:, j, :], pz[:, :], pzB, [qtB], scale=float(128.0 ** -0.5))
                        else:
                            cp("dve", kT_all[:, j - 2, tok], pz[:, :], pzB, [kvB[bi]])
                    for t in range(4):
                        pz, pzB = bank(5 + pcnt % 2)
                        pcnt += 1
                        for kc in range(KC):
                            mm(pz[:, :], hb[:, kc, t * 128:(t + 1) * 128], wsb[:, kc, 512:1024], kc == 0, kc == KC - 1,
                               [B_wsb, hbB], pzB)
                        stg, stgB = kvst.next()
                        k_ = (kvst.i - 1) % 2
                        cp("act", stg[:], pz[:, :], pzB, [stgB])
                        rows = slice(bi * 512 + t * 128, bi * 512 + (t + 1) * 128)
                        dma("sp", sbk_p_d[rows, :], stg[:, 0:256], kvst_ds[k_], [stgB], [])
                        dma("sp", sbv_p_d[rows, :], stg[:, 256:512], kvst_ds[k_], [stgB], [])
                        cp("dve", v_all[:, bi * 4 + t, :], pz[:, 256:512], pzB, [kvB[bi]])
                    tk2 = slice((bi % 2) * 512, (bi % 2) * 512 + 512)

                    def gla_block_gen(bi=bi, hb=hb, hbB=hbB, tk2=tk2):
                        ot, otB = oaTb.next()
                        for t in range(4):
                            yield from gla_tile(W, hb[:, :, t * 128:(t + 1) * 128], [hbB], wup, gnb, [stp, stp],
                                                lambda t=t: (ot[:, :, t * 128:(t + 1) * 128], [otB]))
                        dma("sp", ag2sv[bi // 2, :, 0:2, tk2], ot[:], "auto", [otB], [B_ag2s[bi]])
                    bg_p = Replay(record(gla_block_gen()), 2 * (4 * bi + 6))
                    for hh in range(2):
                        blocks = []
                        for j in range(3, -1, -1):
                            kb = 4 * bi + j
                            c0 = 128 * j
                            blocks.append(dict(
                                c0=c0, mask=(ident, MASKB, c0, c0 + 128), mask_first=False,
                                z=[(kT_all[:, hh, kb * 128:(kb + 1) * 128], qt[:, hh, c0:512], c0, 512,
                                    [kvB[bi], qtB])],
                                pv=[(v_all[:, kb, hh * 128:(hh + 1) * 128], c0, 512, [kvB[bi]])]))
                        for kb in range(4 * bi - 1, -1, -1):
                            blocks.append(dict(
                                c0=0,
                                z=[(kT_all[:, hh, kb * 128:(kb + 1) * 128], qt[:, hh, :], 0, 512,
                                    [kvB[kb // 4], qtB])],
                                pv=[(v_all[:, kb, hh * 128:(hh + 1) * 128], 0, 512, [kvB[kb // 4]])]))

                        def evict(ob, obB, hh=hh, tk2=tk2, bi=bi):
                            o, oB = obst.next()
                            cp("act", o[:], ob[:, :], obB, [oB])
                            dma("sp", ag2sv[bi // 2, :, 2 + hh, tk2], o[:], obst_ds[(obst.i - 1) % 2], [oB],
                                [B_ag2s[bi]])
                        sb_stream(blocks, evict, bg_p)
                    bg_p.drain()
                    if bi % 2 == 1:
                        k = bi // 2
                        S.add("pool", ag_fn(ag2s_d[k * 512:(k + 1) * 512, :], ag2_d[k * 2048:(k + 1) * 2048, :]),
                              R=[B_ag2s[bi - 1], B_ag2s[bi]], W=[B_ag2[k]], dsem=cc2)
                dma("sp", st_p_d, stp.f32, ds_c, [stp.fB], [])
                S.flush()

        if stop == 3:
            return nc

        B_own = Buf()
        ds_own = dsem("d_own")

        pid_cache = {}

        def own_fn(kk, r4):
            def fn(e):
                if "g" not in pid_cache:
                    pid_cache["g"] = e.partition_id() % 4
                g_ = pid_cache["g"]
                return e.dma_start(out=own_d[r4 * 512:(r4 + 1) * 512, kk * 1024:(kk + 1) * 1024],
                                   in_=ag2_d[bass.ds((g_ * 2 + kk) * 2048 + r4 * 512, 512), :])
            return fn
        for kk in range(2):
            for r4 in range(4):
                S.add("pool", own_fn(kk, r4), R=B_ag2, W=[B_own], dsem=ds_own)

        G3 = 512
        groups3 = [(o, min(G3, NTOK - o)) for o in range(0, NTOK, G3)]
        B_h2 = bufs(len(groups3))
        with ExitStack() as p3:
            lnp, lnpB = load_lnp(p3, 1)
            wo = sb("wo_sb", [128, KC, D], BF16, p3)
            B_wo = Buf()
            for i in range(2):
                dma("pool", wo[:, :, i * 512:(i + 1) * 512], wo_d[:, :, i * 512:(i + 1) * 512], ds_c, [], [B_wo])
            wm_all = [sb("wm%d" % i, [128, KC, 512], BF16, p3) for i in range(KC)]
            wm_B = bufs(KC)
            for fc in range(KC):
                dma("pool", wm_all[fc][:], wmix_d[fc], "auto", [], [wm_B[fc]])
            hin = Ring([sb("m_h%d" % i, [128, KC, G3], BF16, p3) for i in range(2)])
            oain = Ring([sb("m_oa%d" % i, [128, KC, G3], BF16, p3) for i in range(2)])
            obin = Ring([sb("m_ob%d" % i, [128, KC, G3], BF16, p3) for i in range(2)])
            in_ds = [dsem("d_min%d" % i) for i in range(2)]
            sgr = Ring([sb("m_sg%d" % i, [128, G3], F32, p3) for i in range(4)])
            t12 = Ring([sb("m_t%d" % i, [128, G3], F32, p3) for i in range(2)])
            mT = sb("m_mT", [128, KC, G3], BF16, p3)
            mTB = bufs(KC)
            rr = Ring([sb("m_r%d" % i, [128, D], F32, p3) for i in range(3)])
            r_ds = [dsem("d_mr%d" % i) for i in range(3)]
            hbf = Ring([sb("m_hbf%d" % i, [128, D], BF16, p3) for i in range(2)])
            h2Tg = sb("m_h2Tg", [128, KC, G3], BF16, p3)
            h2TgB = [bufs(2) for _ in range(G3 // 128)]
            ds_h2T = dsem("d_h2T")
            h2Tv = h2T_d.rearrange("(kc p) t -> p kc t", p=128)
            for gi, (t0, n) in enumerate(groups3):
                if t0 < NP_TOK:
                    hi, hiB = hin.next()
                    oi, oiB = oain.next()
                    qi, qiB = obin.next()
                    k_ = (hin.i - 1) % 2
                    dma("sp", hi[:, :, 0:n], hTv[t0 // 512], in_ds[k_], B_hTp, [hiB])
                    for r4 in range(4):
                        for which, dstt in ((0, oi), (1, qi)):
                            r0 = r4 * 512 + which * 256
                            dma("sp", dstt[:, 2 * r4:2 * r4 + 2, 0:n],
                                own_d[r0:r0 + 256, t0:t0 + n].rearrange("(c p) t -> p c t", p=128), in_ds[k_],
                                [B_own], [oiB if which == 0 else qiB])
                    hT_g, hT_gB = hi, [hiB]
                    oa_g, oa_gB = oi, [oiB]
                    ob_g, ob_gB = qi, [qiB]
                    off = 0
                else:
                    hT_g, hT_gB = hT_s, [hT_sB]
                    oa_g, oa_gB = oaT_s, oaT_sB
                    ob_g, ob_gB = obT_s, obT_sB
                    off = t0 - NP_TOK
                for fc in range(KC):
                    w, wB = wm_all[fc], wm_B[fc]
                    srcs = ((hT_g, hT_gB), (hT_g, hT_gB), (oa_g, oa_gB), (ob_g, ob_gB))
                    for j in range(4):
                        pz, pzB = bank(j)
                        src, srcB = srcs[j]
                        for kc in range(KC):
                            mm(pz[:, 0:n], w[:, kc, j * 128:(j + 1) * 128], src[:, kc, off:off + n], kc == 0,
                               kc == KC - 1, [wB] + srcB, pzB)
                    sga, sgaB = sgr.next()
                    sgb, sgbB = sgr.next()
                    act(sga[:, 0:n], bank(0)[0][:, 0:n], AF.Sigmoid, bank(0)[1], [sgaB])
                    act(sgb[:, 0:n], bank(1)[0][:, 0:n], AF.Sigmoid, bank(1)[1], [sgbB])
                    t1, t1B = t12.next()
                    t2, t2B = t12.next()
                    tt("dve", t1[:, 0:n], sga[:, 0:n], bank(2)[0][:, 0:n], ALU.mult, [sgaB] + bank(2)[1], [t1B])
                    tt("dve", t2[:, 0:n], sgb[:, 0:n], bank(3)[0][:, 0:n], ALU.mult, [sgbB] + bank(3)[1], [t2B])
                    tt("dve", mT[:, fc, 0:n], t1[:, 0:n], t2[:, 0:n], ALU.add, [t1B, t2B], [mTB[fc]])
                for t in range(n // 128):
                    po = PS[2]
                    for fc in range(KC):
                        for hf in range(2):
                            mm(po[:, hf * 512:(hf + 1) * 512], mT[:, fc, t * 128:(t + 1) * 128],
                               wo[:, fc, hf * 512:(hf + 1) * 512], fc == 0, fc == KC - 1, [mTB[fc], B_wo], PSQ[4 + hf])
                    r, rB = rr.next()
                    k_ = (rr.i - 1) % 3
                    rows = slice(t0 + t * 128, t0 + (t + 1) * 128)
                    dma("sp", r[:], h_tm_d[rows, :], r_ds[k_], [], [rB])
                    act(r[:], r[:], AF.Copy, [rB], [rB], scale=ALPHA)
                    tt("dve", r[:], r[:], po[:], ALU.add, [rB] + PSQ[4] + PSQ[5], [rB])
                    layer_norm(r[:], rB, lnp, lnpB)
                    dma("sp", h2_tm_d[rows, :], r[:], r_ds[k_], [rB], [B_h2[gi]])
                    hb, hbB = hbf.next()
                    cp("act", hb[:], r[:], [rB], [hbB])
                    transpose_tile(hb, hbB, lambda half, t=t: (
                        h2Tg[:, half * 4:half * 4 + 4, t * 128:(t + 1) * 128], h2TgB[t][half]), t)
                dma("sp", h2Tv[:, :, t0:t0 + n], h2Tg[:, :, 0:n], ds_h2T,
                    [b for bb in h2TgB[:n // 128] for b in bb], [B_h2[gi]])
            S.flush()

        if stop == 4:
            return nc
        ffn_phase("f2", h2T_d, h2_tm_d, f2_win_d, f2_wout_d, 2, y_d, None)
        S.flush()
    return nc


def _consts():
    c = np.zeros((128, 8 * 128), np.float32)
    s = np.arange(128)[:, None]
    t = np.arange(128)[None, :]
    same = (s // 64) == (t // 64)
    c[:, 0:128] = np.eye(128, dtype=np.float32)
    c[:, 128:256] = np.where(s >= t, -1.0, 0.0)
    c[:, 256:384] = np.where(s < t, -1.0, 0.0)
    c[:, 384:512] = np.where(s < t, 0.0, NEG)
    c[:, 512:640] = np.where((s <= t) & same, -1.0 / 16, 0.0)
    c[:, 640:768] = np.where((s > t) & same, -1.0 / 16, 0.0)
    c[:, 768:896] = np.where((s <= t) & same, 1.0, 0.0)
    c[:, 896:1024] = -1.0
    c2 = np.full((128, 1024), NEG, np.float32)
    sk = np.arange(128)[:, None]
    tq = np.arange(64)[None, :]
    for par in range(2):
        m = np.where(((sk // 64) == par) & ((sk % 64) < tq), 0.0, NEG).astype(np.float32)
        for h in range(8):
            c2[:, par * 512 + h * 64:par * 512 + (h + 1) * 64] = m
    return c, c2


def _r(w):
    return np.ascontiguousarray(w.reshape(KC, 128, w.shape[1]).transpose(1, 0, 2))


def _ffn_layout(w_in, w_out):
    wi = w_in.reshape(KC, 128, 2, NFC, 128)
    wi = np.ascontiguousarray(wi.transpose(3, 1, 0, 2, 4)).reshape(NFC, 128, KC * 256)
    wo = np.ascontiguousarray(w_out.reshape(NFC, 128, D).transpose(1, 0, 2)).reshape(128, NFC * D)
    return wi, wo


O_GQ, O_GK, O_GV, O_GR, O_GLR, O_SQ, O_SK, O_SV, O_GA, O_GB = 0, 512, 1024, 2048, 3072, 3088, 4112, 5136, 6160, 7184


def make_in_maps(inp):
    f1_win, f1_wout = _ffn_layout(inp["ffn1_w_in"][0], inp["ffn1_w_out"][0])
    f2_win, f2_wout = _ffn_layout(inp["ffn2_w_in"][0], inp["ffn2_w_out"][0])
    lnp = np.ascontiguousarray(np.stack([inp["ln1_g"][0], inp["ln1_b"][0], inp["ln2_g"][0], inp["ln2_b"][0],
                                         inp["ln3_g"][0], inp["ln3_b"][0]]).astype(np.float32))
    consts, consts2 = _consts()
    w = inp["w_in"][0]

    def c(off, n):
        return w[:, off:off + n]

    def gla_cols(h):
        return np.concatenate([c(O_GQ + h * 128, 128), c(O_GK + h * 128, 128), c(O_GV + h * 256, 256),
                               c(O_GR + h * 256, 256), c(O_GK + h * 128, 128)], axis=1)

    ws_gla = np.stack([_r(gla_cols(h)) for h in range(4)])
    ws_sb = _r(np.concatenate([c(O_SQ, 1024), c(O_SK, 1024), c(O_SK, 1024), c(O_SV, 1024)], axis=1))
    wglr = _r(c(O_GLR, 16))
    wupa = np.zeros((4, 128, 128), np.float32)
    for h in range(4):
        wupa[h, 0:16] = inp["w_gla_gate_up"][0][:, h * 128:(h + 1) * 128]
        wupa[h, 32] = inp["b_gla_gate"][0][h * 128:(h + 1) * 128]
    gn = inp["g_gla_norm"][0]
    wmix = np.stack([_r(np.concatenate([c(O_GA + fc * 128, 128), c(O_GB + fc * 128, 128),
                                        inp["w_gla_o"][0][:, fc * 128:(fc + 1) * 128],
                                        inp["w_sb_o"][0][:, fc * 128:(fc + 1) * 128]], axis=1)) for fc in range(8)])
    wo = _r(inp["w_out"][0])
    maps = []
    for core in range(NCORES):
        b, g = core // 4, core % 4
        xtm = np.concatenate([inp["x_prompt"][b, g * NP_TOK:(g + 1) * NP_TOK],
                              inp["x_sample"][4 * core:4 * core + 4].reshape(NS_TOK, D)], axis=0)
        xtm = np.ascontiguousarray(xtm)
        wp_sb = _r(np.concatenate([c(O_SQ + 2 * g * 128, 256), c(O_SK + 2 * g * 128, 256),
                                   c(O_SK + 2 * g * 128, 256), c(O_SV + 2 * g * 128, 256)], axis=1))
        sl = slice(4 * core, 4 * core + 4)
        kcT = np.ascontiguousarray(inp["cache_sb_k"][0, sl].transpose(0, 2, 3, 1))
        vc = np.ascontiguousarray(inp["cache_sb_v"][0, sl].reshape(4, 32, 128, D).transpose(0, 2, 1, 3))
        maps.append(dict(
            xT=np.ascontiguousarray(xtm.T), xtm=xtm, f1_win=f1_win, f1_wout=f1_wout, f2_win=f2_win, f2_wout=f2_wout,
            lnp=lnp, consts=consts, consts2=consts2, wp_gla=ws_gla[g], wp_sb=wp_sb, ws_gla=ws_gla, ws_sb=ws_sb,
            wglr=wglr, wup=np.ascontiguousarray(np.concatenate([wupa, wupa[g:g + 1]])),
            gnorm=np.ascontiguousarray(np.concatenate([gn, gn[g:g + 1]])),
            state_s=np.ascontiguousarray(inp["state_gla"][0, sl]), kcT=kcT, vc=vc, wmix=wmix, wo=wo))
    return maps


_NC_CACHE = {}
_STOP = None


def kernel(**inputs):
    inp = {k: np.asarray(v) for k, v in inputs.items()}
    if "nc" not in _NC_CACHE:
        _NC_CACHE["nc"] = build_nc(stop=_STOP)
    nc = _NC_CACHE["nc"]
    maps = make_in_maps(inp)
    res = run_bass_kernel_spmd(nc, maps, core_ids=list(range(NCORES)))
    R = res.results
    y_p = np.zeros((2, SEQ, D), np.float32)
    y_s = np.zeros((32, 64, D), np.float32)
    st_p = np.zeros((1, 2, 4, 128, 256), np.float32)
    k_p = np.zeros((1, 2, SEQ, 8, 128), np.float32)
    v_p = np.zeros((1, 2, SEQ, 8, 128), np.float32)
    st_s = np.zeros((1, 32, 4, 128, 256), np.float32)
    k_s = np.zeros((1, 32, 64, 8, 128), np.float32)
    v_s = np.zeros((1, 32, 64, 8, 128), np.float32)
    for core in range(NCORES):
        b, g = core // 4, core % 4
        r = R[core]
        y = np.asarray(r["y"])
        y_p[b, g * NP_TOK:(g + 1) * NP_TOK] = y[:NP_TOK]
        y_s[4 * core:4 * core + 4] = y[NP_TOK:].reshape(4, 64, D)
        st_p[0, b, g] = np.asarray(r["st_p"])
        k_p[0, b, :, 2 * g:2 * g + 2, :] = np.asarray(r["sbk_p"]).reshape(SEQ, 2, 128)
        v_p[0, b, :, 2 * g:2 * g + 2, :] = np.asarray(r["sbv_p"]).reshape(SEQ, 2, 128)
        st_s[0, 4 * core:4 * core + 4] = np.asarray(r["st_s"])
        k_s[0, 4 * core:4 * core + 4] = np.asarray(r["sbk_s"]).reshape(4, 64, 8, 128)
        v_s[0, 4 * core:4 * core + 4] = np.asarray(r["sbv_s"]).reshape(4, 64, 8, 128)
    return (y_p, y_s, st_p, k_p, v_p, st_s, k_s, v_s)
```
